# Optimizing a Trainium2 kernel written in Bass

```python
import jax
import jax.numpy as jnp
from jax import lax
import numpy as np


D_MODEL = 1024
BATCH = 4
SEQ = 4096
DEPTH = 1
DEC_BATCH = 2
DEC_SEQ = 16384
PAST_LEN = 128

GRID_W = 64
PLE_DIM = 256
NORM_EPS = 1e-6
DN_HEADS = 4
DN_HEAD_DIM = 128
DN_WIDTH = DN_HEADS * DN_HEAD_DIM
CONV_K = 5
CHUNK = 64
ATT_HEADS = 8
ATT_KV_HEADS = 2
ATT_HEAD_DIM = 64
ATT_WIDTH = ATT_HEADS * ATT_HEAD_DIM
ATT_KV_WIDTH = ATT_KV_HEADS * ATT_HEAD_DIM
Q_BLOCK = 128
ROPE_THETA = 10000.0
MIX_WIDTH = DN_WIDTH + ATT_WIDTH
IN_SIZES = (3 * DN_WIDTH, DN_WIDTH, DN_HEADS, DN_HEADS, DN_HEADS, DN_HEADS, ATT_WIDTH, ATT_KV_WIDTH, ATT_KV_WIDTH)
IN_COLS = sum(IN_SIZES)
IN_SPLITS = tuple(sum(IN_SIZES[:i + 1]) for i in range(len(IN_SIZES) - 1))
N_KEYS = 128
N_EXPERTS = N_KEYS * N_KEYS
PEER_HEADS = 8
PEER_TOPK = 16
D_KEY = 256
D_KEY_HALF = D_KEY // 2
PEER_BLOCK = 128

kernel_name = 'hymba_deltanet_gqa_peer_encoder'


def rmsnorm(x, gain):
    x32 = x.astype(jnp.float32)
    y = x32 * lax.rsqrt(jnp.mean(x32 * x32, axis=-1, keepdims=True) + NORM_EPS)
    return (y * gain.astype(jnp.float32)).astype(x.dtype)


def l2norm(x):
    return x * lax.rsqrt(jnp.sum(x * x, axis=-1, keepdims=True) + NORM_EPS)


def short_conv(x, w):
    pad = CONV_K // 2
    return lax.conv_general_dilated(x, w[:, None, :].astype(x.dtype), window_strides=(1,), padding=[(pad, pad)], dimension_numbers=('NWC', 'WIO', 'NWC'), feature_group_count=x.shape[-1])


def chunk_gated_delta(q, k, v, g, beta):
    B, T, H, dk = q.shape
    dv = v.shape[-1]
    n_chunks = T // CHUNK
    q = q * (dk ** -0.5)

    def chunks(t):
        return jnp.moveaxis(t.reshape((B, n_chunks, CHUNK) + t.shape[2:]), 3, 1)

    q, k, v, g, beta = chunks(q), chunks(k), chunks(v), chunks(g), chunks(beta)
    gc = jnp.cumsum(g, axis=-1)
    tril = jnp.tril(jnp.ones((CHUNK, CHUNK), dtype=bool))
    strict = jnp.tril(jnp.ones((CHUNK, CHUNK), dtype=bool), -1)
    diff = gc[..., :, None] - gc[..., None, :]
    decay = jnp.where(tril, jnp.exp(jnp.where(tril, diff, 0.0)), 0.0)
    kb = k * beta[..., None]
    vb = v * beta[..., None]
    lower = jnp.where(strict, jnp.einsum('bhncd,bhnsd->bhncs', kb, k) * decay, 0.0)
    rhs = jnp.concatenate([vb, kb * jnp.exp(gc)[..., None]], axis=-1)
    sol = lax.linalg.triangular_solve(lower, rhs, left_side=True, lower=True, unit_diagonal=True)
    u, w = sol[..., :dv], sol[..., dv:]
    qk_intra = jnp.where(tril, jnp.einsum('bhncd,bhnsd->bhncs', q, k) * decay, 0.0)

    def step(S, xs):
        qi, ki, ui, wi, gi, ai = xs
        v_new = ui - jnp.einsum('bhck,bhkv->bhcv', wi, S)
        o = jnp.einsum('bhck,bhkv->bhcv', qi * jnp.exp(gi)[..., None], S) + jnp.einsum('bhcs,bhsv->bhcv', ai, v_new)
        g_last = gi[..., -1]
        S = S * jnp.exp(g_last)[..., None, None] + jnp.einsum('bhck,bhcv->bhkv', ki * jnp.exp(g_last[..., None] - gi)[..., None], v_new)
        return S, o

    xs = tuple(jnp.moveaxis(t, 2, 0) for t in (q, k, u, w, gc, qk_intra))
    S0 = jnp.zeros((B, H, dk, dv), dtype=jnp.float32)
    _, o = lax.scan(step, S0, xs)
    return jnp.transpose(o, (1, 0, 3, 2, 4)).reshape(B, T, H, dv)


def deltanet_group(qkv, z, b_f, b_b, a_f, a_b, conv_w, a_log_f, a_log_b, dt_bias_f, dt_bias_b, out_norm):
    B, T, _ = qkv.shape
    f32 = jnp.float32
    qkv = jax.nn.silu(short_conv(qkv, conv_w)).astype(f32)
    q, k, v = jnp.split(qkv, 3, axis=-1)
    q = l2norm(q.reshape(B, T, DN_HEADS, DN_HEAD_DIM))
    k = l2norm(k.reshape(B, T, DN_HEADS, DN_HEAD_DIM))
    v = v.reshape(B, T, DN_HEADS, DN_HEAD_DIM)
    g_f = -jnp.exp(a_log_f.astype(f32)) * jax.nn.softplus(a_f.astype(f32) + dt_bias_f.astype(f32))
    g_b = -jnp.exp(a_log_b.astype(f32)) * jax.nn.softplus(a_b.astype(f32) + dt_bias_b.astype(f32))
    beta_f = jax.nn.sigmoid(b_f.astype(f32))
    beta_b = jax.nn.sigmoid(b_b.astype(f32))
    o_f = chunk_gated_delta(q, k, v, g_f, beta_f)
    o_b = jnp.flip(chunk_gated_delta(jnp.flip(q, 1), jnp.flip(k, 1), jnp.flip(v, 1), jnp.flip(g_b, 1), jnp.flip(beta_b, 1)), 1)
    o = rmsnorm(o_f + o_b, out_norm) * jax.nn.silu(z.astype(f32).reshape(B, T, DN_HEADS, DN_HEAD_DIM))
    return o.reshape(B, T, DN_WIDTH).astype(z.dtype)


def rope_1d(x, pos):
    d = x.shape[-1]
    inv_freq = ROPE_THETA ** (-jnp.arange(0, d, 2, dtype=jnp.float32) / d)
    ang = pos[:, None] * inv_freq[None, :]
    cos = jnp.cos(ang)[None, :, None, :]
    sin = jnp.sin(ang)[None, :, None, :]
    x1, x2 = jnp.split(x, 2, axis=-1)
    return jnp.concatenate([x1 * cos - x2 * sin, x2 * cos + x1 * sin], axis=-1)


def axial_rope(x, row_pos, col_pos):
    xr, xc = jnp.split(x.astype(jnp.float32), 2, axis=-1)
    return jnp.concatenate([rope_1d(xr, row_pos), rope_1d(xc, col_pos)], axis=-1).astype(x.dtype)


def gqa_group(q, k, v, q_norm, k_norm):
    B, T, _ = q.shape
    rows = T // GRID_W
    row_pos = jnp.repeat(jnp.arange(rows, dtype=jnp.float32), GRID_W)
    col_pos = jnp.tile(jnp.arange(GRID_W, dtype=jnp.float32), rows)
    G = ATT_HEADS // ATT_KV_HEADS
    q = axial_rope(rmsnorm(q.reshape(B, T, ATT_HEADS, ATT_HEAD_DIM), q_norm), row_pos, col_pos)
    k = axial_rope(rmsnorm(k.reshape(B, T, ATT_KV_HEADS, ATT_HEAD_DIM), k_norm), row_pos, col_pos)
    v = v.reshape(B, T, ATT_KV_HEADS, ATT_HEAD_DIM)
    n_blocks = T // Q_BLOCK
    qb = jnp.moveaxis(q.reshape(B, n_blocks, Q_BLOCK, ATT_KV_HEADS, G, ATT_HEAD_DIM), 1, 0)
    scale = ATT_HEAD_DIM ** -0.5

    def attend(qi):
        s = jnp.einsum('bqkgd,bskd->bkgqs', qi, k).astype(jnp.float32) * scale
        p = jax.nn.softmax(s, axis=-1).astype(v.dtype)
        return jnp.einsum('bkgqs,bskd->bqkgd', p, v)

    o = lax.map(attend, qb)
    return jnp.moveaxis(o, 0, 1).reshape(B, T, ATT_WIDTH)


def peer(x, w_query, keys_a, keys_b, u_emb, v_emb):
    B, T, D = x.shape
    n_tok = B * T
    xf = x.reshape(n_tok, D)
    q = jnp.einsum('nd,dk->nk', xf, w_query).astype(jnp.float32).reshape(n_tok, PEER_HEADS, 2, D_KEY_HALF)
    s_a = jnp.einsum('nhd,kd->nhk', q[:, :, 0], keys_a.astype(jnp.float32))
    s_b = jnp.einsum('nhd,kd->nhk', q[:, :, 1], keys_b.astype(jnp.float32))
    va, ia = lax.top_k(s_a, PEER_TOPK)
    vb, ib = lax.top_k(s_b, PEER_TOPK)
    cand_s = (va[..., :, None] + vb[..., None, :]).reshape(n_tok, PEER_HEADS, PEER_TOPK * PEER_TOPK)
    cand_i = (ia[..., :, None] * N_KEYS + ib[..., None, :]).reshape(n_tok, PEER_HEADS, PEER_TOPK * PEER_TOPK)
    top_s, pos = lax.top_k(cand_s, PEER_TOPK)
    idx = jnp.take_along_axis(cand_i, pos, axis=-1).reshape(n_tok, PEER_HEADS * PEER_TOPK)
    gates = jax.nn.softmax(top_s, axis=-1).reshape(n_tok, PEER_HEADS * PEER_TOPK).astype(x.dtype)
    n_blk = n_tok // PEER_BLOCK

    def block(args):
        xb, ibk, gb = args
        h = jax.nn.gelu(jnp.einsum('pd,pkd->pk', xb, u_emb[ibk]), approximate=False)
        return jnp.einsum('pk,pkd->pd', gb * h, v_emb[ibk])

    out = lax.map(block, (xf.reshape(n_blk, PEER_BLOCK, D), idx.reshape(n_blk, PEER_BLOCK, -1), gates.reshape(n_blk, PEER_BLOCK, -1)))
    return out.reshape(B, T, D)


def encoder_layer(h, p_i, attn_norm, w_in, conv_w, a_log_fwd, a_log_bwd, dt_bias_fwd, dt_bias_bwd, dn_out_norm, q_norm, k_norm, w_out, ffn_norm, peer_query, peer_keys_a, peer_keys_b, peer_u, peer_v, ple_proj, ple_norm, ple_gate):
    a = rmsnorm(h, attn_norm)
    proj = jnp.einsum('btd,dc->btc', a, w_in)
    dn_qkv, dn_z, b_f, b_b, a_f, a_b, at_q, at_k, at_v = jnp.split(proj, IN_SPLITS, axis=-1)
    dn_out = deltanet_group(dn_qkv, dn_z, b_f, b_b, a_f, a_b, conv_w, a_log_fwd, a_log_bwd, dt_bias_fwd, dt_bias_bwd, dn_out_norm)
    at_out = gqa_group(at_q, at_k, at_v, q_norm, k_norm)
    mix = jnp.concatenate([dn_out, at_out], axis=-1)
    h = h + jnp.einsum('btc,cd->btd', mix, w_out)
    h = h + peer(rmsnorm(h, ffn_norm), peer_query, peer_keys_a, peer_keys_b, peer_u, peer_v)
    ple = rmsnorm(jnp.einsum('btp,pd->btd', p_i, ple_proj), ple_norm)
    h = h + ple * jax.nn.sigmoid(jnp.einsum('btd,de->bte', h, ple_gate))
    return h


def setup_inputs(seed: int = 0) -> dict:
    key = jax.random.key(seed)
    ks = jax.random.split(key, 32)
    f32 = jnp.float32

    def nrm(k, shape, scale):
        return jax.random.normal(k, shape, f32) * scale

    def gain(k, shape):
        return 1.0 + 0.02 * jax.random.normal(k, shape, f32)

    def inv_softplus_dt(k, shape):
        dt = jnp.exp(jax.random.uniform(k, shape, f32, minval=float(np.log(1e-3)), maxval=float(np.log(1e-1))))
        return dt + jnp.log(-jnp.expm1(-dt))

    return {
        'x_prompt': nrm(ks[0], (BATCH, SEQ, D_MODEL), 1.0),
        'x_sample': nrm(ks[1], (DEC_BATCH, DEC_SEQ, D_MODEL), 1.0),
        'p_prompt': nrm(ks[2], (DEPTH, BATCH, SEQ, PLE_DIM), 1.0),
        'p_sample': nrm(ks[3], (DEPTH, DEC_BATCH, DEC_SEQ, PLE_DIM), 1.0),
        'attn_norm': gain(ks[4], (DEPTH, D_MODEL)),
        'w_in': nrm(ks[5], (DEPTH, D_MODEL, IN_COLS), D_MODEL ** -0.5),
        'conv_w': nrm(ks[6], (DEPTH, CONV_K, 3 * DN_WIDTH), CONV_K ** -0.5),
        'a_log_fwd': jnp.log(jax.random.uniform(ks[7], (DEPTH, DN_HEADS), f32, minval=1.0, maxval=16.0)),
        'a_log_bwd': jnp.log(jax.random.uniform(ks[8], (DEPTH, DN_HEADS), f32, minval=1.0, maxval=16.0)),
        'dt_bias_fwd': inv_softplus_dt(ks[9], (DEPTH, DN_HEADS)),
        'dt_bias_bwd': inv_softplus_dt(ks[10], (DEPTH, DN_HEADS)),
        'dn_out_norm': gain(ks[11], (DEPTH, DN_HEAD_DIM)),
        'q_norm': gain(ks[12], (DEPTH, ATT_HEAD_DIM)),
        'k_norm': gain(ks[13], (DEPTH, ATT_HEAD_DIM)),
        'w_out': nrm(ks[14], (DEPTH, MIX_WIDTH, D_MODEL), MIX_WIDTH ** -0.5),
        'ffn_norm': gain(ks[15], (DEPTH, D_MODEL)),
        'peer_query': nrm(ks[16], (DEPTH, D_MODEL, PEER_HEADS * D_KEY), D_MODEL ** -0.5),
        'peer_keys_a': nrm(ks[17], (DEPTH, N_KEYS, D_KEY_HALF), D_KEY_HALF ** -0.5),
        'peer_keys_b': nrm(ks[18], (DEPTH, N_KEYS, D_KEY_HALF), D_KEY_HALF ** -0.5),
        'peer_u': nrm(ks[19], (DEPTH, N_EXPERTS, D_MODEL), D_MODEL ** -0.5),
        'peer_v': nrm(ks[20], (DEPTH, N_EXPERTS, D_MODEL), D_MODEL ** -0.5),
        'ple_proj': nrm(ks[21], (DEPTH, PLE_DIM, D_MODEL), PLE_DIM ** -0.5),
        'ple_norm': gain(ks[22], (DEPTH, D_MODEL)),
        'ple_gate': nrm(ks[23], (DEPTH, D_MODEL, D_MODEL), D_MODEL ** -0.5),
        'final_norm': gain(ks[24], (D_MODEL,)),
    }


def reference(x_prompt, x_sample, p_prompt, p_sample, attn_norm, w_in, conv_w, a_log_fwd, a_log_bwd, dt_bias_fwd, dt_bias_bwd, dn_out_norm, q_norm, k_norm, w_out, ffn_norm, peer_query, peer_keys_a, peer_keys_b, peer_u, peer_v, ple_proj, ple_norm, ple_gate, final_norm):
    h_p = x_prompt
    h_s = x_sample
    for i in range(DEPTH):
        layer_w = (attn_norm[i], w_in[i], conv_w[i], a_log_fwd[i], a_log_bwd[i], dt_bias_fwd[i], dt_bias_bwd[i], dn_out_norm[i], q_norm[i], k_norm[i], w_out[i], ffn_norm[i], peer_query[i], peer_keys_a[i], peer_keys_b[i], peer_u[i], peer_v[i], ple_proj[i], ple_norm[i], ple_gate[i])
        h_p = encoder_layer(h_p, p_prompt[i], *layer_w)
        h_s = encoder_layer(h_s, p_sample[i], *layer_w)
    y_prompt = rmsnorm(h_p, final_norm)
    y_sample = rmsnorm(h_s, final_norm)
    return (y_prompt, y_sample)
```

```python
import numpy as np
from contextlib import ExitStack
import ml_dtypes
import concourse.bass as bass
import concourse.mybir as mybir
from concourse.bass_utils import run_bass_kernel_spmd

F32 = mybir.dt.float32
BF16 = mybir.dt.bfloat16
U32 = mybir.dt.uint32
I32 = mybir.dt.int32
AF = mybir.ActivationFunctionType
ALU = mybir.AluOpType

D = 1024
TP = 4096
TS = 16384
NCORES = 8
EPS = 1e-6
NEG = -30000.0
NTOK2 = 6144


class Buf:
    __slots__ = ("name", "last_w", "readers", "t")

    def __init__(self, t=None, name=""):
        self.name = name
        self.t = t
        self.last_w = None
        self.readers = []

    def __getitem__(self, k):
        return self.t[k]


class Trk:
    ENGS = ("pe", "act", "dve", "pool", "sp")
    NS = 8

    def __init__(self, nc, stack, tag, ns=None):
        self.nc = nc
        ns = ns or {}
        self.items = {e: [] for e in self.ENGS}
        self.sem = {}
        self.count = {e: 0 for e in self.ENGS}
        self.seen = {e: {} for e in self.ENGS}
        self.dma_n = {e: 0 for e in self.ENGS}
        self.dma_sems = {}
        self.semobj = {}
        self.ninst = 0
        for e in self.ENGS:
            s = stack.enter_context(nc.semaphore(f"{tag}_c_{e}"))
            self.sem[e] = s
            self.semobj[id(s)] = s
        for e in ("sp", "pool"):
            self.dma_sems[e] = []
            for i in range(ns.get(e, self.NS)):
                s = stack.enter_context(nc.semaphore(f"{tag}_d_{e}{i}"))
                self.dma_sems[e].append(s)
                self.semobj[id(s)] = s

    def _deps(self, eng, reads, writes, extra=(), skip_self=False):
        need = {}
        own = id(self.sem[eng])

        def add(tok):
            if tok is None:
                return
            k, v = tok
            if skip_self and k == own:
                return
            if need.get(k, 0) < v:
                need[k] = v
        for r in reads:
            add(r.last_w)
        for w in writes:
            add(w.last_w)
            for t in w.readers:
                add(t)
        for t in extra:
            add(t)
        seen = self.seen[eng]
        waits = []
        for k, v in need.items():
            if seen.get(k, 0) < v:
                seen[k] = v
                waits.append((self.semobj[k], v))
        return waits

    def _finish(self, tok, reads, writes):
        for w in writes:
            w.last_w = tok
            w.readers = []
        for r in reads:
            if r not in writes:
                if len(r.readers) > 64:
                    r.readers = r.readers[-32:]
                r.readers.append(tok)

    def op(self, eng, name, kw, reads=(), writes=()):
        return self.group(eng, [(name, kw)], reads, writes)

    def group(self, eng, fns, reads=(), writes=()):
        waits = self._deps(eng, reads, writes, skip_self=(eng == "pe"))
        self.count[eng] += 1
        tok = (id(self.sem[eng]), self.count[eng])
        self.items[eng].append((waits, list(fns), (self.sem[eng], 1)))
        self._finish(tok, reads, writes)
        self.ninst += len(fns)
        return tok

    def dma(self, eng, name, kw, reads=(), writes=()):
        fn = (name, kw)
        n = self.dma_n[eng]
        self.dma_n[eng] += 1
        nsl = len(self.dma_sems[eng])
        slot = n % nsl
        val = 16 * (n // nsl + 1)
        s = self.dma_sems[eng][slot]
        extra = [(id(s), val - 16)] if val > 16 else []
        waits = self._deps(eng, reads, writes, extra)
        tok = (id(s), val)
        self.items[eng].append((waits, [fn], (s, 16)))
        self._finish(tok, reads, writes)
        self.ninst += 1
        return tok

    def final_waits(self, eng, bufs):
        waits = self._deps(eng, bufs, [])
        self.items[eng].append((waits, [], None))

    def emit(self):
        nc = self.nc
        items = self.items

        def run(e, lst):
            for waits, fns, inc in lst:
                for s, v in waits:
                    e.wait_ge(s, v)
                ins = None
                for name, kw in fns:
                    ins = getattr(e, name)(**kw)
                if inc is not None and ins is not None:
                    ins.then_inc(inc[0], inc[1])

        with nc.Block() as block:
            @block.tensor
            def _(e):
                run(e, items["pe"])

            @block.scalar
            def _(e):
                run(e, items["act"])

            @block.vector
            def _(e):
                run(e, items["dve"])

            @block.gpsimd
            def _(e):
                run(e, items["pool"])

            @block.sync
            def _(e):
                run(e, items["sp"])
        self.items = {e: [] for e in self.ENGS}


class Ctx:
    def __init__(self, nc, st):
        self.nc = nc
        self.st = st
        self.ntrk = 0
        self.T = Trk(nc, st, "k0")
        self.n = 0
        self.dbufs = {}
        self.subs = []
        self.sub = None
        self.all = []
        self.ninst = 0

    def begin(self):
        self.subs.append(ExitStack())
        self.sub = self.subs[-1]

    def release_all(self):
        if any(self.T.items[e] for e in self.T.ENGS):
            self.T.final_waits("sp", list(self.dbufs.values()))
            self.T.final_waits("pool", list(self.dbufs.values()))
            self.T.emit()
        self.subs.pop().close()
        self.sub = self.subs[-1] if self.subs else None

    def switch_trk(self, ns=None):
        assert not any(self.T.items[e] for e in self.T.ENGS)
        self.ninst += self.T.ninst
        self.ntrk += 1
        self.T = Trk(self.nc, self.st, f"k{self.ntrk}", ns=ns)
        for b in self.all:
            b.last_w = None
            b.readers = []
        return self.T

    def _reg(self, b):
        self.all.append(b)
        return b

    def sb(self, shape, dt, name=None):
        self.n += 1
        st = self.sub or self.st
        t = st.enter_context(self.nc.sbuf_tensor(name or f"sb{self.n}", shape, dt))
        return self._reg(Buf(t))

    def ps(self, shape, dt, name=None):
        self.n += 1
        st = self.sub or self.st
        t = st.enter_context(self.nc.psum_tensor(name or f"ps{self.n}", shape, dt))
        return self._reg(Buf(t))

    def ring(self, n, shape, dt, psum=False):
        return [(self.ps if psum else self.sb)(shape, dt) for _ in range(n)]

    def db(self, key):
        b = self.dbufs.get(key)
        if b is None:
            b = self._reg(Buf(None, str(key)))
            self.dbufs[key] = b
        return b


SEQS = (("p", TP), ("s", TS))
OWN = {"p": range(0, 4), "s": range(4, 12)}
NROWS_MIX = 512 * (TP // 128) + 512 * (TS // 128)


def build_fused(small=False):
    nc = bass.Bass("TRN2", target_bir_lowering=False)

    def din(name, shape, dt=F32):
        return nc.dram_tensor(name, list(shape), dt, kind="ExternalInput").ap()

    def dscr(name, shape, dt=F32):
        return nc.dram_tensor(name, list(shape), dt).ap()

    TN = dict(SEQS)
    x_in = {"p": din("xp", [TP, D]), "s": din("xs", [TS, D])}
    xo_in = din("x", [NTOK2, D])
    wdn_in = din("wdn", [4, D, 1024])
    wkv_in = din("wkv", [D, 256])
    wqa_in = din("wqa", [D, 512])
    anorm_in = din("anorm", [128, 8])
    convw_in = din("convw", [4, 128, 15])
    gatep_in = din("gatep", [4, 128, 4])
    dnorm_in = din("dnorm", [128, 1])
    qkg_in = din("qkg", [128, 2])
    ident_in = din("ident", [128, 128])
    ones_in = din("ones", [128, 128])
    onesbd_in = din("onesbd", [128, 128])
    pmask_in = din("pmask", [2, 128, 128])
    strict_in = din("strict", [2, 128, 128])
    permT_in = din("permT", [128, 128])
    cs_in = (din("cosT", [128, TS]), din("sinT", [128, TS]))
    cso_in = (din("cosO", [128, NTOK2]), din("sinO", [128, NTOK2]))
    JOBS = [(sn, h) for sn, _ in SEQS for h in range(4)]
    dn_pre = {jb: dscr(f"dnpre_{jb[0]}{jb[1]}", [8, 128, TN[jb[0]]]) for jb in JOBS}
    att_k = {sn: dscr(f"attk_{sn}", [128, Tn]) for sn, Tn in SEQS}
    att_v = {sn: dscr(f"attv_{sn}", [Tn, 128], BF16) for sn, Tn in SEQS}
    att_kr = {sn: dscr(f"attkr_{sn}", [128, Tn], BF16) for sn, Tn in SEQS}
    att_q = dscr("attq", [512, NTOK2])
    att_qr = dscr("attqr", [512, NTOK2], BF16)
    dnop = {(jb, d): dscr(f"dnop_{jb[0]}{jb[1]}_{d}", [TN[jb[0]] // 128, 5, 128, 128]) for jb in JOBS for d in range(2)}
    dnd = {(jb, d): dscr(f"dnd_{jb[0]}{jb[1]}_{d}", [128, TN[jb[0]] // 128]) for jb in JOBS for d in range(2)}
    dno = {(jb, d): dscr(f"dno_{jb[0]}{jb[1]}_{d}", [128, TN[jb[0]]]) for jb in JOBS for d in range(2)}
    dnmix = dscr("dnmix", [NROWS_MIX, 128], BF16)
    dnmix_v = {"p": dnmix[0:512 * (TP // 128), :].rearrange("(ch b) t -> ch (b t)", b=TP // 128),
               "s": dnmix[512 * (TP // 128):NROWS_MIX, :].rearrange("(ch b) t -> ch (b t)", b=TS // 128)}
    attmix = dscr("attmix", [512, NTOK2], BF16)

    with ExitStack() as st:
        C = Ctx(nc, st)
        TB = [C.T]

        def DMA(out, in_, r, w, eng="sp"):
            return TB[0].dma(eng, "dma_start", dict(out=out, in_=in_), r, w)

        def ACT(out, in_, func, r, w, **kw):
            return TB[0].op("act", "activation", dict(out=out, in_=in_, func=func, **kw), r, w)

        def DVE(name, r, w, **kw):
            return TB[0].op("dve", name, kw, r, w)

        def POOL(name, r, w, **kw):
            return TB[0].op("pool", name, kw, r, w)

        def MM(out, lhsT, rhs, r, w, start=True, stop=True):
            return TB[0].op("pe", "matmul", dict(out=out, lhsT=lhsT, rhs=rhs, start=start, stop=stop), r, w)

        def MMG(lst, r, w):
            return TB[0].group("pe", [("matmul", dict(out=o, lhsT=l, rhs=rh, start=(i == 0), stop=(i == len(lst) - 1))) for i, (o, l, rh) in enumerate(lst)], r, w)

        def TR(out, in_, idn, r, w):
            return TB[0].op("pe", "transpose", dict(out=out, in_=in_, identity=idn), r, w)

        ev = [0]

        def evac(out_ap, in_ap, reads, writes):
            ev[0] += 1
            if ev[0] % 2:
                TB[0].op("act", "copy", dict(out=out_ap, in_=in_ap), reads, writes)
            else:
                TB[0].op("dve", "tensor_copy", dict(out=out_ap, in_=in_ap), reads, writes)

        def rotate(ns=None):
            TB[0] = C.switch_trk(ns)

        C.begin()
        ident = C.sb([128, 128], F32)
        identb = C.sb([128, 128], BF16)
        ones = C.sb([128, 128], F32)
        onesbd = C.sb([128, 128], F32)
        pmask = C.sb([128, 2, 128], F32)
        strict = C.sb([128, 2, 128], F32)
        permT = C.sb([128, 128], F32)
        anorm = C.sb([128, 8], F32)
        convw = C.sb([128, 4, 15], F32)
        gatep = C.sb([128, 4, 4], F32)
        negA = C.sb([128, 4, 2], F32)
        dnorm = C.sb([128, 1], F32)
        qkg = C.sb([128, 2], F32)
        msk = C.sb([128, 512], F32)

        def load_consts():
            DMA(ident[:], ident_in, [], [ident])
            DMA(ones[:], ones_in, [], [ones])
            DMA(onesbd[:], onesbd_in, [], [onesbd])
            DMA(pmask[:], pmask_in.rearrange("d p c -> p d c"), [], [pmask])
            DMA(strict[:], strict_in.rearrange("d p c -> p d c"), [], [strict])
            DMA(permT[:], permT_in, [], [permT])
            DMA(anorm[:], anorm_in, [], [anorm])
            DMA(convw[:], convw_in.rearrange("j p k -> p j k"), [], [convw])
            DMA(gatep[:], gatep_in.rearrange("j p k -> p j k"), [], [gatep])
            DMA(dnorm[:], dnorm_in, [], [dnorm])
            DMA(qkg[:], qkg_in, [], [qkg])
            DVE("tensor_copy", [ident], [identb], out=identb[:], in_=ident[:])
            ACT(negA[:], gatep[:, :, 0:2], AF.Exp, [gatep], [negA])
            DVE("tensor_scalar_mul", [negA], [negA], out=negA[:], in0=negA[:], scalar1=-1.0)
            POOL("memset", [], [msk], ap=msk[:], constant=1.0)
            POOL("memset", [], [msk], ap=msk[:].rearrange("p (n c) -> p n c", c=128)[:, :, 0:1], constant=0.0)
        load_consts()

        C.begin()
        wdn = C.sb([128, 4, 8, 1024], BF16)
        wkv = C.sb([128, 8, 256], BF16)
        wqa = C.sb([128, 8, 512], BF16)
        wst = C.ring(2, [128, 1024], F32)
        k = 0
        for h in range(4):
            for c in range(8):
                w = wst[k % 2]
                k += 1
                DMA(w[:], wdn_in[h, c * 128:(c + 1) * 128, :], [], [w])
                DVE("tensor_scalar_mul", [w, anorm], [wdn], out=wdn[:, h, c, :], in0=w[:], scalar1=anorm[:, c:c + 1])
        for src, dst, ncol in ((wkv_in, wkv, 256), (wqa_in, wqa, 512)):
            for c in range(8):
                w = wst[k % 2]
                k += 1
                DMA(w[:, 0:ncol], src[c * 128:(c + 1) * 128, :], [], [w])
                DVE("tensor_scalar_mul", [w, anorm], [dst], out=dst[:, c, :], in0=w[:, 0:ncol], scalar1=anorm[:, c:c + 1])
        xt = C.ring(2, [128, 4, 1024], F32)
        junk = C.sb([128, 1024], BF16)
        ss = C.ring(2, [128, 4], F32)
        xn = C.ring(2, [128, 4, 1024], BF16)
        xnT = C.ring(2, [128, 8, 512], BF16)
        pT = C.ring(2, [128, 512], BF16, psum=True)
        psA = C.ring(4, [128, 512], F32, psum=True)
        psV = C.ps([128, 4, 128], F32)
        stg = C.ring(4, [128, 512], F32)
        vstg = C.ring(2, [128, 4, 128], BF16)
        work = [("full", sn, i) for sn, Tn in SEQS for i in range(2 if small else Tn // 512)]
        work += [("own", None, ot) for ot in (range(0, 12, 6) if small else range(12))]

        def load_x(wi):
            kind, sn, i = work[wi]
            b = xt[wi % 2]
            src = x_in[sn] if kind == "full" else xo_in
            DMA(b[:], src[i * 512:(i + 1) * 512, :].rearrange("(b p) d -> p b d", p=128), [], [b])

        load_x(0)
        kk = 0
        for wi, (kind, sn, i) in enumerate(work):
            if wi + 1 < len(work):
                load_x(wi + 1)
            x_b, ss_b, xn_b, xnT_b = xt[wi % 2], ss[wi % 2], xn[wi % 2], xnT[wi % 2]
            cols = slice(i * 512, (i + 1) * 512)
            for b in range(4):
                ACT(junk[:], x_b[:, b, :], AF.Square, [x_b], [junk, ss_b], accum_out=ss_b[:, b:b + 1])
            ACT(ss_b[:], ss_b[:], AF.Sqrt, [ss_b], [ss_b], scale=1.0 / D, bias=EPS)
            DVE("reciprocal", [ss_b], [ss_b], out=ss_b[:], in_=ss_b[:])
            for b in range(4):
                DVE("tensor_scalar_mul", [x_b, ss_b], [xn_b], out=xn_b[:, b, :], in0=x_b[:, b, :], scalar1=ss_b[:, b:b + 1])
            for c in range(8):
                p = pT[c % 2]
                TB[0].group("pe", [("transpose", dict(out=p[:, b * 128:(b + 1) * 128], in_=xn_b[:, b, c * 128:(c + 1) * 128], identity=identb[:])) for b in range(4)], [xn_b, identb], [p])
                evac(xnT_b[:, c, :], p[:], [p], [xnT_b])
            if kind == "full":
                for h in range(4):
                    for cc in range(8):
                        p, s_ = psA[kk % 4], stg[kk % 4]
                        kk += 1
                        MMG([(p[:], wdn[:, h, c, cc * 128:(cc + 1) * 128], xnT_b[:, c, :]) for c in range(8)], [wdn, xnT_b], [p])
                        evac(s_[:], p[:], [p], [s_])
                        DMA(dn_pre[(sn, h)][cc, :, cols], s_[:], [s_], [C.db(("dnpre", sn, h, cc, i))])
                p, s_ = psA[kk % 4], stg[kk % 4]
                kk += 1
                MMG([(p[:], wkv[:, c, 0:128], xnT_b[:, c, :]) for c in range(8)], [wkv, xnT_b], [p])
                evac(s_[:], p[:], [p], [s_])
                DMA(att_k[sn][:, cols], s_[:], [s_], [C.db(("attk", sn, i))])
                for b in range(4):
                    MMG([(psV[:, b, :], xnT_b[:, c, b * 128:(b + 1) * 128], wkv[:, c, 128:256]) for c in range(8)], [wkv, xnT_b], [psV])
                vs = vstg[wi % 2]
                evac(vs[:], psV[:], [psV], [vs])
                DMA(att_v[sn][cols, :].rearrange("(b p) d -> p b d", p=128), vs[:], [vs], [C.db(("attv", sn, i))])
            else:
                for qc in range(4):
                    p, s_ = psA[kk % 4], stg[kk % 4]
                    kk += 1
                    MMG([(p[:], wqa[:, c, qc * 128:(qc + 1) * 128], xnT_b[:, c, :]) for c in range(8)], [wqa, xnT_b], [p])
                    evac(s_[:], p[:], [p], [s_])
                    DMA(att_q[qc * 128:(qc + 1) * 128, cols], s_[:], [s_], [C.db(("attq", qc, i))])
        C.release_all()

        def p1b(jobs):
            C.begin()
            pre = C.ring(2, [128, 3, 516], F32)
            gts = C.ring(2, [128, 4, 512], F32)
            cv = C.sb([128, 3, 512], F32)
            sq = C.sb([128, 2, 512], F32)
            rn = C.sb([128, 2, 512], F32)
            qn = C.sb([128, 512], F32)
            kn = C.sb([128, 512], F32)
            sig = C.sb([128, 2, 512], F32)
            gl = C.sb([128, 2, 512], F32)
            gc = C.sb([128, 2, 512], F32)
            tmpb = C.sb([128, 512], F32)
            egc = C.sb([128, 512], F32)
            kb = C.ring(2, [128, 512], F32)
            kbg = C.ring(2, [128, 512], F32)
            qt = C.ring(2, [128, 512], F32)
            vb = C.ring(2, [128, 512], F32)
            ek = C.sb([128, 512], F32)
            kt = C.ring(2, [128, 512], F32)
            dec = C.ring(2, [128, 4], F32)
            psN = C.ring(2, [128, 512], F32, psum=True)
            psX = C.ring(6, [128, 128], F32, psum=True)
            Vb = C.ring(2, [128, 128], F32)
            Kbg = C.ring(2, [128, 128], F32)
            opst = [C.ring(2, [128, 5, 128], F32) for _ in range(2)]
            dif = C.ring(2, [128, 128], F32)
            dcy = C.ring(2, [128, 128], F32)
            dcyS = C.ring(2, [128, 128], F32)
            Pm = [C.ring(2, [128, 128], F32) for _ in range(2)]
            PTm = [C.ring(2, [128, 128], F32) for _ in range(2)]
            TT = [C.ring(2, [128, 128], F32) for _ in range(2)]
            px = [0]

            def nps():
                px[0] += 1
                return psX[px[0] % 6]

            un = 0
            for jb in jobs:
                sn, h = jb
                Tn = TN[sn]
                ntl = 2 if small else Tn // 512
                for i in range(ntl):
                    pr, gt = pre[i % 2], gts[i % 2]
                    lo, hi = i * 512 - 2, i * 512 + 514
                    clo, chi = max(lo, 0), min(hi, Tn)
                    if lo < 0:
                        POOL("memset", [], [pr], ap=pr[:, :, 0:2], constant=0.0)
                    if hi > Tn:
                        POOL("memset", [], [pr], ap=pr[:, :, 514:516], constant=0.0)
                    DMA(pr[:, :, clo - lo:chi - lo], dn_pre[jb][0:3, :, clo:chi].rearrange("m p t -> p m t"),
                        [C.db(("dnpre", sn, h, m_, t_)) for m_ in range(3) for t_ in (i - 1, i, i + 1) if 0 <= t_ < Tn // 512], [pr])
                    DMA(gt[:], dn_pre[jb][4:8, :, i * 512:(i + 1) * 512].rearrange("m p t -> p m t"), [C.db(("dnpre", sn, h, m_, i)) for m_ in range(4, 8)], [gt])
                    for m in range(3):
                        DVE("tensor_scalar_mul", [pr, convw], [cv], out=cv[:, m, :], in0=pr[:, m, 0:512], scalar1=convw[:, h, m * 5:m * 5 + 1])
                        for tap in range(1, 5):
                            DVE("scalar_tensor_tensor", [pr, convw, cv], [cv], out=cv[:, m, :], in0=pr[:, m, tap:tap + 512],
                                scalar=convw[:, h, m * 5 + tap:m * 5 + tap + 1], in1=cv[:, m, :], op0=ALU.mult, op1=ALU.add)
                    ACT(cv[:], cv[:], AF.Silu, [cv], [cv])
                    ACT(sq[:], cv[:, 0:2, :], AF.Square, [cv], [sq])
                    for m in range(2):
                        MM(psN[m][:], ones[:], sq[:, m, :], [ones, sq], [psN[m]])
                        ACT(rn[:, m, :], psN[m][:], AF.Sqrt, [psN[m]], [rn], bias=EPS)
                    DVE("reciprocal", [rn], [rn], out=rn[:], in_=rn[:])
                    DVE("scalar_tensor_tensor", [cv, rn], [qn], out=qn[:], in0=cv[:, 0, :], scalar=float(128 ** -0.5), in1=rn[:, 0, :], op0=ALU.mult, op1=ALU.mult)
                    DVE("tensor_tensor", [cv, rn], [kn], out=kn[:], in0=cv[:, 1, :], in1=rn[:, 1, :], op=ALU.mult)
                    ACT(sig[:], gt[:, 0:2, :], AF.Sigmoid, [gt], [sig])
                    for d in range(2):
                        ACT(gl[:, d, :], gt[:, 2 + d, :], AF.Exp, [gt, gatep], [gl], bias=gatep[:, h, 2 + d:3 + d])
                        ACT(gl[:, d, :], gl[:, d, :], AF.Ln, [gl], [gl], bias=1.0)
                        DVE("tensor_scalar_mul", [gl, negA], [gl], out=gl[:, d, :], in0=gl[:, d, :], scalar1=negA[:, h, d:d + 1])
                    DVE("tensor_tensor_scan", [msk, gl], [gc], out=gc[:, 0, :], data0=msk[:], data1=gl[:, 0, :], initial=0.0, op0=ALU.mult, op1=ALU.add)
                    DVE("tensor_tensor_scan", [msk, gl], [gc], out=gc[:, 1, :], data0=msk[:], data1=gl[:, 1, :], initial=0.0, op0=ALU.mult, op1=ALU.add)
                    DVE("tensor_tensor", [gl, gc], [tmpb], out=tmpb[:], in0=gl[:, 1, :], in1=gc[:, 1, :], op=ALU.subtract)
                    for u in range(4):
                        DVE("tensor_scalar_add", [tmpb, gc], [gc], out=gc[:, 1, u * 128:(u + 1) * 128], in0=tmpb[:, u * 128:(u + 1) * 128], scalar1=gc[:, 1, u * 128 + 127:u * 128 + 128])
                    for d in range(2):
                        qt_b, dec_b = qt[d], dec[d]
                        lastc = 127 if d == 0 else 0
                        ACT(egc[:], gc[:, d, :], AF.Exp, [gc], [egc])
                        POOL("tensor_tensor", [kn, sig], [kb[d]], out=kb[d][:], in0=kn[:], in1=sig[:, d, :], op=ALU.mult)
                        POOL("tensor_tensor", [kb[d], egc], [kbg[d]], out=kbg[d][:], in0=kb[d][:], in1=egc[:], op=ALU.mult)
                        POOL("tensor_tensor", [qn, egc], [qt_b], out=qt_b[:], in0=qn[:], in1=egc[:], op=ALU.mult)
                        POOL("tensor_tensor", [cv, sig], [vb[d]], out=vb[d][:], in0=cv[:, 2, :], in1=sig[:, d, :], op=ALU.mult)
                        for u in range(4):
                            ACT(ek[:, u * 128:(u + 1) * 128], gc[:, d, u * 128:(u + 1) * 128], AF.Exp, [gc], [ek], scale=-1.0, bias=gc[:, d, u * 128 + lastc:u * 128 + lastc + 1])
                        POOL("tensor_tensor", [kn, ek], [kt[d]], out=kt[d][:], in0=kn[:], in1=ek[:], op=ALU.mult)
                        ACT(dec_b[:], gc[:, d, :].rearrange("p (u c) -> p u c", c=128)[:, :, lastc], AF.Exp, [gc], [dec_b])
                        DMA(dnd[(jb, d)][:, i * 4:(i + 1) * 4], dec_b[:], [dec_b], [C.db(("dnd", jb, d))])
                    for u in range(4):
                        n = i * 4 + u
                        un += 1
                        sl = slice(u * 128, (u + 1) * 128)
                        osd = [opst[d][un % 2] for d in range(2)]
                        R = [dict() for _ in range(2)]
                        DD = (0, 1)
                        for d in DD:
                            R[d]["p1"], R[d]["p2"], R[d]["p3"] = nps(), nps(), nps()
                            TR(R[d]["p1"][:], vb[d][:, sl], ident[:], [vb[d], ident], [R[d]["p1"]])
                            TR(R[d]["p2"][:], kbg[d][:, sl], ident[:], [kbg[d], ident], [R[d]["p2"]])
                            TR(R[d]["p3"][:], kt[d][:, sl], ident[:], [kt[d], ident], [R[d]["p3"]])
                        for d in DD:
                            evac(Vb[d][:], R[d]["p1"][:], [R[d]["p1"]], [Vb[d]])
                            evac(Kbg[d][:], R[d]["p2"][:], [R[d]["p2"]], [Kbg[d]])
                            evac(osd[d][:, 4, :], R[d]["p3"][:], [R[d]["p3"]], [osd[d]])
                        for d in DD:
                            R[d]["p4"] = nps()
                            MMG([(R[d]["p4"][:], gc[:, d, sl], ident[:]), (R[d]["p4"][:], ident[:], pmask[:, d, :])], [gc, ident, pmask], [R[d]["p4"]])
                        for d in DD:
                            DVE("tensor_tensor", [gc, R[d]["p4"]], [dif[d]], out=dif[d][:], in0=gc[:, d, sl], in1=R[d]["p4"][:], op=ALU.subtract)
                        for d in DD:
                            ACT(dcy[d][:], dif[d][:], AF.Exp, [dif[d]], [dcy[d]])
                        for d in DD:
                            POOL("tensor_tensor", [dcy[d], strict], [dcyS[d]], out=dcyS[d][:], in0=dcy[d][:], in1=strict[:, d, :], op=ALU.mult)
                        for d in DD:
                            R[d]["p5"], R[d]["p6"] = nps(), nps()
                            MM(R[d]["p5"][:], kn[:, sl], kb[d][:, sl], [kn, kb[d]], [R[d]["p5"]])
                            MM(R[d]["p6"][:], kn[:, sl], qn[:, sl], [kn, qn], [R[d]["p6"]])
                        for d in DD:
                            DVE("tensor_tensor", [R[d]["p5"], dcyS[d]], [PTm[d][0]], out=PTm[d][0][:], in0=R[d]["p5"][:], in1=dcyS[d][:], op=ALU.mult)
                            DVE("tensor_tensor", [R[d]["p6"], dcy[d]], [osd[d]], out=osd[d][:, 3, :], in0=R[d]["p6"][:], in1=dcy[d][:], op=ALU.mult)
                        for d in DD:
                            R[d]["p7"] = nps()
                            TR(R[d]["p7"][:], PTm[d][0][:], ident[:], [PTm[d][0], ident], [R[d]["p7"]])
                        for d in DD:
                            evac(Pm[d][0][:], R[d]["p7"][:], [R[d]["p7"]], [Pm[d][0]])
                            DVE("tensor_tensor", [ident, PTm[d][0]], [TT[d][0]], out=TT[d][0][:], in0=ident[:], in1=PTm[d][0][:], op=ALU.subtract)
                        cur = 0
                        for lv in range(1, 7):
                            nx = 1 - cur
                            for d in DD:
                                R[d]["pa"] = nps()
                                MM(R[d]["pa"][:], PTm[d][cur][:], Pm[d][cur][:], [PTm[d][cur], Pm[d][cur]], [R[d]["pa"]])
                                if lv < 6:
                                    R[d]["pb"] = nps()
                                    MM(R[d]["pb"][:], Pm[d][cur][:], PTm[d][cur][:], [PTm[d][cur], Pm[d][cur]], [R[d]["pb"]])
                            for d in DD:
                                evac(Pm[d][nx][:], R[d]["pa"][:], [R[d]["pa"]], [Pm[d][nx]])
                                if lv < 6:
                                    evac(PTm[d][nx][:], R[d]["pb"][:], [R[d]["pb"]], [PTm[d][nx]])
                            for d in DD:
                                R[d]["pc"] = nps()
                                MM(R[d]["pc"][:], Pm[d][nx][:], TT[d][cur][:], [Pm[d][nx], TT[d][cur]], [R[d]["pc"]])
                            for d in DD:
                                DVE("tensor_tensor", [R[d]["pc"], TT[d][cur]], [TT[d][nx]], out=TT[d][nx][:], in0=R[d]["pc"][:], in1=TT[d][cur][:], op=ALU.add)
                            cur = nx
                        for d in DD:
                            R[d]["p8"], R[d]["p9"] = nps(), nps()
                            MM(R[d]["p8"][:], TT[d][cur][:], Vb[d][:], [TT[d][cur], Vb[d]], [R[d]["p8"]])
                            MM(R[d]["p9"][:], Kbg[d][:], TT[d][cur][:], [TT[d][cur], Kbg[d]], [R[d]["p9"]])
                        for d in DD:
                            evac(osd[d][:, 0, :], R[d]["p8"][:], [R[d]["p8"]], [osd[d]])
                            evac(osd[d][:, 1, :], R[d]["p9"][:], [R[d]["p9"]], [osd[d]])
                            POOL("tensor_copy", [qt[d]], [osd[d]], out=osd[d][:, 2, :], in_=qt[d][:, sl])
                        for d in DD:
                            DMA(dnop[(jb, d)][n].rearrange("k p c -> p k c"), osd[d][:], [osd[d]], [C.db(("dnop", jb, d, n))])
            C.release_all()

        p1b([("p", h) for h in range(4)] + [("s", 0), ("s", 1)])
        rotate()
        p1b([("s", 2), ("s", 3)])

        C.begin()
        chains = [(jb, d) for jb in ([("s", h) for h in range(4)] + [("p", h) for h in range(4)]) for d in range(2)]
        S = {ch: C.ring(2, [128, 128], F32) for ch in chains}
        decs = {ch: C.sb([128, TN[ch[0][0]] // 128], F32) for ch in chains}
        ops = {ch: C.ring(2, [128, 5, 128], F32) for ch in chains}
        vn = {ch: C.ring(2, [128, 128], F32) for ch in chains}
        ostg = C.ring(4, [128, 128], F32)
        psC = C.ring(8, [128, 128], F32, psum=True)
        pc_ = [0]

        def npc():
            pc_[0] += 1
            return psC[pc_[0] % 8]
        for ch in chains:
            DMA(decs[ch][:], dnd[ch], [C.db(("dnd", ch[0], ch[1]))], [decs[ch]])
            POOL("memset", [], [S[ch][0]], ap=S[ch][0][:], constant=0.0)
        nsteps = {ch: (8 if small else TN[ch[0][0]] // 128) for ch in chains}
        oc = 0
        GS = 4
        for step in range(max(nsteps.values())):
            act_ch = [ch for ch in chains if step < nsteps[ch]]
            for g0 in range(0, len(act_ch), GS):
                grp = act_ch[g0:g0 + GS]
                info = {}
                for ch in grp:
                    jb, d = ch
                    ns = nsteps[ch]
                    nun = TN[jb[0]] // 128
                    n = step if d == 0 else (ns - 1 - step if small else nun - 1 - step)
                    o_ = ops[ch][step % 2]
                    DMA(o_[:], dnop[ch][n].rearrange("k p c -> p k c"), [C.db(("dnop", jb, d, n))], [o_])
                    info[ch] = dict(n=n, o=o_, Sc=S[ch][step % 2], Sn=S[ch][(step + 1) % 2], v=vn[ch][step % 2])
                for ch in grp:
                    I = info[ch]
                    I["pa"] = npc()
                    MM(I["pa"][:], I["o"][:, 1, :], I["Sc"][:], [I["o"], I["Sc"]], [I["pa"]])
                for ch in grp:
                    I = info[ch]
                    DVE("tensor_tensor", [I["o"], I["pa"]], [I["v"]], out=I["v"][:], in0=I["o"][:, 0, :], in1=I["pa"][:], op=ALU.subtract)
                for ch in grp:
                    I = info[ch]
                    I["pb"], I["pc"] = npc(), npc()
                    MMG([(I["pb"][:], I["Sc"][:], I["o"][:, 2, :]), (I["pb"][:], I["v"][:], I["o"][:, 3, :])], [I["Sc"], I["o"], I["v"]], [I["pb"]])
                    MM(I["pc"][:], I["o"][:, 4, :], I["v"][:], [I["o"], I["v"]], [I["pc"]])
                for ch in grp:
                    jb, d = ch
                    I = info[ch]
                    n = I["n"]
                    og = ostg[oc % 4]
                    oc += 1
                    TB[0].op("act", "copy", dict(out=og[:], in_=I["pb"][:]), [I["pb"]], [og])
                    DMA(dno[ch][:, n * 128:(n + 1) * 128], og[:], [og], [C.db(("dno", jb, d, n // 4))])
                    DVE("scalar_tensor_tensor", [I["Sc"], decs[ch], I["pc"]], [I["Sn"]], out=I["Sn"][:], in0=I["Sc"][:], scalar=decs[ch][:, n:n + 1], in1=I["pc"][:], op0=ALU.mult, op1=ALU.add)
        C.release_all()

        C.begin()
        of_ = C.ring(2, [128, 2, 512], F32)
        zt = C.ring(2, [128, 512], F32)
        osum = C.sb([128, 512], F32)
        osq = C.sb([128, 512], F32)
        orn = C.sb([128, 512], F32)
        res = C.ring(2, [128, 512], BF16)
        psD = C.ring(2, [128, 512], F32, psum=True)
        k = 0
        for jb in JOBS:
            sn, h = jb
            Tn = TN[sn]
            for i in range(2 if small else Tn // 512):
                o_b, z_b, r_b, p_ = of_[k % 2], zt[k % 2], res[k % 2], psD[k % 2]
                k += 1
                for d in range(2):
                    DMA(o_b[:, d, :], dno[(jb, d)][:, i * 512:(i + 1) * 512], [C.db(("dno", jb, d, i))], [o_b])
                DMA(z_b[:], dn_pre[jb][3, :, i * 512:(i + 1) * 512], [C.db(("dnpre", sn, h, 3, i))], [z_b])
                DVE("tensor_tensor", [o_b], [osum], out=osum[:], in0=o_b[:, 0, :], in1=o_b[:, 1, :], op=ALU.add)
                ACT(osq[:], osum[:], AF.Square, [osum], [osq])
                MM(p_[:], ones[:], osq[:], [ones, osq], [p_])
                ACT(orn[:], p_[:], AF.Sqrt, [p_], [orn], scale=1.0 / 128, bias=EPS)
                DVE("reciprocal", [orn], [orn], out=orn[:], in_=orn[:])
                DVE("tensor_tensor", [osum, orn], [osum], out=osum[:], in0=osum[:], in1=orn[:], op=ALU.mult)
                ACT(z_b[:], z_b[:], AF.Silu, [z_b], [z_b])
                DVE("scalar_tensor_tensor", [osum, dnorm, z_b], [r_b], out=r_b[:], in0=osum[:], scalar=dnorm[:, 0:1], in1=z_b[:], op0=ALU.mult, op1=ALU.mult)
                DMA(dnmix_v[sn][h * 128:(h + 1) * 128, i * 512:(i + 1) * 512], r_b[:], [r_b], [C.db(("dnmix",))])
        C.release_all()
        rotate()

        C.begin()
        qk = C.ring(2, [128, 512], F32)
        cs = C.ring(2, [128, 2, 512], F32)
        asq = C.sb([128, 512], F32)
        arn = C.sb([128, 512], F32)
        aqn = C.sb([128, 512], F32)
        t1 = C.sb([128, 512], F32)
        t2 = C.sb([128, 512], F32)
        qr = C.ring(2, [128, 512], BF16)
        psE = C.ring(2, [128, 512], F32, psum=True)
        psR = C.ring(2, [128, 512], F32, psum=True)
        k = 0
        ework = [("k", sn, i) for sn, Tn in SEQS for i in range(2 if small else Tn // 512)]
        ework += [("q", None, ot) for ot in (range(0, 12, 6) if small else range(12))]
        for ci, (kind, sn, i) in enumerate(ework):
            c_b = cs[ci % 2]
            cols = slice(i * 512, (i + 1) * 512)
            tabs = cs_in if kind == "k" else cso_in
            DMA(c_b[:, 0, :], tabs[0][:, cols], [], [c_b])
            DMA(c_b[:, 1, :], tabs[1][:, cols], [], [c_b])
            for qc in range(1 if kind == "k" else 4):
                gcol = 1 if kind == "k" else 0
                q_b, o_b, pe_, pr_ = qk[k % 2], qr[k % 2], psE[k % 2], psR[k % 2]
                k += 1
                if kind == "k":
                    DMA(q_b[:], att_k[sn][:, cols], [C.db(("attk", sn, i))], [q_b])
                else:
                    DMA(q_b[:], att_q[qc * 128:(qc + 1) * 128, cols], [C.db(("attq", qc, i))], [q_b])
                ACT(asq[:], q_b[:], AF.Square, [q_b], [asq])
                MM(pe_[:], onesbd[:], asq[:], [onesbd, asq], [pe_])
                ACT(arn[:], pe_[:], AF.Sqrt, [pe_], [arn], scale=1.0 / 64, bias=EPS)
                DVE("reciprocal", [arn], [arn], out=arn[:], in_=arn[:])
                DVE("scalar_tensor_tensor", [q_b, qkg, arn], [aqn], out=aqn[:], in0=q_b[:], scalar=qkg[:, gcol:gcol + 1], in1=arn[:], op0=ALU.mult, op1=ALU.mult)
                MM(pr_[:], permT[:], aqn[:], [permT, aqn], [pr_])
                DVE("tensor_tensor", [aqn, c_b], [t1], out=t1[:], in0=aqn[:], in1=c_b[:, 0, :], op=ALU.mult)
                DVE("tensor_tensor", [pr_, c_b], [t2], out=t2[:], in0=pr_[:], in1=c_b[:, 1, :], op=ALU.mult)
                POOL("tensor_tensor", [t1, t2], [o_b], out=o_b[:], in0=t1[:], in1=t2[:], op=ALU.add)
                if kind == "k":
                    DMA(att_kr[sn][:, cols], o_b[:], [o_b], [C.db(("attkr", sn, i))])
                else:
                    DMA(att_qr[qc * 128:(qc + 1) * 128, cols], o_b[:], [o_b], [C.db(("attqr", i))])
        C.release_all()

        C.begin()
        kr = C.sb([64, TS], BF16)
        vv = C.sb([128, TS // 128, 65], BF16)
        qh = C.ring(2, [64, 512], BF16)
        pT_ = C.ring(3, [128, 2, 512], BF16)
        psS = C.ring(2, [128, 2, 512], F32, psum=True)
        psO = C.ring(2, [128, 512], F32, psum=True)
        psB = C.ps([64, 512], F32)
        rs = C.sb([128, 512], F32)
        bc = C.sb([64, 512], F32)
        ao = C.ring(2, [64, 512], BF16)
        k = 0
        qi = 0
        for sn, Tn in SEQS:
            nkt = 8 if small else Tn // 128
            ots = [o for o in OWN[sn] if (not small or o % 6 == 0)]
            for kv in range(2):
                DMA(kr[:, 0:Tn], att_kr[sn][kv * 64:(kv + 1) * 64, :], [C.db(("attkr", sn, i)) for i in range(Tn // 512)], [kr])
                POOL("memset", [], [vv], ap=vv[:, 0:Tn // 128, 64:65], constant=1.0)
                DMA(vv[:, 0:Tn // 128, 0:64], att_v[sn][:, kv * 64:(kv + 1) * 64].rearrange("(k p) d -> p k d", p=128),
                    [C.db(("attv", sn, i)) for i in range(Tn // 512)], [vv])
                for qh_ in range(4):
                    h = kv * 4 + qh_
                    for ot in ots:
                        q_b, po, a_b = qh[qi % 2], psO[qi % 2], ao[qi % 2]
                        qi += 1
                        cols = slice(ot * 512, (ot + 1) * 512)
                        DMA(q_b[:], att_qr[h * 64:(h + 1) * 64, cols], [C.db(("attqr", ot))], [q_b])
                        npair = nkt // 2
                        k0 = k
                        k += npair

                        def emit_s(j):
                            ps_ = psS[(k0 + j) % 2]
                            for e_ in range(2):
                                kt_ = 2 * j + e_
                                MM(ps_[:, e_, :], kr[:, kt_ * 128:(kt_ + 1) * 128], q_b[:], [kr, q_b], [ps_])
                        for j in range(min(2, npair)):
                            emit_s(j)
                        for j in range(npair):
                            ps_, p_b = psS[(k0 + j) % 2], pT_[(k0 + j) % 3]
                            ACT(p_b[:], ps_[:], AF.Exp, [ps_], [p_b], scale=0.125)
                            for e_ in range(2):
                                kt_ = 2 * j + e_
                                MM(po[0:65, :], vv[:, kt_, :], p_b[:, e_, :], [vv, p_b], [po], start=(kt_ == 0), stop=(kt_ == nkt - 1))
                            if j + 2 < npair:
                                emit_s(j + 2)
                        DVE("reciprocal", [po], [rs], out=rs[64:65, :], in_=po[64:65, :])
                        MM(psB[:], ones[64:65, 0:64], rs[64:65, :], [ones, rs], [psB])
                        TB[0].op("act", "copy", dict(out=bc[:], in_=psB[:]), [psB], [bc])
                        DVE("tensor_tensor", [po, bc], [a_b], out=a_b[:], in0=po[0:64, :], in1=bc[:], op=ALU.mult)
                        DMA(attmix[h * 64:(h + 1) * 64, cols], a_b[:], [a_b], [C.db(("attmix", ot))])
        C.release_all()
        C.release_all()
        rotate({"pool": 16})

        build_p2_body(nc, C, TB[0], xo_in, dnmix, attmix, small)
        C.ninst += TB[0].ninst
        print("fused instructions", C.ninst)
    return nc


def build_p2_body(nc, C, T, x_in, dnmix, attmix, small):
    nblk = 2 if small else NTOK2 // 128

    def din(name, shape, dt=F32):
        return nc.dram_tensor(name, list(shape), dt, kind="ExternalInput").ap()

    p_in = din("p", [NTOK2, 256])
    wout_in = din("w_out", [D, D])
    wq_in = din("peer_query", [D, 2048])
    keysT_in = din("keysT", [128, 2, 128])
    pu_in = din("peer_u", [16384, D])
    pv_in = din("peer_v", [16384, D])
    pproj_in = din("ple_proj", [256, D])
    pgate_in = din("ple_gate", [D, D])
    gains_in = din("gains", [3, 128, D])
    ident_in = din("ident2", [128, 128])
    mixidx_in = din("mixidx", [128, (NTOK2 // 128) * 4], I32)
    y_out = nc.dram_tensor("y", [NTOK2, D], F32, kind="ExternalOutput").ap()

    if True:
        def DMA(out, in_, r, w, eng="sp"):
            return T.dma(eng, "dma_start", dict(out=out, in_=in_), r, w)

        def ACT(out, in_, func, r, w, **kw):
            return T.op("act", "activation", dict(out=out, in_=in_, func=func, **kw), r, w)

        def DVE(name, r, w, **kw):
            return T.op("dve", name, kw, r, w)

        def POOL(name, r, w, **kw):
            return T.op("pool", name, kw, r, w)

        def MM(out, lhsT, rhs, r, w, start=True, stop=True):
            return T.op("pe", "matmul", dict(out=out, lhsT=lhsT, rhs=rhs, start=start, stop=stop), r, w)

        def MMG(lst, r, w):
            return T.group("pe", [("matmul", dict(out=o, lhsT=l, rhs=rh, start=(i == 0), stop=(i == len(lst) - 1))) for i, (o, l, rh) in enumerate(lst)], r, w)

        ev = [0]

        def evac(out_ap, in_ap, reads, writes):
            ev[0] += 1
            if ev[0] % 2:
                T.op("act", "copy", dict(out=out_ap, in_=in_ap), reads, writes)
            else:
                T.op("dve", "tensor_copy", dict(out=out_ap, in_=in_ap), reads, writes)

        ident = C.sb([128, 128], F32)
        identb = C.sb([128, 128], BF16)
        gains = C.sb([128, 3, D], F32)
        wout = C.sb([128, 8, D], BF16)
        wq = C.sb([128, 8, 2048], BF16)
        wgate = C.sb([128, 8, D], BF16)
        wproj = C.sb([128, 2, D], BF16)
        keysf = C.sb([128, 2, 128], F32)
        keysT = C.sb([128, 2, 128], BF16)
        C.begin()
        wst = C.ring(2, [128, 2048], F32)
        DMA(ident[:], ident_in, [], [ident])
        DVE("tensor_copy", [ident], [identb], out=identb[:], in_=ident[:])
        DMA(gains[:], gains_in.rearrange("g p d -> p g d"), [], [gains])
        DMA(keysf[:], keysT_in, [], [keysf])
        DVE("tensor_copy", [keysf], [keysT], out=keysT[:], in_=keysf[:])
        k = 0
        for (src, dst, nch, ncol) in ((wout_in, wout, 8, D), (wq_in, wq, 8, 2048), (pgate_in, wgate, 8, D), (pproj_in, wproj, 2, D)):
            for c in range(nch):
                w = wst[k % 2]
                k += 1
                DMA(w[:, 0:ncol], src[c * 128:(c + 1) * 128, :], [], [w])
                evac(dst[:, c, :], w[:, 0:ncol], [w], [dst])
        puv16 = nc.dram_tensor("puv16", [16384, 2, D], BF16).ap()
        s32 = C.ring(2, [128, 4, D], F32)
        s16 = C.ring(2, [128, 4, D], BF16)
        k = 0
        for t_, src in enumerate((pu_in, pv_in)):
            for kk in range(16384 // 512):
                a, b_ = s32[k % 2], s16[k % 2]
                k += 1
                rows = slice(kk * 512, (kk + 1) * 512)
                DMA(a[:], src[rows, :].rearrange("(k p) d -> p k d", p=128), [], [a])
                evac(b_[:, 0:2, :], a[:, 0:2, :], [a], [b_])
                evac(b_[:, 2:4, :], a[:, 2:4, :], [a], [b_])
                DMA(puv16[rows, t_, :].rearrange("(k p) d -> p k d", p=128), b_[:], [b_], [C.db(("tab16", t_, kk))])
        puv_in = puv16.rearrange("e t d -> e (t d)")
        C.release_all()

        mixidx = C.sb([128, (NTOK2 // 128) * 4], I32)
        DMA(mixidx[:], mixidx_in, [], [mixidx])
        xt = C.ring(2, [128, D], F32)
        mxt = C.ring(2, [128, 8, 128], BF16)
        mxb = [[Buf(mxt[r_].t) for _ in range(8)] for r_ in range(2)]
        pt = C.ring(2, [128, 256], F32)
        h1 = C.sb([128, D], F32)
        junk = C.sb([128, D], BF16)
        ss = C.sb([128, 4], F32)
        xnb = C.sb([128, D], BF16)
        xnT = C.sb([128, 8, 128], BF16)
        qT = C.sb([128, 16, 128], BF16)
        sc = C.sb([128, 16, 128], F32)
        wrk = C.sb([128, 128], F32)
        tv = C.sb([128, 16, 16], F32)
        ti = C.sb([128, 16, 16], U32)
        tif = C.sb([128, 16, 16], F32)
        tif128 = C.sb([128, 16, 16], F32)
        cand_s = C.sb([128, 8, 256], F32)
        cand_i = C.sb([128, 8, 256], F32)
        wrk2 = C.sb([128, 256], F32)
        junk2 = C.sb([128, 256], F32)
        ts = C.sb([128, 8, 16], F32)
        negm = C.sb([128, 8], F32)
        gsum = C.sb([128, 8], F32)
        ge = C.sb([128, 8, 16], F32)
        gates = C.sb([128, 128], F32)
        idxf = C.sb([128, 128], F32)
        idx = C.sb([128, 128], I32)
        hraw = C.sb([128, 128], F32)
        hg = C.sb([128, 128], F32)
        wgt = C.sb([128, 128], F32)
        G = 4
        NR = 3
        UVt = C.ring(NR, [128, G, 2 * D], BF16)
        UVb = [[Buf(UVt[r_].t) for _ in range(G)] for r_ in range(NR)]
        dg = C.ring(4, [128, 128], BF16)
        h2 = C.sb([128, D], F32)
        h2b = C.sb([128, D], BF16)
        h2T = C.sb([128, 8, 128], BF16)
        pb = C.sb([128, 256], BF16)
        pTt = C.sb([128, 2, 128], BF16)
        ple = C.sb([128, D], F32)
        sg = h1
        yt = [ple]
        psG = C.ring(4, [128, 512], F32, psum=True)
        psAcc = C.ps([128, 2, 512], F32)
        psT = C.ps([128, 8, 128], BF16)
        gi = [0]

        def npg():
            gi[0] += 1
            return psG[gi[0] % 4]

        def load(b):
            sl = slice(b * 128, (b + 1) * 128)
            DMA(xt[b % 2][:], x_in[sl, :], [], [xt[b % 2]])
            for c in range(4):
                T.dma("pool", "indirect_dma_start", dict(out=mxt[b % 2][:, c, :], out_offset=None, in_=dnmix,
                                                         in_offset=bass.IndirectOffsetOnAxis(ap=mixidx[:, b * 4 + c:b * 4 + c + 1], axis=0)),
                      [mixidx], [mxb[b % 2][c]])
            for c in range(4, 8):
                DMA(mxt[b % 2][:, c, :], attmix[(c - 4) * 128:(c - 3) * 128, sl], [], [mxb[b % 2][c]])
            DMA(pt[b % 2][:], p_in[sl, :], [], [pt[b % 2]])

        def rms(src, gidx, out_ap, out_buf):
            ACT(junk[:], src[:], AF.Square, [src], [junk, ss], accum_out=ss[:, 0:1])
            ACT(ss[:, 0:1], ss[:, 0:1], AF.Sqrt, [ss], [ss], scale=1.0 / D, bias=EPS)
            DVE("reciprocal", [ss], [ss], out=ss[:, 0:1], in_=ss[:, 0:1])
            DVE("scalar_tensor_tensor", [src, ss, gains], [out_buf], out=out_ap, in0=src[:], scalar=ss[:, 0:1], in1=gains[:, gidx, :], op0=ALU.mult, op1=ALU.mult)

        def transpose8(src_b, dstT):
            T.group("pe", [("transpose", dict(out=psT[:, c, :], in_=src_b[:, c * 128:(c + 1) * 128], identity=identb[:])) for c in range(8)], [src_b, identb], [psT])
            evac(dstT[:], psT[:], [psT], [dstT])

        outs = []
        load(0)
        for b in range(nblk):
            if b + 1 < nblk:
                load(b + 1)
            x_b, m_b, p_b = xt[b % 2], mxt[b % 2], pt[b % 2]
            for half in range(2):
                hs = slice(half * 512, (half + 1) * 512)
                pg = npg()
                MMG([(pg[:], m_b[:, c, :], wout[:, c, hs]) for c in range(8)], [wout] + mxb[b % 2], [pg])
                DVE("tensor_tensor", [pg, x_b], [h1], out=h1[:, hs], in0=pg[:], in1=x_b[:, hs], op=ALU.add)
            rms(h1, 0, xnb[:], xnb)
            transpose8(xnb, xnT)
            for g4 in range(4):
                pg = npg()
                for u in range(4):
                    hh = g4 * 4 + u
                    MMG([(pg[:, u * 128:(u + 1) * 128], wq[:, c, hh * 128:(hh + 1) * 128], xnT[:, c, :]) for c in range(8)], [wq, xnT], [pg])
                evac(qT[:, g4 * 4:(g4 + 1) * 4, :], pg[:].rearrange("p (u t) -> p u t", u=4), [pg], [qT])
            for g4 in range(4):
                pg = npg()
                for u in range(4):
                    hh = g4 * 4 + u
                    MM(pg[:, u * 128:(u + 1) * 128], qT[:, hh, :], keysT[:, hh % 2, :], [qT, keysT], [pg])
                evac(sc[:, g4 * 4:(g4 + 1) * 4, :], pg[:].rearrange("p (u t) -> p u t", u=4), [pg], [sc])
            for hh in range(16):
                DVE("max", [sc], [tv], out=tv[:, hh, 0:8], in_=sc[:, hh, :])
                DVE("max_index", [sc, tv], [ti], out=ti[:, hh, 0:8], in_max=tv[:, hh, 0:8], in_values=sc[:, hh, :])
                DVE("match_replace", [sc, tv], [wrk], out=wrk[:], in_to_replace=tv[:, hh, 0:8], in_values=sc[:, hh, :], imm_value=-1e30)
                DVE("max", [wrk], [tv], out=tv[:, hh, 8:16], in_=wrk[:])
                DVE("max_index", [wrk, tv], [ti], out=ti[:, hh, 8:16], in_max=tv[:, hh, 8:16], in_values=wrk[:])
            DVE("tensor_copy", [ti], [tif], out=tif[:], in_=ti[:])
            tv4 = tv[:].rearrange("p (h two) k -> p h two k", two=2)
            tf4 = tif[:].rearrange("p (h two) k -> p h two k", two=2)
            DVE("tensor_scalar_mul", [tif], [tif128], out=tif128[:], in0=tif[:], scalar1=128.0)
            t128 = tif128[:].rearrange("p (h two) k -> p h two k", two=2)
            DVE("tensor_tensor", [tv], [cand_s], out=cand_s[:].rearrange("p h (a b) -> p h a b", a=16),
                in0=tv4[:, :, 0, :].unsqueeze(3).to_broadcast([128, 8, 16, 16]), in1=tv4[:, :, 1, :].unsqueeze(2).to_broadcast([128, 8, 16, 16]), op=ALU.add)
            DVE("tensor_tensor", [tif, tif128], [cand_i], out=cand_i[:].rearrange("p h (a b) -> p h a b", a=16),
                in0=t128[:, :, 0, :].unsqueeze(3).to_broadcast([128, 8, 16, 16]), in1=tf4[:, :, 1, :].unsqueeze(2).to_broadcast([128, 8, 16, 16]), op=ALU.add)
            for h in range(8):
                DVE("max", [cand_s], [ts], out=ts[:, h, 0:8], in_=cand_s[:, h, :])
                DVE("match_replace", [cand_s, ts], [wrk2], out=wrk2[:], in_to_replace=ts[:, h, 0:8], in_values=cand_s[:, h, :], imm_value=-1e30)
                DVE("max", [wrk2], [ts], out=ts[:, h, 8:16], in_=wrk2[:])
                for r_ in range(16):
                    DVE("scalar_tensor_tensor", [cand_s, ts, cand_i], [junk2, idxf], out=junk2[:], in0=cand_s[:, h, :], scalar=ts[:, h, r_:r_ + 1], in1=cand_i[:, h, :],
                        op0=ALU.is_equal, op1=ALU.mult, accum_out=idxf[:, h * 16 + r_:h * 16 + r_ + 1])
            DVE("tensor_scalar_min", [idxf], [idxf], out=idxf[:], in0=idxf[:], scalar1=16383.0)
            DVE("tensor_copy", [idxf], [idx], out=idx[:], in_=idxf[:])
            DVE("tensor_scalar_mul", [ts], [negm], out=negm[:], in0=ts[:, :, 0], scalar1=-1.0)
            for h in range(8):
                ACT(ge[:, h, :], ts[:, h, :], AF.Exp, [ts, negm], [ge, gsum], bias=negm[:, h:h + 1], accum_out=gsum[:, h:h + 1])
            DVE("reciprocal", [gsum], [gsum], out=gsum[:], in_=gsum[:])
            for h in range(8):
                DVE("tensor_scalar_mul", [ge, gsum], [gates], out=gates[:, h * 16:(h + 1) * 16], in0=ge[:, h, :], scalar1=gsum[:, h:h + 1])
            def issue_gathers(grp):
                r_ = grp % NR
                for s in range(G):
                    kk = grp * G + s
                    T.dma("pool", "indirect_dma_start", dict(out=UVt[r_][:, s, :], out_offset=None, in_=puv_in, in_offset=bass.IndirectOffsetOnAxis(ap=idx[:, kk:kk + 1], axis=0)), [idx], [UVb[r_][s]])
            for g_ in range(NR - 1):
                issue_gathers(g_)
            for grp in range(128 // G):
                r_ = grp % NR
                if grp + NR - 1 < 128 // G:
                    issue_gathers(grp + NR - 1)
                for s in range(G):
                    kk = grp * G + s
                    DVE("scalar_tensor_tensor", [UVb[r_][s], xnb], [junk, hraw], out=junk[:], in0=UVt[r_][:, s, 0:D], scalar=1.0, in1=xnb[:], op0=ALU.mult, op1=ALU.mult, accum_out=hraw[:, kk:kk + 1])
                gs = slice(grp * G, (grp + 1) * G)
                ACT(hg[:, gs], hraw[:, gs], AF.Gelu, [hraw], [hg])
                DVE("tensor_tensor", [hg, gates], [wgt], out=wgt[:, gs], in0=hg[:, gs], in1=gates[:, gs], op=ALU.mult)
                for s in range(G):
                    kk = grp * G + s
                    d_ = dg[kk % 4]
                    ACT(d_[:], identb[:], AF.Copy, [identb, wgt], [d_], scale=wgt[:, kk:kk + 1])
                    for half in range(2):
                        MM(psAcc[:, half, :], d_[:], UVt[r_][:, s, D + half * 512:D + (half + 1) * 512], [d_, UVb[r_][s]], [psAcc], start=(kk == 0), stop=(kk == 127))
            for half in range(2):
                hs = slice(half * 512, (half + 1) * 512)
                DVE("tensor_tensor", [psAcc, h1], [h2], out=h2[:, hs], in0=psAcc[:, half, :], in1=h1[:, hs], op=ALU.add)
            T.op("act", "copy", dict(out=pb[:], in_=p_b[:]), [p_b], [pb])
            T.group("pe", [("transpose", dict(out=psT[:, c, :], in_=pb[:, c * 128:(c + 1) * 128], identity=identb[:])) for c in range(2)], [pb, identb], [psT])
            evac(pTt[:], psT[:, 0:2, :], [psT], [pTt])
            pgs = [npg(), npg()]
            for half in range(2):
                hs = slice(half * 512, (half + 1) * 512)
                MMG([(pgs[half][:], pTt[:, c, :], wproj[:, c, hs]) for c in range(2)], [pTt, wproj], [pgs[half]])
                ACT(junk[:, hs], pgs[half][:], AF.Square, [pgs[half]], [junk, ss], accum_out=ss[:, 1 + half:2 + half])
            DVE("tensor_tensor", [ss], [ss], out=ss[:, 3:4], in0=ss[:, 1:2], in1=ss[:, 2:3], op=ALU.add)
            ACT(ss[:, 3:4], ss[:, 3:4], AF.Sqrt, [ss], [ss], scale=1.0 / D, bias=EPS)
            DVE("reciprocal", [ss], [ss], out=ss[:, 3:4], in_=ss[:, 3:4])
            for half in range(2):
                hs = slice(half * 512, (half + 1) * 512)
                DVE("scalar_tensor_tensor", [pgs[half], ss, gains], [ple], out=ple[:, hs], in0=pgs[half][:], scalar=ss[:, 3:4], in1=gains[:, 1, hs], op0=ALU.mult, op1=ALU.mult)
            T.op("act", "copy", dict(out=h2b[:], in_=h2[:]), [h2], [h2b])
            transpose8(h2b, h2T)
            for half in range(2):
                hs = slice(half * 512, (half + 1) * 512)
                pg = npg()
                MMG([(pg[:], h2T[:, c, :], wgate[:, c, hs]) for c in range(8)], [h2T, wgate], [pg])
                ACT(sg[:, hs], pg[:], AF.Sigmoid, [pg], [sg])
            DVE("tensor_tensor", [ple, sg], [ple], out=ple[:], in0=ple[:], in1=sg[:], op=ALU.mult)
            DVE("tensor_tensor", [ple, h2], [h2], out=h2[:], in0=ple[:], in1=h2[:], op=ALU.add)
            y_b = yt[0]
            rms(h2, 2, y_b[:], y_b)
            ob = C.db(("y", b))
            DMA(y_out[b * 128:(b + 1) * 128, :], y_b[:], [y_b], [ob])
        T.final_waits("sp", list(C.dbufs.values()))
        T.emit()


def _consts():
    ident = np.eye(128, dtype=np.float32)
    ones = np.ones((128, 128), np.float32)
    onesbd = np.zeros((128, 128), np.float32)
    onesbd[:64, :64] = 1
    onesbd[64:, 64:] = 1
    s = np.arange(128)[:, None]
    c = np.arange(128)[None, :]
    pmask = np.stack([np.where(c >= s, 0.0, -NEG), np.where(c <= s, 0.0, -NEG)]).astype(np.float32)
    strict = np.stack([(c > s), (c < s)]).astype(np.float32)
    P = np.zeros((64, 64), np.float32)
    for blk in range(2):
        for i in range(32):
            if i < 16:
                P[blk * 32 + i, blk * 32 + i + 16] = -1.0
            else:
                P[blk * 32 + i, blk * 32 + i - 16] = 1.0
    permT = np.zeros((128, 128), np.float32)
    permT[:64, :64] = P.T
    permT[64:, 64:] = P.T
    inv_freq = (np.float32(10000.0) ** (-np.arange(0, 32, 2, dtype=np.float32) / np.float32(32))).astype(np.float32)
    t = np.arange(TS)
    rowpos = (t // 64).astype(np.float32)
    colpos = (t % 64).astype(np.float32)
    ang = np.zeros((64, TS), np.float32)
    for dd in range(64):
        pos = rowpos if dd < 32 else colpos
        ang[dd] = pos * inv_freq[dd % 16]
    cosT = np.tile(np.cos(ang).astype(np.float32), (2, 1))
    sinT = np.tile(np.sin(ang).astype(np.float32), (2, 1))
    return dict(ident=ident, ones=ones, onesbd=onesbd, pmask=pmask, strict=strict, permT=permT, cosT=cosT, sinT=sinT)


def fused_inputs(inp, c, consts):
    w_in = inp["w_in"][0]
    conv_w = inp["conv_w"][0]
    ps_, hp, ss_, r = c // 2, c % 2, c // 4, c % 4
    f = lambda a: np.ascontiguousarray(a, dtype=np.float32)
    wdn = np.empty((4, D, 1024), np.float32)
    convw = np.empty((4, 128, 15), np.float32)
    gatep = np.empty((4, 128, 4), np.float32)
    for hd in range(4):
        cols = [w_in[:, hd * 128:(hd + 1) * 128], w_in[:, 512 + hd * 128:512 + (hd + 1) * 128],
                w_in[:, 1024 + hd * 128:1024 + (hd + 1) * 128], w_in[:, 1536 + hd * 128:1536 + (hd + 1) * 128]]
        for g in range(4):
            cols.append(np.repeat(w_in[:, 2048 + 4 * g + hd:2048 + 4 * g + hd + 1], 128, axis=1))
        wdn[hd] = np.concatenate(cols, axis=1)
        for m in range(3):
            convw[hd][:, m * 5:(m + 1) * 5] = conv_w[:, m * 512 + hd * 128:m * 512 + (hd + 1) * 128].T
        gatep[hd] = np.stack([np.full(128, inp["a_log_fwd"][0][hd]), np.full(128, inp["a_log_bwd"][0][hd]),
                              np.full(128, inp["dt_bias_fwd"][0][hd]), np.full(128, inp["dt_bias_bwd"][0][hd])], axis=1)
    qb = 2064
    x_own = np.concatenate([inp["x_prompt"][ps_, hp * 2048:(hp + 1) * 2048], inp["x_sample"][ss_, r * 4096:(r + 1) * 4096]], axis=0)
    p_own = np.concatenate([inp["p_prompt"][0, ps_, hp * 2048:(hp + 1) * 2048], inp["p_sample"][0, ss_, r * 4096:(r + 1) * 4096]], axis=0)
    cosO = np.concatenate([consts["cosT"][:, hp * 2048:(hp + 1) * 2048], consts["cosT"][:, r * 4096:(r + 1) * 4096]], axis=1)
    sinO = np.concatenate([consts["sinT"][:, hp * 2048:(hp + 1) * 2048], consts["sinT"][:, r * 4096:(r + 1) * 4096]], axis=1)
    gains = np.stack([np.tile(inp["ffn_norm"][0][None], (128, 1)), np.tile(inp["ple_norm"][0][None], (128, 1)), np.tile(inp["final_norm"][None], (128, 1))])
    keysT = np.stack([inp["peer_keys_a"][0].T, inp["peer_keys_b"][0].T], axis=1)
    pp = np.arange(128)[:, None]
    cc = np.arange(4)[None, :]
    mixidx = np.empty((128, 48 * 4), np.int32)
    for b in range(48):
        if b < 16:
            mixidx[:, b * 4:(b + 1) * 4] = (cc * 128 + pp) * (TP // 128) + hp * 16 + b
        else:
            mixidx[:, b * 4:(b + 1) * 4] = 512 * (TP // 128) + (cc * 128 + pp) * (TS // 128) + r * 32 + (b - 16)
    d = dict(xp=f(inp["x_prompt"][ps_]), xs=f(inp["x_sample"][ss_]), x=f(x_own), p=f(p_own), wdn=wdn,
             wkv=f(w_in[:, qb + 512:qb + 768]), wqa=f(w_in[:, qb:qb + 512]),
             anorm=f(inp["attn_norm"][0].reshape(8, 128).T), convw=convw, gatep=f(gatep),
             dnorm=f(inp["dn_out_norm"][0].reshape(128, 1)),
             qkg=f(np.stack([np.tile(inp["q_norm"][0], 2), np.tile(inp["k_norm"][0], 2)], axis=1)),
             ident=consts["ident"], ones=consts["ones"], onesbd=consts["onesbd"], pmask=consts["pmask"], strict=consts["strict"],
             permT=consts["permT"], cosT=consts["cosT"], sinT=consts["sinT"], cosO=f(cosO), sinO=f(sinO),
             w_out=f(inp["w_out"][0]), peer_query=f(inp["peer_query"][0]), keysT=f(keysT), peer_u=f(inp["peer_u"][0]),
             peer_v=f(inp["peer_v"][0]), ple_proj=f(inp["ple_proj"][0]), ple_gate=f(inp["ple_gate"][0]), gains=f(gains),
             ident2=consts["ident"], mixidx=mixidx)
    return d


def kernel(**inputs):
    inp = {k: np.asarray(v) for k, v in inputs.items()}
    consts = _consts()
    cores = list(range(NCORES))
    nc = build_fused()
    in_maps = [fused_inputs(inp, c, consts) for c in cores]
    res = run_bass_kernel_spmd(nc, in_maps, core_ids=cores).results
    y_p = np.empty((4, TP, D), np.float32)
    y_s = np.empty((2, TS, D), np.float32)
    for c in cores:
        ps_, hp, ss_, r = c // 2, c % 2, c // 4, c % 4
        y = res[c]["y"]
        y_p[ps_, hp * 2048:(hp + 1) * 2048] = y[:2048]
        y_s[ss_, r * 4096:(r + 1) * 4096] = y[2048:]
    return (y_p, y_s)
```

```python
import numpy as np
from contextlib import ExitStack
import ml_dtypes
import concourse.bass as bass
import concourse.mybir as mybir
from concourse.bass_utils import run_bass_kernel_spmd

F32 = mybir.dt.float32
BF16 = mybir.dt.bfloat16
U32 = mybir.dt.uint32
I32 = mybir.dt.int32
AF = mybir.ActivationFunctionType
ALU = mybir.AluOpType

D = 1024
TP = 4096
TS = 16384
NCORES = 8
EPS = 1e-6
NEG = -30000.0
NTOK2 = 6144


class Buf:
    __slots__ = ("name", "last_w", "readers", "t")

    def __init__(self, t=None, name=""):
        self.name = name
        self.t = t
        self.last_w = None
        self.readers = []

    def __getitem__(self, k):
        return self.t[k]


class Trk:
    ENGS = ("pe", "act", "dve", "pool", "sp")
    NS = 8

    def __init__(self, nc, stack, tag, ns=None):
        self.nc = nc
        ns = ns or {}
        self.items = {e: [] for e in self.ENGS}
        self.sem = {}
        self.count = {e: 0 for e in self.ENGS}
        self.seen = {e: {} for e in self.ENGS}
        self.dma_n = {e: 0 for e in self.ENGS}
        self.dma_sems = {}
        self.semobj = {}
        self.ninst = 0
        for e in self.ENGS:
            s = stack.enter_context(nc.semaphore(f"{tag}_c_{e}"))
            self.sem[e] = s
            self.semobj[id(s)] = s
        for e in ("sp", "pool"):
            self.dma_sems[e] = []
            for i in range(ns.get(e, self.NS)):
                s = stack.enter_context(nc.semaphore(f"{tag}_d_{e}{i}"))
                self.dma_sems[e].append(s)
                self.semobj[id(s)] = s

    def _deps(self, eng, reads, writes, extra=(), skip_self=False):
        need = {}
        own = id(self.sem[eng])

        def add(tok):
            if tok is None:
                return
            k, v = tok
            if skip_self and k == own:
                return
            if need.get(k, 0) < v:
                need[k] = v
        for r in reads:
            add(r.last_w)
        for w in writes:
            add(w.last_w)
            for t in w.readers:
                add(t)
        for t in extra:
            add(t)
        seen = self.seen[eng]
        waits = []
        for k, v in need.items():
            if seen.get(k, 0) < v:
                seen[k] = v
                waits.append((self.semobj[k], v))
        return waits

    def _finish(self, tok, reads, writes):
        for w in writes:
            w.last_w = tok
            w.readers = []
        for r in reads:
            if r not in writes:
                if len(r.readers) > 64:
                    r.readers = r.readers[-32:]
                r.readers.append(tok)

    def op(self, eng, name, kw, reads=(), writes=()):
        return self.group(eng, [(name, kw)], reads, writes)

    def group(self, eng, fns, reads=(), writes=()):
        waits = self._deps(eng, reads, writes, skip_self=(eng == "pe"))
        self.count[eng] += 1
        tok = (id(self.sem[eng]), self.count[eng])
        self.items[eng].append((waits, list(fns), (self.sem[eng], 1)))
        self._finish(tok, reads, writes)
        self.ninst += len(fns)
        return tok

    def dma(self, eng, name, kw, reads=(), writes=()):
        fn = (name, kw)
        n = self.dma_n[eng]
        self.dma_n[eng] += 1
        nsl = len(self.dma_sems[eng])
        slot = n % nsl
        val = 16 * (n // nsl + 1)
        s = self.dma_sems[eng][slot]
        extra = [(id(s), val - 16)] if val > 16 else []
        waits = self._deps(eng, reads, writes, extra)
        tok = (id(s), val)
        self.items[eng].append((waits, [fn], (s, 16)))
        self._finish(tok, reads, writes)
        self.ninst += 1
        return tok

    def final_waits(self, eng, bufs):
        waits = self._deps(eng, bufs, [])
        self.items[eng].append((waits, [], None))

    def emit(self):
        nc = self.nc
        items = self.items

        def run(e, lst):
            for waits, fns, inc in lst:
                for s, v in waits:
                    e.wait_ge(s, v)
                ins = None
                for name, kw in fns:
                    ins = getattr(e, name)(**kw)
                if inc is not None and ins is not None:
                    ins.then_inc(inc[0], inc[1])

        with nc.Block() as block:
            @block.tensor
            def _(e):
                run(e, items["pe"])

            @block.scalar
            def _(e):
                run(e, items["act"])

            @block.vector
            def _(e):
                run(e, items["dve"])

            @block.gpsimd
            def _(e):
                run(e, items["pool"])

            @block.sync
            def _(e):
                run(e, items["sp"])
        self.items = {e: [] for e in self.ENGS}


class Ctx:
    def __init__(self, nc, st):
        self.nc = nc
        self.st = st
        self.ntrk = 0
        self.T = Trk(nc, st, "k0")
        self.n = 0
        self.dbufs = {}
        self.subs = []
        self.sub = None
        self.all = []
        self.ninst = 0

    def begin(self):
        self.subs.append(ExitStack())
        self.sub = self.subs[-1]

    def release_all(self):
        if any(self.T.items[e] for e in self.T.ENGS):
            self.T.final_waits("sp", list(self.dbufs.values()))
            self.T.final_waits("pool", list(self.dbufs.values()))
            self.T.emit()
        self.subs.pop().close()
        self.sub = self.subs[-1] if self.subs else None

    def switch_trk(self, ns=None):
        assert not any(self.T.items[e] for e in self.T.ENGS)
        self.ninst += self.T.ninst
        self.ntrk += 1
        self.T = Trk(self.nc, self.st, f"k{self.ntrk}", ns=ns)
        for b in self.all:
            b.last_w = None
            b.readers = []
        return self.T

    def _reg(self, b):
        self.all.append(b)
        return b

    def sb(self, shape, dt, name=None):
        self.n += 1
        st = self.sub or self.st
        t = st.enter_context(self.nc.sbuf_tensor(name or f"sb{self.n}", shape, dt))
        return self._reg(Buf(t))

    def ps(self, shape, dt, name=None):
        self.n += 1
        st = self.sub or self.st
        t = st.enter_context(self.nc.psum_tensor(name or f"ps{self.n}", shape, dt))
        return self._reg(Buf(t))

    def ring(self, n, shape, dt, psum=False):
        return [(self.ps if psum else self.sb)(shape, dt) for _ in range(n)]

    def db(self, key):
        b = self.dbufs.get(key)
        if b is None:
            b = self._reg(Buf(None, str(key)))
            self.dbufs[key] = b
        return b


SEQS = (("p", TP), ("s", TS))
OWN = {"p": range(0, 4), "s": range(4, 12)}
NROWS_MIX = 512 * (TP // 128) + 512 * (TS // 128)


def build_fused(small=False):
    nc = bass.Bass("TRN2", target_bir_lowering=False)

    def din(name, shape, dt=F32):
        return nc.dram_tensor(name, list(shape), dt, kind="ExternalInput").ap()

    def dscr(name, shape, dt=F32):
        return nc.dram_tensor(name, list(shape), dt).ap()

    TN = dict(SEQS)
    x_in = {"p": din("xp", [TP, D]), "s": din("xs", [TS, D])}
    xo_in = din("x", [NTOK2, D])
    wdn_in = din("wdn", [4, D, 1024])
    wkv_in = din("wkv", [D, 256])
    wqa_in = din("wqa", [D, 512])
    anorm_in = din("anorm", [128, 8])
    convw_in = din("convw", [4, 128, 15])
    gatep_in = din("gatep", [4, 128, 4])
    dnorm_in = din("dnorm", [128, 1])
    qkg_in = din("qkg", [128, 2])
    ident_in = din("ident", [128, 128])
    ones_in = din("ones", [128, 128])
    onesbd_in = din("onesbd", [128, 128])
    pmask_in = din("pmask", [2, 128, 128])
    strict_in = din("strict", [2, 128, 128])
    permT_in = din("permT", [128, 128])
    cs_in = (din("cosT", [128, TS]), din("sinT", [128, TS]))
    cso_in = (din("cosO", [128, NTOK2]), din("sinO", [128, NTOK2]))
    JOBS = [(sn, h) for sn, _ in SEQS for h in range(4)]
    dn_pre = {jb: dscr(f"dnpre_{jb[0]}{jb[1]}", [8, 128, TN[jb[0]]]) for jb in JOBS}
    att_k = {sn: dscr(f"attk_{sn}", [128, Tn]) for sn, Tn in SEQS}
    att_v = {sn: dscr(f"attv_{sn}", [Tn, 128], BF16) for sn, Tn in SEQS}
    att_kr = {sn: dscr(f"attkr_{sn}", [128, Tn], BF16) for sn, Tn in SEQS}
    att_q = dscr("attq", [512, NTOK2])
    att_qr = dscr("attqr", [512, NTOK2], BF16)
    dnop = {(jb, d): dscr(f"dnop_{jb[0]}{jb[1]}_{d}", [TN[jb[0]] // 128, 5, 128, 128]) for jb in JOBS for d in range(2)}
    dnd = {(jb, d): dscr(f"dnd_{jb[0]}{jb[1]}_{d}", [128, TN[jb[0]] // 128]) for jb in JOBS for d in range(2)}
    dno = {(jb, d): dscr(f"dno_{jb[0]}{jb[1]}_{d}", [128, TN[jb[0]]]) for jb in JOBS for d in range(2)}
    dnmix = dscr("dnmix", [NROWS_MIX, 128], BF16)
    dnmix_v = {"p": dnmix[0:512 * (TP // 128), :].rearrange("(ch b) t -> ch (b t)", b=TP // 128),
               "s": dnmix[512 * (TP // 128):NROWS_MIX, :].rearrange("(ch b) t -> ch (b t)", b=TS // 128)}
    attmix = dscr("attmix", [512, NTOK2], BF16)

    with ExitStack() as st:
        C = Ctx(nc, st)
        TB = [C.T]

        def DMA(out, in_, r, w, eng="sp"):
            return TB[0].dma(eng, "dma_start", dict(out=out, in_=in_), r, w)

        def ACT(out, in_, func, r, w, **kw):
            return TB[0].op("act", "activation", dict(out=out, in_=in_, func=func, **kw), r, w)

        def DVE(name, r, w, **kw):
            return TB[0].op("dve", name, kw, r, w)

        def POOL(name, r, w, **kw):
            return TB[0].op("pool", name, kw, r, w)

        def MM(out, lhsT, rhs, r, w, start=True, stop=True):
            return TB[0].op("pe", "matmul", dict(out=out, lhsT=lhsT, rhs=rhs, start=start, stop=stop), r, w)

        def MMG(lst, r, w):
            return TB[0].group("pe", [("matmul", dict(out=o, lhsT=l, rhs=rh, start=(i == 0), stop=(i == len(lst) - 1))) for i, (o, l, rh) in enumerate(lst)], r, w)

        def TR(out, in_, idn, r, w):
            return TB[0].op("pe", "transpose", dict(out=out, in_=in_, identity=idn), r, w)

        ev = [0]

        def evac(out_ap, in_ap, reads, writes):
            ev[0] += 1
            if ev[0] % 2:
                TB[0].op("act", "copy", dict(out=out_ap, in_=in_ap), reads, writes)
            else:
                TB[0].op("dve", "tensor_copy", dict(out=out_ap, in_=in_ap), reads, writes)

        def rotate(ns=None):
            TB[0] = C.switch_trk(ns)

        C.begin()
        ident = C.sb([128, 128], F32)
        identb = C.sb([128, 128], BF16)
        ones = C.sb([128, 128], F32)
        onesbd = C.sb([128, 128], F32)
        pmask = C.sb([128, 2, 128], F32)
        strict = C.sb([128, 2, 128], F32)
        permT = C.sb([128, 128], F32)
        anorm = C.sb([128, 8], F32)
        convw = C.sb([128, 4, 15], F32)
        gatep = C.sb([128, 4, 4], F32)
        negA = C.sb([128, 4, 2], F32)
        dnorm = C.sb([128, 1], F32)
        qkg = C.sb([128, 2], F32)
        msk = C.sb([128, 512], F32)

        def load_consts():
            DMA(ident[:], ident_in, [], [ident])
            DMA(ones[:], ones_in, [], [ones])
            DMA(onesbd[:], onesbd_in, [], [onesbd])
            DMA(pmask[:], pmask_in.rearrange("d p c -> p d c"), [], [pmask])
            DMA(strict[:], strict_in.rearrange("d p c -> p d c"), [], [strict])
            DMA(permT[:], permT_in, [], [permT])
            DMA(anorm[:], anorm_in, [], [anorm])
            DMA(convw[:], convw_in.rearrange("j p k -> p j k"), [], [convw])
            DMA(gatep[:], gatep_in.rearrange("j p k -> p j k"), [], [gatep])
            DMA(dnorm[:], dnorm_in, [], [dnorm])
            DMA(qkg[:], qkg_in, [], [qkg])
            DVE("tensor_copy", [ident], [identb], out=identb[:], in_=ident[:])
            ACT(negA[:], gatep[:, :, 0:2], AF.Exp, [gatep], [negA])
            DVE("tensor_scalar_mul", [negA], [negA], out=negA[:], in0=negA[:], scalar1=-1.0)
            POOL("memset", [], [msk], ap=msk[:], constant=1.0)
            POOL("memset", [], [msk], ap=msk[:].rearrange("p (n c) -> p n c", c=128)[:, :, 0:1], constant=0.0)
        load_consts()

        C.begin()
        wdn = C.sb([128, 4, 8, 1024], BF16)
        wkv = C.sb([128, 8, 256], BF16)
        wqa = C.sb([128, 8, 512], BF16)
        wst = C.ring(2, [128, 1024], F32)
        k = 0
        for h in range(4):
            for c in range(8):
                w = wst[k % 2]
                k += 1
                DMA(w[:], wdn_in[h, c * 128:(c + 1) * 128, :], [], [w])
                DVE("tensor_scalar_mul", [w, anorm], [wdn], out=wdn[:, h, c, :], in0=w[:], scalar1=anorm[:, c:c + 1])
        for src, dst, ncol in ((wkv_in, wkv, 256), (wqa_in, wqa, 512)):
            for c in range(8):
                w = wst[k % 2]
                k += 1
                DMA(w[:, 0:ncol], src[c * 128:(c + 1) * 128, :], [], [w])
                DVE("tensor_scalar_mul", [w, anorm], [dst], out=dst[:, c, :], in0=w[:, 0:ncol], scalar1=anorm[:, c:c + 1])
        xt = C.ring(2, [128, 4, 1024], F32)
        junk = C.sb([128, 1024], BF16)
        ss = C.ring(2, [128, 4], F32)
        xn = C.ring(2, [128, 4, 1024], BF16)
        xnT = C.ring(2, [128, 8, 512], BF16)
        pT = C.ring(2, [128, 512], BF16, psum=True)
        psA = C.ring(4, [128, 512], F32, psum=True)
        psV = C.ps([128, 4, 128], F32)
        stg = C.ring(4, [128, 512], F32)
        vstg = C.ring(2, [128, 4, 128], BF16)
        work = [("full", sn, i) for sn, Tn in SEQS for i in range(2 if small else Tn // 512)]
        work += [("own", None, ot) for ot in (range(0, 12, 6) if small else range(12))]

        def load_x(wi):
            kind, sn, i = work[wi]
            b = xt[wi % 2]
            src = x_in[sn] if kind == "full" else xo_in
            DMA(b[:], src[i * 512:(i + 1) * 512, :].rearrange("(b p) d -> p b d", p=128), [], [b])

        load_x(0)
        kk = 0
        for wi, (kind, sn, i) in enumerate(work):
            if wi + 1 < len(work):
                load_x(wi + 1)
            x_b, ss_b, xn_b, xnT_b = xt[wi % 2], ss[wi % 2], xn[wi % 2], xnT[wi % 2]
            cols = slice(i * 512, (i + 1) * 512)
            for b in range(4):
                ACT(junk[:], x_b[:, b, :], AF.Square, [x_b], [junk, ss_b], accum_out=ss_b[:, b:b + 1])
            ACT(ss_b[:], ss_b[:], AF.Sqrt, [ss_b], [ss_b], scale=1.0 / D, bias=EPS)
            DVE("reciprocal", [ss_b], [ss_b], out=ss_b[:], in_=ss_b[:])
            for b in range(4):
                DVE("tensor_scalar_mul", [x_b, ss_b], [xn_b], out=xn_b[:, b, :], in0=x_b[:, b, :], scalar1=ss_b[:, b:b + 1])
            for c in range(8):
                p = pT[c % 2]
                TB[0].group("pe", [("transpose", dict(out=p[:, b * 128:(b + 1) * 128], in_=xn_b[:, b, c * 128:(c + 1) * 128], identity=identb[:])) for b in range(4)], [xn_b, identb], [p])
                evac(xnT_b[:, c, :], p[:], [p], [xnT_b])
            if kind == "full":
                for h in range(4):
                    for cc in range(8):
                        p, s_ = psA[kk % 4], stg[kk % 4]
                        kk += 1
                        MMG([(p[:], wdn[:, h, c, cc * 128:(cc + 1) * 128], xnT_b[:, c, :]) for c in range(8)], [wdn, xnT_b], [p])
                        evac(s_[:], p[:], [p], [s_])
                        DMA(dn_pre[(sn, h)][cc, :, cols], s_[:], [s_], [C.db(("dnpre", sn, h, cc, i))])
                p, s_ = psA[kk % 4], stg[kk % 4]
                kk += 1
                MMG([(p[:], wkv[:, c, 0:128], xnT_b[:, c, :]) for c in range(8)], [wkv, xnT_b], [p])
                evac(s_[:], p[:], [p], [s_])
                DMA(att_k[sn][:, cols], s_[:], [s_], [C.db(("attk", sn, i))])
                for b in range(4):
                    MMG([(psV[:, b, :], xnT_b[:, c, b * 128:(b + 1) * 128], wkv[:, c, 128:256]) for c in range(8)], [wkv, xnT_b], [psV])
                vs = vstg[wi % 2]
                evac(vs[:], psV[:], [psV], [vs])
                DMA(att_v[sn][cols, :].rearrange("(b p) d -> p b d", p=128), vs[:], [vs], [C.db(("attv", sn, i))])
            else:
                for qc in range(4):
                    p, s_ = psA[kk % 4], stg[kk % 4]
                    kk += 1
                    MMG([(p[:], wqa[:, c, qc * 128:(qc + 1) * 128], xnT_b[:, c, :]) for c in range(8)], [wqa, xnT_b], [p])
                    evac(s_[:], p[:], [p], [s_])
                    DMA(att_q[qc * 128:(qc + 1) * 128, cols], s_[:], [s_], [C.db(("attq", qc, i))])
        C.release_all()

        def p1b(jobs):
            C.begin()
            pre = C.ring(2, [128, 3, 516], F32)
            gts = C.ring(2, [128, 4, 512], F32)
            cv = C.sb([128, 3, 512], F32)
            sq = C.sb([128, 2, 512], F32)
            rn = C.sb([128, 2, 512], F32)
            qn = C.sb([128, 512], F32)
            kn = C.sb([128, 512], F32)
            sig = C.sb([128, 2, 512], F32)
            gl = C.sb([128, 2, 512], F32)
            gc = C.sb([128, 2, 512], F32)
            tmpb = C.sb([128, 512], F32)
            egc = C.sb([128, 512], F32)
            kb = C.ring(2, [128, 512], F32)
            kbg = C.ring(2, [128, 512], F32)
            qt = C.ring(2, [128, 512], F32)
            vb = C.ring(2, [128, 512], F32)
            ek = C.sb([128, 512], F32)
            kt = C.ring(2, [128, 512], F32)
            dec = C.ring(2, [128, 4], F32)
            psN = C.ring(2, [128, 512], F32, psum=True)
            psX = C.ring(6, [128, 128], F32, psum=True)
            NUQ = 4
            Vb = C.ring(NUQ, [128, 128], F32)
            Kbg = C.ring(NUQ, [128, 128], F32)
            opst = [C.ring(2, [128, 5, 128], F32) for _ in range(NUQ)]
            dif = C.ring(NUQ, [128, 128], F32)
            dcy = C.ring(NUQ, [128, 128], F32)
            dcyS = C.ring(NUQ, [128, 128], F32)
            Pm = [C.ring(2, [128, 128], F32) for _ in range(NUQ)]
            PTm = [C.ring(2, [128, 128], F32) for _ in range(NUQ)]
            TT = [C.ring(2, [128, 128], F32) for _ in range(NUQ)]
            px = [0]

            def nps():
                px[0] += 1
                return psX[px[0] % 6]

            un = 0
            for jb in jobs:
                sn, h = jb
                Tn = TN[sn]
                ntl = 2 if small else Tn // 512
                for i in range(ntl):
                    pr, gt = pre[i % 2], gts[i % 2]
                    lo, hi = i * 512 - 2, i * 512 + 514
                    clo, chi = max(lo, 0), min(hi, Tn)
                    if lo < 0:
                        POOL("memset", [], [pr], ap=pr[:, :, 0:2], constant=0.0)
                    if hi > Tn:
                        POOL("memset", [], [pr], ap=pr[:, :, 514:516], constant=0.0)
                    DMA(pr[:, :, clo - lo:chi - lo], dn_pre[jb][0:3, :, clo:chi].rearrange("m p t -> p m t"),
                        [C.db(("dnpre", sn, h, m_, t_)) for m_ in range(3) for t_ in (i - 1, i, i + 1) if 0 <= t_ < Tn // 512], [pr])
                    DMA(gt[:], dn_pre[jb][4:8, :, i * 512:(i + 1) * 512].rearrange("m p t -> p m t"), [C.db(("dnpre", sn, h, m_, i)) for m_ in range(4, 8)], [gt])
                    for m in range(3):
                        DVE("tensor_scalar_mul", [pr, convw], [cv], out=cv[:, m, :], in0=pr[:, m, 0:512], scalar1=convw[:, h, m * 5:m * 5 + 1])
                        for tap in range(1, 5):
                            DVE("scalar_tensor_tensor", [pr, convw, cv], [cv], out=cv[:, m, :], in0=pr[:, m, tap:tap + 512],
                                scalar=convw[:, h, m * 5 + tap:m * 5 + tap + 1], in1=cv[:, m, :], op0=ALU.mult, op1=ALU.add)
                    ACT(cv[:], cv[:], AF.Silu, [cv], [cv])
                    ACT(sq[:], cv[:, 0:2, :], AF.Square, [cv], [sq])
                    for m in range(2):
                        MM(psN[m][:], ones[:], sq[:, m, :], [ones, sq], [psN[m]])
                        ACT(rn[:, m, :], psN[m][:], AF.Sqrt, [psN[m]], [rn], bias=EPS)
                    DVE("reciprocal", [rn], [rn], out=rn[:], in_=rn[:])
                    DVE("scalar_tensor_tensor", [cv, rn], [qn], out=qn[:], in0=cv[:, 0, :], scalar=float(128 ** -0.5), in1=rn[:, 0, :], op0=ALU.mult, op1=ALU.mult)
                    DVE("tensor_tensor", [cv, rn], [kn], out=kn[:], in0=cv[:, 1, :], in1=rn[:, 1, :], op=ALU.mult)
                    ACT(sig[:], gt[:, 0:2, :], AF.Sigmoid, [gt], [sig])
                    for d in range(2):
                        ACT(gl[:, d, :], gt[:, 2 + d, :], AF.Exp, [gt, gatep], [gl], bias=gatep[:, h, 2 + d:3 + d])
                        ACT(gl[:, d, :], gl[:, d, :], AF.Ln, [gl], [gl], bias=1.0)
                        DVE("tensor_scalar_mul", [gl, negA], [gl], out=gl[:, d, :], in0=gl[:, d, :], scalar1=negA[:, h, d:d + 1])
                    DVE("tensor_tensor_scan", [msk, gl], [gc], out=gc[:, 0, :], data0=msk[:], data1=gl[:, 0, :], initial=0.0, op0=ALU.mult, op1=ALU.add)
                    DVE("tensor_tensor_scan", [msk, gl], [gc], out=gc[:, 1, :], data0=msk[:], data1=gl[:, 1, :], initial=0.0, op0=ALU.mult, op1=ALU.add)
                    DVE("tensor_tensor", [gl, gc], [tmpb], out=tmpb[:], in0=gl[:, 1, :], in1=gc[:, 1, :], op=ALU.subtract)
                    for u in range(4):
                        DVE("tensor_scalar_add", [tmpb, gc], [gc], out=gc[:, 1, u * 128:(u + 1) * 128], in0=tmpb[:, u * 128:(u + 1) * 128], scalar1=gc[:, 1, u * 128 + 127:u * 128 + 128])
                    for d in range(2):
                        qt_b, dec_b = qt[d], dec[d]
                        lastc = 127 if d == 0 else 0
                        ACT(egc[:], gc[:, d, :], AF.Exp, [gc], [egc])
                        POOL("tensor_tensor", [kn, sig], [kb[d]], out=kb[d][:], in0=kn[:], in1=sig[:, d, :], op=ALU.mult)
                        POOL("tensor_tensor", [kb[d], egc], [kbg[d]], out=kbg[d][:], in0=kb[d][:], in1=egc[:], op=ALU.mult)
                        POOL("tensor_tensor", [qn, egc], [qt_b], out=qt_b[:], in0=qn[:], in1=egc[:], op=ALU.mult)
                        POOL("tensor_tensor", [cv, sig], [vb[d]], out=vb[d][:], in0=cv[:, 2, :], in1=sig[:, d, :], op=ALU.mult)
                        for u in range(4):
                            ACT(ek[:, u * 128:(u + 1) * 128], gc[:, d, u * 128:(u + 1) * 128], AF.Exp, [gc], [ek], scale=-1.0, bias=gc[:, d, u * 128 + lastc:u * 128 + lastc + 1])
                        POOL("tensor_tensor", [kn, ek], [kt[d]], out=kt[d][:], in0=kn[:], in1=ek[:], op=ALU.mult)
                        ACT(dec_b[:], gc[:, d, :].rearrange("p (u c) -> p u c", c=128)[:, :, lastc], AF.Exp, [gc], [dec_b])
                        DMA(dnd[(jb, d)][:, i * 4:(i + 1) * 4], dec_b[:], [dec_b], [C.db(("dnd", jb, d))])
                    for u0 in (0, 2):
                        un += 1
                        UN = [(0, u0), (1, u0), (0, u0 + 1), (1, u0 + 1)]
                        SLS = [slice(u * 128, (u + 1) * 128) for (_, u) in UN]
                        osd = [opst[q][un % 2] for q in range(NUQ)]
                        R = [dict() for _ in UN]

                        def run_stage(nalloc, produce, consume):
                            gsz = 2 if nalloc >= 2 else 4
                            for g0 in range(0, len(UN), gsz):
                                qs = range(g0, min(g0 + gsz, len(UN)))
                                for q in qs:
                                    produce(q, UN[q][0], SLS[q])
                                for q in qs:
                                    consume(q, UN[q][0], SLS[q])

                        def tr_p(q, d, sl):
                            R[q]["p1"], R[q]["p2"], R[q]["p3"] = nps(), nps(), nps()
                            TR(R[q]["p1"][:], vb[d][:, sl], ident[:], [vb[d], ident], [R[q]["p1"]])
                            TR(R[q]["p2"][:], kbg[d][:, sl], ident[:], [kbg[d], ident], [R[q]["p2"]])
                            TR(R[q]["p3"][:], kt[d][:, sl], ident[:], [kt[d], ident], [R[q]["p3"]])

                        def tr_c(q, d, sl):
                            evac(Vb[q][:], R[q]["p1"][:], [R[q]["p1"]], [Vb[q]])
                            evac(Kbg[q][:], R[q]["p2"][:], [R[q]["p2"]], [Kbg[q]])
                            evac(osd[q][:, 4, :], R[q]["p3"][:], [R[q]["p3"]], [osd[q]])
                        run_stage(3, tr_p, tr_c)

                        def dec_p(q, d, sl):
                            R[q]["p4"] = nps()
                            MMG([(R[q]["p4"][:], gc[:, d, sl], ident[:]), (R[q]["p4"][:], ident[:], pmask[:, d, :])], [gc, ident, pmask], [R[q]["p4"]])

                        def dec_c(q, d, sl):
                            DVE("tensor_tensor", [gc, R[q]["p4"]], [dif[q]], out=dif[q][:], in0=gc[:, d, sl], in1=R[q]["p4"][:], op=ALU.subtract)
                        run_stage(1, dec_p, dec_c)
                        for q, (d, u) in enumerate(UN):
                            ACT(dcy[q][:], dif[q][:], AF.Exp, [dif[q]], [dcy[q]])
                        for q, (d, u) in enumerate(UN):
                            POOL("tensor_tensor", [dcy[q], strict], [dcyS[q]], out=dcyS[q][:], in0=dcy[q][:], in1=strict[:, d, :], op=ALU.mult)

                        def mm_p(q, d, sl):
                            R[q]["p5"], R[q]["p6"] = nps(), nps()
                            MM(R[q]["p5"][:], kn[:, sl], kb[d][:, sl], [kn, kb[d]], [R[q]["p5"]])
                            MM(R[q]["p6"][:], kn[:, sl], qn[:, sl], [kn, qn], [R[q]["p6"]])

                        def mm_c(q, d, sl):
                            DVE("tensor_tensor", [R[q]["p5"], dcyS[q]], [PTm[q][0]], out=PTm[q][0][:], in0=R[q]["p5"][:], in1=dcyS[q][:], op=ALU.mult)
                            DVE("tensor_tensor", [R[q]["p6"], dcy[q]], [osd[q]], out=osd[q][:, 3, :], in0=R[q]["p6"][:], in1=dcy[q][:], op=ALU.mult)
                        run_stage(2, mm_p, mm_c)

                        def lt_p(q, d, sl):
                            R[q]["p7"] = nps()
                            TR(R[q]["p7"][:], PTm[q][0][:], ident[:], [PTm[q][0], ident], [R[q]["p7"]])

                        def lt_c(q, d, sl):
                            evac(Pm[q][0][:], R[q]["p7"][:], [R[q]["p7"]], [Pm[q][0]])
                            DVE("tensor_tensor", [ident, PTm[q][0]], [TT[q][0]], out=TT[q][0][:], in0=ident[:], in1=PTm[q][0][:], op=ALU.subtract)
                        run_stage(1, lt_p, lt_c)
                        cur = 0
                        for lv in range(1, 7):
                            nx = 1 - cur

                            def sq_p(q, d, sl, cur=cur, nx=nx, lv=lv):
                                R[q]["pa"] = nps()
                                MM(R[q]["pa"][:], PTm[q][cur][:], Pm[q][cur][:], [PTm[q][cur], Pm[q][cur]], [R[q]["pa"]])
                                if lv < 6:
                                    R[q]["pb"] = nps()
                                    MM(R[q]["pb"][:], Pm[q][cur][:], PTm[q][cur][:], [PTm[q][cur], Pm[q][cur]], [R[q]["pb"]])

                            def sq_c(q, d, sl, cur=cur, nx=nx, lv=lv):
                                evac(Pm[q][nx][:], R[q]["pa"][:], [R[q]["pa"]], [Pm[q][nx]])
                                if lv < 6:
                                    evac(PTm[q][nx][:], R[q]["pb"][:], [R[q]["pb"]], [PTm[q][nx]])
                            run_stage(2, sq_p, sq_c)

                            def tt_p(q, d, sl, cur=cur, nx=nx):
                                R[q]["pc"] = nps()
                                MM(R[q]["pc"][:], Pm[q][nx][:], TT[q][cur][:], [Pm[q][nx], TT[q][cur]], [R[q]["pc"]])

                            def tt_c(q, d, sl, cur=cur, nx=nx):
                                DVE("tensor_tensor", [R[q]["pc"], TT[q][cur]], [TT[q][nx]], out=TT[q][nx][:], in0=R[q]["pc"][:], in1=TT[q][cur][:], op=ALU.add)
                            run_stage(1, tt_p, tt_c)
                            cur = nx

                        def uw_p(q, d, sl, cur=cur):
                            R[q]["p8"], R[q]["p9"] = nps(), nps()
                            MM(R[q]["p8"][:], TT[q][cur][:], Vb[q][:], [TT[q][cur], Vb[q]], [R[q]["p8"]])
                            MM(R[q]["p9"][:], Kbg[q][:], TT[q][cur][:], [TT[q][cur], Kbg[q]], [R[q]["p9"]])

                        def uw_c(q, d, sl):
                            evac(osd[q][:, 0, :], R[q]["p8"][:], [R[q]["p8"]], [osd[q]])
                            evac(osd[q][:, 1, :], R[q]["p9"][:], [R[q]["p9"]], [osd[q]])
                            POOL("tensor_copy", [qt[d]], [osd[q]], out=osd[q][:, 2, :], in_=qt[d][:, sl])
                        run_stage(2, uw_p, uw_c)
                        for q, (d, u) in enumerate(UN):
                            n = i * 4 + u
                            DMA(dnop[(jb, d)][n].rearrange("k p c -> p k c"), osd[q][:], [osd[q]], [C.db(("dnop", jb, d, n))])
            C.release_all()

        p1b([("p", h) for h in range(4)] + [("s", 0), ("s", 1)])
        rotate()
        p1b([("s", 2), ("s", 3)])

        C.begin()
        chains = [(jb, d) for jb in ([("s", h) for h in range(4)] + [("p", h) for h in range(4)]) for d in range(2)]
        S = {ch: C.ring(2, [128, 128], F32) for ch in chains}
        decs = {ch: C.sb([128, TN[ch[0][0]] // 128], F32) for ch in chains}
        ops = {ch: C.ring(2, [128, 5, 128], F32) for ch in chains}
        vn = {ch: C.ring(2, [128, 128], F32) for ch in chains}
        ostg = C.ring(4, [128, 128], F32)
        psC = C.ring(8, [128, 128], F32, psum=True)
        pc_ = [0]

        def npc():
            pc_[0] += 1
            return psC[pc_[0] % 8]
        for ch in chains:
            DMA(decs[ch][:], dnd[ch], [C.db(("dnd", ch[0], ch[1]))], [decs[ch]])
            POOL("memset", [], [S[ch][0]], ap=S[ch][0][:], constant=0.0)
        nsteps = {ch: (8 if small else TN[ch[0][0]] // 128) for ch in chains}
        oc = 0
        GS = 4
        for step in range(max(nsteps.values())):
            act_ch = [ch for ch in chains if step < nsteps[ch]]
            for g0 in range(0, len(act_ch), GS):
                grp = act_ch[g0:g0 + GS]
                info = {}
                for ch in grp:
                    jb, d = ch
                    ns = nsteps[ch]
                    nun = TN[jb[0]] // 128
                    n = step if d == 0 else (ns - 1 - step if small else nun - 1 - step)
                    o_ = ops[ch][step % 2]
                    DMA(o_[:], dnop[ch][n].rearrange("k p c -> p k c"), [C.db(("dnop", jb, d, n))], [o_])
                    info[ch] = dict(n=n, o=o_, Sc=S[ch][step % 2], Sn=S[ch][(step + 1) % 2], v=vn[ch][step % 2])
                for ch in grp:
                    I = info[ch]
                    I["pa"] = npc()
                    MM(I["pa"][:], I["o"][:, 1, :], I["Sc"][:], [I["o"], I["Sc"]], [I["pa"]])
                for ch in grp:
                    I = info[ch]
                    DVE("tensor_tensor", [I["o"], I["pa"]], [I["v"]], out=I["v"][:], in0=I["o"][:, 0, :], in1=I["pa"][:], op=ALU.subtract)
                for ch in grp:
                    I = info[ch]
                    I["pb"], I["pc"] = npc(), npc()
                    MMG([(I["pb"][:], I["Sc"][:], I["o"][:, 2, :]), (I["pb"][:], I["v"][:], I["o"][:, 3, :])], [I["Sc"], I["o"], I["v"]], [I["pb"]])
                    MM(I["pc"][:], I["o"][:, 4, :], I["v"][:], [I["o"], I["v"]], [I["pc"]])
                for ch in grp:
                    jb, d = ch
                    I = info[ch]
                    n = I["n"]
                    og = ostg[oc % 4]
                    oc += 1
                    TB[0].op("act", "copy", dict(out=og[:], in_=I["pb"][:]), [I["pb"]], [og])
                    DMA(dno[ch][:, n * 128:(n + 1) * 128], og[:], [og], [C.db(("dno", jb, d, n // 4))])
                    DVE("scalar_tensor_tensor", [I["Sc"], decs[ch], I["pc"]], [I["Sn"]], out=I["Sn"][:], in0=I["Sc"][:], scalar=decs[ch][:, n:n + 1], in1=I["pc"][:], op0=ALU.mult, op1=ALU.add)
        C.release_all()

        C.begin()
        of_ = C.ring(2, [128, 2, 512], F32)
        zt = C.ring(2, [128, 512], F32)
        osum = C.sb([128, 512], F32)
        osq = C.sb([128, 512], F32)
        orn = C.sb([128, 512], F32)
        res = C.ring(2, [128, 512], BF16)
        psD = C.ring(2, [128, 512], F32, psum=True)
        k = 0
        for jb in JOBS:
            sn, h = jb
            Tn = TN[sn]
            for i in range(2 if small else Tn // 512):
                o_b, z_b, r_b, p_ = of_[k % 2], zt[k % 2], res[k % 2], psD[k % 2]
                k += 1
                for d in range(2):
                    DMA(o_b[:, d, :], dno[(jb, d)][:, i * 512:(i + 1) * 512], [C.db(("dno", jb, d, i))], [o_b])
                DMA(z_b[:], dn_pre[jb][3, :, i * 512:(i + 1) * 512], [C.db(("dnpre", sn, h, 3, i))], [z_b])
                DVE("tensor_tensor", [o_b], [osum], out=osum[:], in0=o_b[:, 0, :], in1=o_b[:, 1, :], op=ALU.add)
                ACT(osq[:], osum[:], AF.Square, [osum], [osq])
                MM(p_[:], ones[:], osq[:], [ones, osq], [p_])
                ACT(orn[:], p_[:], AF.Sqrt, [p_], [orn], scale=1.0 / 128, bias=EPS)
                DVE("reciprocal", [orn], [orn], out=orn[:], in_=orn[:])
                DVE("tensor_tensor", [osum, orn], [osum], out=osum[:], in0=osum[:], in1=orn[:], op=ALU.mult)
                ACT(z_b[:], z_b[:], AF.Silu, [z_b], [z_b])
                DVE("scalar_tensor_tensor", [osum, dnorm, z_b], [r_b], out=r_b[:], in0=osum[:], scalar=dnorm[:, 0:1], in1=z_b[:], op0=ALU.mult, op1=ALU.mult)
                DMA(dnmix_v[sn][h * 128:(h + 1) * 128, i * 512:(i + 1) * 512], r_b[:], [r_b], [C.db(("dnmix",))])
        C.release_all()
        rotate()

        C.begin()
        qk = C.ring(2, [128, 512], F32)
        cs = C.ring(2, [128, 2, 512], F32)
        asq = C.sb([128, 512], F32)
        arn = C.sb([128, 512], F32)
        aqn = C.sb([128, 512], F32)
        t1 = C.sb([128, 512], F32)
        t2 = C.sb([128, 512], F32)
        qr = C.ring(2, [128, 512], BF16)
        psE = C.ring(2, [128, 512], F32, psum=True)
        psR = C.ring(2, [128, 512], F32, psum=True)
        k = 0
        ework = [("k", sn, i) for sn, Tn in SEQS for i in range(2 if small else Tn // 512)]
        ework += [("q", None, ot) for ot in (range(0, 12, 6) if small else range(12))]
        for ci, (kind, sn, i) in enumerate(ework):
            c_b = cs[ci % 2]
            cols = slice(i * 512, (i + 1) * 512)
            tabs = cs_in if kind == "k" else cso_in
            DMA(c_b[:, 0, :], tabs[0][:, cols], [], [c_b])
            DMA(c_b[:, 1, :], tabs[1][:, cols], [], [c_b])
            for qc in range(1 if kind == "k" else 4):
                gcol = 1 if kind == "k" else 0
                q_b, o_b, pe_, pr_ = qk[k % 2], qr[k % 2], psE[k % 2], psR[k % 2]
                k += 1
                if kind == "k":
                    DMA(q_b[:], att_k[sn][:, cols], [C.db(("attk", sn, i))], [q_b])
                else:
                    DMA(q_b[:], att_q[qc * 128:(qc + 1) * 128, cols], [C.db(("attq", qc, i))], [q_b])
                ACT(asq[:], q_b[:], AF.Square, [q_b], [asq])
                MM(pe_[:], onesbd[:], asq[:], [onesbd, asq], [pe_])
                ACT(arn[:], pe_[:], AF.Sqrt, [pe_], [arn], scale=1.0 / 64, bias=EPS)
                DVE("reciprocal", [arn], [arn], out=arn[:], in_=arn[:])
                DVE("scalar_tensor_tensor", [q_b, qkg, arn], [aqn], out=aqn[:], in0=q_b[:], scalar=qkg[:, gcol:gcol + 1], in1=arn[:], op0=ALU.mult, op1=ALU.mult)
                MM(pr_[:], permT[:], aqn[:], [permT, aqn], [pr_])
                DVE("tensor_tensor", [aqn, c_b], [t1], out=t1[:], in0=aqn[:], in1=c_b[:, 0, :], op=ALU.mult)
                DVE("tensor_tensor", [pr_, c_b], [t2], out=t2[:], in0=pr_[:], in1=c_b[:, 1, :], op=ALU.mult)
                POOL("tensor_tensor", [t1, t2], [o_b], out=o_b[:], in0=t1[:], in1=t2[:], op=ALU.add)
                if kind == "k":
                    DMA(att_kr[sn][:, cols], o_b[:], [o_b], [C.db(("attkr", sn, i))])
                else:
                    DMA(att_qr[qc * 128:(qc + 1) * 128, cols], o_b[:], [o_b], [C.db(("attqr", i))])
        C.release_all()

        C.begin()
        kr = C.sb([64, TS], BF16)
        vv = C.sb([128, TS // 128, 65], BF16)
        qh = C.ring(2, [64, 512], BF16)
        pT_ = C.ring(3, [128, 2, 512], BF16)
        psS = C.ring(2, [128, 2, 512], F32, psum=True)
        psO = C.ring(2, [128, 512], F32, psum=True)
        psB = C.ps([64, 512], F32)
        rs = C.sb([128, 512], F32)
        bc = C.sb([64, 512], F32)
        ao = C.ring(2, [64, 512], BF16)
        k = 0
        qi = 0
        for sn, Tn in SEQS:
            nkt = 8 if small else Tn // 128
            ots = [o for o in OWN[sn] if (not small or o % 6 == 0)]
            for kv in range(2):
                DMA(kr[:, 0:Tn], att_kr[sn][kv * 64:(kv + 1) * 64, :], [C.db(("attkr", sn, i)) for i in range(Tn // 512)], [kr])
                POOL("memset", [], [vv], ap=vv[:, 0:Tn // 128, 64:65], constant=1.0)
                DMA(vv[:, 0:Tn // 128, 0:64], att_v[sn][:, kv * 64:(kv + 1) * 64].rearrange("(k p) d -> p k d", p=128),
                    [C.db(("attv", sn, i)) for i in range(Tn // 512)], [vv])
                for qh_ in range(4):
                    h = kv * 4 + qh_
                    for ot in ots:
                        q_b, po, a_b = qh[qi % 2], psO[qi % 2], ao[qi % 2]
                        qi += 1
                        cols = slice(ot * 512, (ot + 1) * 512)
                        DMA(q_b[:], att_qr[h * 64:(h + 1) * 64, cols], [C.db(("attqr", ot))], [q_b])
                        npair = nkt // 2
                        k0 = k
                        k += npair

                        def emit_s(j):
                            ps_ = psS[(k0 + j) % 2]
                            for e_ in range(2):
                                kt_ = 2 * j + e_
                                MM(ps_[:, e_, :], kr[:, kt_ * 128:(kt_ + 1) * 128], q_b[:], [kr, q_b], [ps_])
                        for j in range(min(2, npair)):
                            emit_s(j)
                        for j in range(npair):
                            ps_, p_b = psS[(k0 + j) % 2], pT_[(k0 + j) % 3]
                            ACT(p_b[:], ps_[:], AF.Exp, [ps_], [p_b], scale=0.125)
                            for e_ in range(2):
                                kt_ = 2 * j + e_
                                MM(po[0:65, :], vv[:, kt_, :], p_b[:, e_, :], [vv, p_b], [po], start=(kt_ == 0), stop=(kt_ == nkt - 1))
                            if j + 2 < npair:
                                emit_s(j + 2)
                        DVE("reciprocal", [po], [rs], out=rs[64:65, :], in_=po[64:65, :])
                        MM(psB[:], ones[64:65, 0:64], rs[64:65, :], [ones, rs], [psB])
                        TB[0].op("act", "copy", dict(out=bc[:], in_=psB[:]), [psB], [bc])
                        DVE("tensor_tensor", [po, bc], [a_b], out=a_b[:], in0=po[0:64, :], in1=bc[:], op=ALU.mult)
                        DMA(attmix[h * 64:(h + 1) * 64, cols], a_b[:], [a_b], [C.db(("attmix", ot))])
        C.release_all()
        C.release_all()
        rotate({"pool": 16})

        build_p2_body(nc, C, TB[0], xo_in, dnmix, attmix, small)
        C.ninst += TB[0].ninst
        print("fused instructions", C.ninst)
    return nc


def build_p2_body(nc, C, T, x_in, dnmix, attmix, small):
    nblk = 2 if small else NTOK2 // 128

    def din(name, shape, dt=F32):
        return nc.dram_tensor(name, list(shape), dt, kind="ExternalInput").ap()

    p_in = din("p", [NTOK2, 256])
    wout_in = din("w_out", [D, D])
    wq_in = din("peer_query", [D, 2048])
    keysT_in = din("keysT", [128, 2, 128])
    pu_in = din("peer_u", [16384, D])
    pv_in = din("peer_v", [16384, D])
    pproj_in = din("ple_proj", [256, D])
    pgate_in = din("ple_gate", [D, D])
    gains_in = din("gains", [3, 128, D])
    ident_in = din("ident2", [128, 128])
    mixidx_in = din("mixidx", [128, (NTOK2 // 128) * 4], I32)
    y_out = nc.dram_tensor("y", [NTOK2, D], F32, kind="ExternalOutput").ap()

    if True:
        def DMA(out, in_, r, w, eng="sp"):
            return T.dma(eng, "dma_start", dict(out=out, in_=in_), r, w)

        def ACT(out, in_, func, r, w, **kw):
            return T.op("act", "activation", dict(out=out, in_=in_, func=func, **kw), r, w)

        def DVE(name, r, w, **kw):
            return T.op("dve", name, kw, r, w)

        def POOL(name, r, w, **kw):
            return T.op("pool", name, kw, r, w)

        def MM(out, lhsT, rhs, r, w, start=True, stop=True):
            return T.op("pe", "matmul", dict(out=out, lhsT=lhsT, rhs=rhs, start=start, stop=stop), r, w)

        def MMG(lst, r, w):
            return T.group("pe", [("matmul", dict(out=o, lhsT=l, rhs=rh, start=(i == 0), stop=(i == len(lst) - 1))) for i, (o, l, rh) in enumerate(lst)], r, w)

        ev = [0]

        def evac(out_ap, in_ap, reads, writes):
            ev[0] += 1
            if ev[0] % 2:
                T.op("act", "copy", dict(out=out_ap, in_=in_ap), reads, writes)
            else:
                T.op("dve", "tensor_copy", dict(out=out_ap, in_=in_ap), reads, writes)

        ident = C.sb([128, 128], F32)
        identb = C.sb([128, 128], BF16)
        gains = C.sb([128, 3, D], F32)
        wout = C.sb([128, 8, D], BF16)
        wq = C.sb([128, 8, 2048], BF16)
        wgate = C.sb([128, 8, D], BF16)
        wproj = C.sb([128, 2, D], BF16)
        keysf = C.sb([128, 2, 128], F32)
        keysT = C.sb([128, 2, 128], BF16)
        C.begin()
        wst = C.ring(2, [128, 2048], F32)
        DMA(ident[:], ident_in, [], [ident])
        DVE("tensor_copy", [ident], [identb], out=identb[:], in_=ident[:])
        DMA(gains[:], gains_in.rearrange("g p d -> p g d"), [], [gains])
        DMA(keysf[:], keysT_in, [], [keysf])
        DVE("tensor_copy", [keysf], [keysT], out=keysT[:], in_=keysf[:])
        k = 0
        for (src, dst, nch, ncol) in ((wout_in, wout, 8, D), (wq_in, wq, 8, 2048), (pgate_in, wgate, 8, D), (pproj_in, wproj, 2, D)):
            for c in range(nch):
                w = wst[k % 2]
                k += 1
                DMA(w[:, 0:ncol], src[c * 128:(c + 1) * 128, :], [], [w])
                evac(dst[:, c, :], w[:, 0:ncol], [w], [dst])
        puv16 = nc.dram_tensor("puv16", [16384, 2, D], BF16).ap()
        s32 = C.ring(2, [128, 4, D], F32)
        s16 = C.ring(2, [128, 4, D], BF16)
        k = 0
        for t_, src in enumerate((pu_in, pv_in)):
            for kk in range(16384 // 512):
                a, b_ = s32[k % 2], s16[k % 2]
                k += 1
                rows = slice(kk * 512, (kk + 1) * 512)
                DMA(a[:], src[rows, :].rearrange("(k p) d -> p k d", p=128), [], [a])
                evac(b_[:, 0:2, :], a[:, 0:2, :], [a], [b_])
                evac(b_[:, 2:4, :], a[:, 2:4, :], [a], [b_])
                DMA(puv16[rows, t_, :].rearrange("(k p) d -> p k d", p=128), b_[:], [b_], [C.db(("tab16", t_, kk))])
        puv_in = puv16.rearrange("e t d -> e (t d)")
        C.release_all()

        mixidx = C.sb([128, (NTOK2 // 128) * 4], I32)
        DMA(mixidx[:], mixidx_in, [], [mixidx])
        xt = C.ring(2, [128, D], F32)
        mxt = C.ring(2, [128, 8, 128], BF16)
        mxb = [[Buf(mxt[r_].t) for _ in range(8)] for r_ in range(2)]
        pt = C.ring(2, [128, 256], F32)
        h1 = C.sb([128, D], F32)
        junk = C.sb([128, D], BF16)
        ss = C.sb([128, 4], F32)
        xnb = C.sb([128, D], BF16)
        xnT = C.sb([128, 8, 128], BF16)
        qT = C.sb([128, 16, 128], BF16)
        sc = C.sb([128, 16, 128], F32)
        wrk = C.sb([128, 128], F32)
        tv = C.sb([128, 16, 16], F32)
        ti = C.sb([128, 16, 16], U32)
        tif = C.sb([128, 16, 16], F32)
        tif128 = C.sb([128, 16, 16], F32)
        cand_s = C.sb([128, 8, 256], F32)
        cand_i = C.sb([128, 8, 256], F32)
        wrk2 = C.sb([128, 256], F32)
        junk2 = C.sb([128, 256], F32)
        ts = C.sb([128, 8, 16], F32)
        negm = C.sb([128, 8], F32)
        gsum = C.sb([128, 8], F32)
        ge = C.sb([128, 8, 16], F32)
        gates = C.sb([128, 128], F32)
        idxf = C.sb([128, 128], F32)
        idx = C.sb([128, 128], I32)
        hraw = C.sb([128, 128], F32)
        hg = C.sb([128, 128], F32)
        wgt = C.sb([128, 128], F32)
        G = 4
        NR = 3
        UVt = C.ring(NR, [128, G, 2 * D], BF16)
        UVb = [[Buf(UVt[r_].t) for _ in range(G)] for r_ in range(NR)]
        dg = C.ring(4, [128, 128], BF16)
        h2 = C.sb([128, D], F32)
        h2b = C.sb([128, D], BF16)
        h2T = C.sb([128, 8, 128], BF16)
        pb = C.sb([128, 256], BF16)
        pTt = C.sb([128, 2, 128], BF16)
        ple = C.sb([128, D], F32)
        sg = h1
        yt = [ple]
        psG = C.ring(4, [128, 512], F32, psum=True)
        psAcc = C.ps([128, 2, 512], F32)
        psT = C.ps([128, 8, 128], BF16)
        gi = [0]

        def npg():
            gi[0] += 1
            return psG[gi[0] % 4]

        def load(b):
            sl = slice(b * 128, (b + 1) * 128)
            DMA(xt[b % 2][:], x_in[sl, :], [], [xt[b % 2]])
            for c in range(4):
                T.dma("pool", "indirect_dma_start", dict(out=mxt[b % 2][:, c, :], out_offset=None, in_=dnmix,
                                                         in_offset=bass.IndirectOffsetOnAxis(ap=mixidx[:, b * 4 + c:b * 4 + c + 1], axis=0)),
                      [mixidx], [mxb[b % 2][c]])
            for c in range(4, 8):
                DMA(mxt[b % 2][:, c, :], attmix[(c - 4) * 128:(c - 3) * 128, sl], [], [mxb[b % 2][c]])
            DMA(pt[b % 2][:], p_in[sl, :], [], [pt[b % 2]])

        def rms(src, gidx, out_ap, out_buf):
            ACT(junk[:], src[:], AF.Square, [src], [junk, ss], accum_out=ss[:, 0:1])
            ACT(ss[:, 0:1], ss[:, 0:1], AF.Sqrt, [ss], [ss], scale=1.0 / D, bias=EPS)
            DVE("reciprocal", [ss], [ss], out=ss[:, 0:1], in_=ss[:, 0:1])
            DVE("scalar_tensor_tensor", [src, ss, gains], [out_buf], out=out_ap, in0=src[:], scalar=ss[:, 0:1], in1=gains[:, gidx, :], op0=ALU.mult, op1=ALU.mult)

        def transpose8(src_b, dstT):
            T.group("pe", [("transpose", dict(out=psT[:, c, :], in_=src_b[:, c * 128:(c + 1) * 128], identity=identb[:])) for c in range(8)], [src_b, identb], [psT])
            evac(dstT[:], psT[:], [psT], [dstT])

        outs = []
        load(0)
        for b in range(nblk):
            if b + 1 < nblk:
                load(b + 1)
            x_b, m_b, p_b = xt[b % 2], mxt[b % 2], pt[b % 2]
            for half in range(2):
                hs = slice(half * 512, (half + 1) * 512)
                pg = npg()
                MMG([(pg[:], m_b[:, c, :], wout[:, c, hs]) for c in range(8)], [wout] + mxb[b % 2], [pg])
                DVE("tensor_tensor", [pg, x_b], [h1], out=h1[:, hs], in0=pg[:], in1=x_b[:, hs], op=ALU.add)
            rms(h1, 0, xnb[:], xnb)
            transpose8(xnb, xnT)
            for g4 in range(4):
                pg = npg()
                for u in range(4):
                    hh = g4 * 4 + u
                    MMG([(pg[:, u * 128:(u + 1) * 128], wq[:, c, hh * 128:(hh + 1) * 128], xnT[:, c, :]) for c in range(8)], [wq, xnT], [pg])
                evac(qT[:, g4 * 4:(g4 + 1) * 4, :], pg[:].rearrange("p (u t) -> p u t", u=4), [pg], [qT])
            for g4 in range(4):
                pg = npg()
                for u in range(4):
                    hh = g4 * 4 + u
                    MM(pg[:, u * 128:(u + 1) * 128], qT[:, hh, :], keysT[:, hh % 2, :], [qT, keysT], [pg])
                evac(sc[:, g4 * 4:(g4 + 1) * 4, :], pg[:].rearrange("p (u t) -> p u t", u=4), [pg], [sc])
            for hh in range(16):
                DVE("max", [sc], [tv], out=tv[:, hh, 0:8], in_=sc[:, hh, :])
                DVE("max_index", [sc, tv], [ti], out=ti[:, hh, 0:8], in_max=tv[:, hh, 0:8], in_values=sc[:, hh, :])
                DVE("match_replace", [sc, tv], [wrk], out=wrk[:], in_to_replace=tv[:, hh, 0:8], in_values=sc[:, hh, :], imm_value=-1e30)
                DVE("max", [wrk], [tv], out=tv[:, hh, 8:16], in_=wrk[:])
                DVE("max_index", [wrk, tv], [ti], out=ti[:, hh, 8:16], in_max=tv[:, hh, 8:16], in_values=wrk[:])
            DVE("tensor_copy", [ti], [tif], out=tif[:], in_=ti[:])
            tv4 = tv[:].rearrange("p (h two) k -> p h two k", two=2)
            tf4 = tif[:].rearrange("p (h two) k -> p h two k", two=2)
            DVE("tensor_scalar_mul", [tif], [tif128], out=tif128[:], in0=tif[:], scalar1=128.0)
            t128 = tif128[:].rearrange("p (h two) k -> p h two k", two=2)
            DVE("tensor_tensor", [tv], [cand_s], out=cand_s[:].rearrange("p h (a b) -> p h a b", a=16),
                in0=tv4[:, :, 0, :].unsqueeze(3).to_broadcast([128, 8, 16, 16]), in1=tv4[:, :, 1, :].unsqueeze(2).to_broadcast([128, 8, 16, 16]), op=ALU.add)
            DVE("tensor_tensor", [tif, tif128], [cand_i], out=cand_i[:].rearrange("p h (a b) -> p h a b", a=16),
                in0=t128[:, :, 0, :].unsqueeze(3).to_broadcast([128, 8, 16, 16]), in1=tf4[:, :, 1, :].unsqueeze(2).to_broadcast([128, 8, 16, 16]), op=ALU.add)
            for h in range(8):
                DVE("max", [cand_s], [ts], out=ts[:, h, 0:8], in_=cand_s[:, h, :])
                DVE("match_replace", [cand_s, ts], [wrk2], out=wrk2[:], in_to_replace=ts[:, h, 0:8], in_values=cand_s[:, h, :], imm_value=-1e30)
                DVE("max", [wrk2], [ts], out=ts[:, h, 8:16], in_=wrk2[:])
                for r_ in range(16):
                    DVE("scalar_tensor_tensor", [cand_s, ts, cand_i], [junk2, idxf], out=junk2[:], in0=cand_s[:, h, :], scalar=ts[:, h, r_:r_ + 1], in1=cand_i[:, h, :],
                        op0=ALU.is_equal, op1=ALU.mult, accum_out=idxf[:, h * 16 + r_:h * 16 + r_ + 1])
            DVE("tensor_scalar_min", [idxf], [idxf], out=idxf[:], in0=idxf[:], scalar1=16383.0)
            DVE("tensor_copy", [idxf], [idx], out=idx[:], in_=idxf[:])
            DVE("tensor_scalar_mul", [ts], [negm], out=negm[:], in0=ts[:, :, 0], scalar1=-1.0)
            for h in range(8):
                ACT(ge[:, h, :], ts[:, h, :], AF.Exp, [ts, negm], [ge, gsum], bias=negm[:, h:h + 1], accum_out=gsum[:, h:h + 1])
            DVE("reciprocal", [gsum], [gsum], out=gsum[:], in_=gsum[:])
            for h in range(8):
                DVE("tensor_scalar_mul", [ge, gsum], [gates], out=gates[:, h * 16:(h + 1) * 16], in0=ge[:, h, :], scalar1=gsum[:, h:h + 1])
            def issue_gathers(grp):
                r_ = grp % NR
                for s in range(G):
                    kk = grp * G + s
                    T.dma("pool", "indirect_dma_start", dict(out=UVt[r_][:, s, :], out_offset=None, in_=puv_in, in_offset=bass.IndirectOffsetOnAxis(ap=idx[:, kk:kk + 1], axis=0)), [idx], [UVb[r_][s]])
            for g_ in range(NR - 1):
                issue_gathers(g_)
            for grp in range(128 // G):
                r_ = grp % NR
                if grp + NR - 1 < 128 // G:
                    issue_gathers(grp + NR - 1)
                for s in range(G):
                    kk = grp * G + s
                    DVE("scalar_tensor_tensor", [UVb[r_][s], xnb], [junk, hraw], out=junk[:], in0=UVt[r_][:, s, 0:D], scalar=1.0, in1=xnb[:], op0=ALU.mult, op1=ALU.mult, accum_out=hraw[:, kk:kk + 1])
                gs = slice(grp * G, (grp + 1) * G)
                ACT(hg[:, gs], hraw[:, gs], AF.Gelu, [hraw], [hg])
                DVE("tensor_tensor", [hg, gates], [wgt], out=wgt[:, gs], in0=hg[:, gs], in1=gates[:, gs], op=ALU.mult)
                for s in range(G):
                    kk = grp * G + s
                    d_ = dg[kk % 4]
                    ACT(d_[:], identb[:], AF.Copy, [identb, wgt], [d_], scale=wgt[:, kk:kk + 1])
                    for half in range(2):
                        MM(psAcc[:, half, :], d_[:], UVt[r_][:, s, D + half * 512:D + (half + 1) * 512], [d_, UVb[r_][s]], [psAcc], start=(kk == 0), stop=(kk == 127))
            for half in range(2):
                hs = slice(half * 512, (half + 1) * 512)
                DVE("tensor_tensor", [psAcc, h1], [h2], out=h2[:, hs], in0=psAcc[:, half, :], in1=h1[:, hs], op=ALU.add)
            T.op("act", "copy", dict(out=pb[:], in_=p_b[:]), [p_b], [pb])
            T.group("pe", [("transpose", dict(out=psT[:, c, :], in_=pb[:, c * 128:(c + 1) * 128], identity=identb[:])) for c in range(2)], [pb, identb], [psT])
            evac(pTt[:], psT[:, 0:2, :], [psT], [pTt])
            pgs = [npg(), npg()]
            for half in range(2):
                hs = slice(half * 512, (half + 1) * 512)
                MMG([(pgs[half][:], pTt[:, c, :], wproj[:, c, hs]) for c in range(2)], [pTt, wproj], [pgs[half]])
                ACT(junk[:, hs], pgs[half][:], AF.Square, [pgs[half]], [junk, ss], accum_out=ss[:, 1 + half:2 + half])
            DVE("tensor_tensor", [ss], [ss], out=ss[:, 3:4], in0=ss[:, 1:2], in1=ss[:, 2:3], op=ALU.add)
            ACT(ss[:, 3:4], ss[:, 3:4], AF.Sqrt, [ss], [ss], scale=1.0 / D, bias=EPS)
            DVE("reciprocal", [ss], [ss], out=ss[:, 3:4], in_=ss[:, 3:4])
            for half in range(2):
                hs = slice(half * 512, (half + 1) * 512)
                DVE("scalar_tensor_tensor", [pgs[half], ss, gains], [ple], out=ple[:, hs], in0=pgs[half][:], scalar=ss[:, 3:4], in1=gains[:, 1, hs], op0=ALU.mult, op1=ALU.mult)
            T.op("act", "copy", dict(out=h2b[:], in_=h2[:]), [h2], [h2b])
            transpose8(h2b, h2T)
            for half in range(2):
                hs = slice(half * 512, (half + 1) * 512)
                pg = npg()
                MMG([(pg[:], h2T[:, c, :], wgate[:, c, hs]) for c in range(8)], [h2T, wgate], [pg])
                ACT(sg[:, hs], pg[:], AF.Sigmoid, [pg], [sg])
            DVE("tensor_tensor", [ple, sg], [ple], out=ple[:], in0=ple[:], in1=sg[:], op=ALU.mult)
            DVE("tensor_tensor", [ple, h2], [h2], out=h2[:], in0=ple[:], in1=h2[:], op=ALU.add)
            y_b = yt[0]
            rms(h2, 2, y_b[:], y_b)
            ob = C.db(("y", b))
            DMA(y_out[b * 128:(b + 1) * 128, :], y_b[:], [y_b], [ob])
        T.final_waits("sp", list(C.dbufs.values()))
        T.emit()


def _consts():
    ident = np.eye(128, dtype=np.float32)
    ones = np.ones((128, 128), np.float32)
    onesbd = np.zeros((128, 128), np.float32)
    onesbd[:64, :64] = 1
    onesbd[64:, 64:] = 1
    s = np.arange(128)[:, None]
    c = np.arange(128)[None, :]
    pmask = np.stack([np.where(c >= s, 0.0, -NEG), np.where(c <= s, 0.0, -NEG)]).astype(np.float32)
    strict = np.stack([(c > s), (c < s)]).astype(np.float32)
    P = np.zeros((64, 64), np.float32)
    for blk in range(2):
        for i in range(32):
            if i < 16:
                P[blk * 32 + i, blk * 32 + i + 16] = -1.0
            else:
                P[blk * 32 + i, blk * 32 + i - 16] = 1.0
    permT = np.zeros((128, 128), np.float32)
    permT[:64, :64] = P.T
    permT[64:, 64:] = P.T
    inv_freq = (np.float32(10000.0) ** (-np.arange(0, 32, 2, dtype=np.float32) / np.float32(32))).astype(np.float32)
    t = np.arange(TS)
    rowpos = (t // 64).astype(np.float32)
    colpos = (t % 64).astype(np.float32)
    ang = np.zeros((64, TS), np.float32)
    for dd in range(64):
        pos = rowpos if dd < 32 else colpos
        ang[dd] = pos * inv_freq[dd % 16]
    cosT = np.tile(np.cos(ang).astype(np.float32), (2, 1))
    sinT = np.tile(np.sin(ang).astype(np.float32), (2, 1))
    return dict(ident=ident, ones=ones, onesbd=onesbd, pmask=pmask, strict=strict, permT=permT, cosT=cosT, sinT=sinT)


def fused_inputs(inp, c, consts):
    w_in = inp["w_in"][0]
    conv_w = inp["conv_w"][0]
    ps_, hp, ss_, r = c // 2, c % 2, c // 4, c % 4
    f = lambda a: np.ascontiguousarray(a, dtype=np.float32)
    wdn = np.empty((4, D, 1024), np.float32)
    convw = np.empty((4, 128, 15), np.float32)
    gatep = np.empty((4, 128, 4), np.float32)
    for hd in range(4):
        cols = [w_in[:, hd * 128:(hd + 1) * 128], w_in[:, 512 + hd * 128:512 + (hd + 1) * 128],
                w_in[:, 1024 + hd * 128:1024 + (hd + 1) * 128], w_in[:, 1536 + hd * 128:1536 + (hd + 1) * 128]]
        for g in range(4):
            cols.append(np.repeat(w_in[:, 2048 + 4 * g + hd:2048 + 4 * g + hd + 1], 128, axis=1))
        wdn[hd] = np.concatenate(cols, axis=1)
        for m in range(3):
            convw[hd][:, m * 5:(m + 1) * 5] = conv_w[:, m * 512 + hd * 128:m * 512 + (hd + 1) * 128].T
        gatep[hd] = np.stack([np.full(128, inp["a_log_fwd"][0][hd]), np.full(128, inp["a_log_bwd"][0][hd]),
                              np.full(128, inp["dt_bias_fwd"][0][hd]), np.full(128, inp["dt_bias_bwd"][0][hd])], axis=1)
    qb = 2064
    x_own = np.concatenate([inp["x_prompt"][ps_, hp * 2048:(hp + 1) * 2048], inp["x_sample"][ss_, r * 4096:(r + 1) * 4096]], axis=0)
    p_own = np.concatenate([inp["p_prompt"][0, ps_, hp * 2048:(hp + 1) * 2048], inp["p_sample"][0, ss_, r * 4096:(r + 1) * 4096]], axis=0)
    cosO = np.concatenate([consts["cosT"][:, hp * 2048:(hp + 1) * 2048], consts["cosT"][:, r * 4096:(r + 1) * 4096]], axis=1)
    sinO = np.concatenate([consts["sinT"][:, hp * 2048:(hp + 1) * 2048], consts["sinT"][:, r * 4096:(r + 1) * 4096]], axis=1)
    gains = np.stack([np.tile(inp["ffn_norm"][0][None], (128, 1)), np.tile(inp["ple_norm"][0][None], (128, 1)), np.tile(inp["final_norm"][None], (128, 1))])
    keysT = np.stack([inp["peer_keys_a"][0].T, inp["peer_keys_b"][0].T], axis=1)
    pp = np.arange(128)[:, None]
    cc = np.arange(4)[None, :]
    mixidx = np.empty((128, 48 * 4), np.int32)
    for b in range(48):
        if b < 16:
            mixidx[:, b * 4:(b + 1) * 4] = (cc * 128 + pp) * (TP // 128) + hp * 16 + b
        else:
            mixidx[:, b * 4:(b + 1) * 4] = 512 * (TP // 128) + (cc * 128 + pp) * (TS // 128) + r * 32 + (b - 16)
    d = dict(xp=f(inp["x_prompt"][ps_]), xs=f(inp["x_sample"][ss_]), x=f(x_own), p=f(p_own), wdn=wdn,
             wkv=f(w_in[:, qb + 512:qb + 768]), wqa=f(w_in[:, qb:qb + 512]),
             anorm=f(inp["attn_norm"][0].reshape(8, 128).T), convw=convw, gatep=f(gatep),
             dnorm=f(inp["dn_out_norm"][0].reshape(128, 1)),
             qkg=f(np.stack([np.tile(inp["q_norm"][0], 2), np.tile(inp["k_norm"][0], 2)], axis=1)),
             ident=consts["ident"], ones=consts["ones"], onesbd=consts["onesbd"], pmask=consts["pmask"], strict=consts["strict"],
             permT=consts["permT"], cosT=consts["cosT"], sinT=consts["sinT"], cosO=f(cosO), sinO=f(sinO),
             w_out=f(inp["w_out"][0]), peer_query=f(inp["peer_query"][0]), keysT=f(keysT), peer_u=f(inp["peer_u"][0]),
             peer_v=f(inp["peer_v"][0]), ple_proj=f(inp["ple_proj"][0]), ple_gate=f(inp["ple_gate"][0]), gains=f(gains),
             ident2=consts["ident"], mixidx=mixidx)
    return d


def kernel(**inputs):
    inp = {k: np.asarray(v) for k, v in inputs.items()}
    consts = _consts()
    cores = list(range(NCORES))
    nc = build_fused()
    in_maps = [fused_inputs(inp, c, consts) for c in cores]
    res = run_bass_kernel_spmd(nc, in_maps, core_ids=cores).results
    y_p = np.empty((4, TP, D), np.float32)
    y_s = np.empty((2, TS, D), np.float32)
    for c in cores:
        ps_, hp, ss_, r = c // 2, c % 2, c // 4, c % 4
        y = res[c]["y"]
        y_p[ps_, hp * 2048:(hp + 1) * 2048] = y[:2048]
        y_s[ss_, r * 4096:(r + 1) * 4096] = y[2048:]
    return (y_p, y_s)
```

```python
import numpy as np
from contextlib import ExitStack
import ml_dtypes
import concourse.bass as bass
import concourse.mybir as mybir
from concourse.bass_utils import run_bass_kernel_spmd

F32 = mybir.dt.float32
BF16 = mybir.dt.bfloat16
U32 = mybir.dt.uint32
I32 = mybir.dt.int32
AF = mybir.ActivationFunctionType
ALU = mybir.AluOpType

D = 1024
TP = 4096
TS = 16384
NCORES = 8
EPS = 1e-6
NEG = -30000.0
NTOK2 = 6144


class Buf:
    __slots__ = ("name", "last_w", "readers", "t")

    def __init__(self, t=None, name=""):
        self.name = name
        self.t = t
        self.last_w = None
        self.readers = []

    def __getitem__(self, k):
        return self.t[k]


class Trk:
    ENGS = ("pe", "act", "dve", "pool", "sp")
    NS = 8

    def __init__(self, nc, stack, tag, ns=None):
        self.nc = nc
        ns = ns or {}
        self.items = {e: [] for e in self.ENGS}
        self.sem = {}
        self.count = {e: 0 for e in self.ENGS}
        self.seen = {e: {} for e in self.ENGS}
        self.dma_n = {e: 0 for e in self.ENGS}
        self.dma_sems = {}
        self.semobj = {}
        self.ninst = 0
        for e in self.ENGS:
            s = stack.enter_context(nc.semaphore(f"{tag}_c_{e}"))
            self.sem[e] = s
            self.semobj[id(s)] = s
        for e in ("sp", "pool"):
            self.dma_sems[e] = []
            for i in range(ns.get(e, self.NS)):
                s = stack.enter_context(nc.semaphore(f"{tag}_d_{e}{i}"))
                self.dma_sems[e].append(s)
                self.semobj[id(s)] = s

    def _deps(self, eng, reads, writes, extra=(), skip_self=False):
        need = {}
        own = id(self.sem[eng])

        def add(tok):
            if tok is None:
                return
            k, v = tok
            if skip_self and k == own:
                return
            if need.get(k, 0) < v:
                need[k] = v
        for r in reads:
            add(r.last_w)
        for w in writes:
            add(w.last_w)
            for t in w.readers:
                add(t)
        for t in extra:
            add(t)
        seen = self.seen[eng]
        waits = []
        for k, v in need.items():
            if seen.get(k, 0) < v:
                seen[k] = v
                waits.append((self.semobj[k], v))
        return waits

    def _finish(self, tok, reads, writes):
        for w in writes:
            w.last_w = tok
            w.readers = []
        for r in reads:
            if r not in writes:
                if len(r.readers) > 64:
                    r.readers = r.readers[-32:]
                r.readers.append(tok)

    def op(self, eng, name, kw, reads=(), writes=()):
        return self.group(eng, [(name, kw)], reads, writes)

    def group(self, eng, fns, reads=(), writes=()):
        waits = self._deps(eng, reads, writes, skip_self=(eng == "pe"))
        self.count[eng] += 1
        tok = (id(self.sem[eng]), self.count[eng])
        self.items[eng].append((waits, list(fns), (self.sem[eng], 1)))
        self._finish(tok, reads, writes)
        self.ninst += len(fns)
        return tok

    def dma(self, eng, name, kw, reads=(), writes=()):
        fn = (name, kw)
        n = self.dma_n[eng]
        self.dma_n[eng] += 1
        nsl = len(self.dma_sems[eng])
        slot = n % nsl
        val = 16 * (n // nsl + 1)
        s = self.dma_sems[eng][slot]
        extra = [(id(s), val - 16)] if val > 16 else []
        waits = self._deps(eng, reads, writes, extra)
        tok = (id(s), val)
        self.items[eng].append((waits, [fn], (s, 16)))
        self._finish(tok, reads, writes)
        self.ninst += 1
        return tok

    def final_waits(self, eng, bufs):
        waits = self._deps(eng, bufs, [])
        self.items[eng].append((waits, [], None))

    def emit(self):
        nc = self.nc
        items = self.items

        def run(e, lst):
            for waits, fns, inc in lst:
                for s, v in waits:
                    e.wait_ge(s, v)
                ins = None
                for name, kw in fns:
                    ins = getattr(e, name)(**kw)
                if inc is not None and ins is not None:
                    ins.then_inc(inc[0], inc[1])

        with nc.Block() as block:
            @block.tensor
            def _(e):
                run(e, items["pe"])

            @block.scalar
            def _(e):
                run(e, items["act"])

            @block.vector
            def _(e):
                run(e, items["dve"])

            @block.gpsimd
            def _(e):
                run(e, items["pool"])

            @block.sync
            def _(e):
                run(e, items["sp"])
        self.items = {e: [] for e in self.ENGS}


class Ctx:
    def __init__(self, nc, st):
        self.nc = nc
        self.st = st
        self.ntrk = 0
        self.T = Trk(nc, st, "k0")
        self.n = 0
        self.dbufs = {}
        self.subs = []
        self.sub = None
        self.all = []
        self.ninst = 0

    def begin(self):
        self.subs.append(ExitStack())
        self.sub = self.subs[-1]

    def release_all(self):
        if any(self.T.items[e] for e in self.T.ENGS):
            self.T.final_waits("sp", list(self.dbufs.values()))
            self.T.final_waits("pool", list(self.dbufs.values()))
            self.T.emit()
        self.subs.pop().close()
        self.sub = self.subs[-1] if self.subs else None

    def switch_trk(self, ns=None):
        assert not any(self.T.items[e] for e in self.T.ENGS)
        self.ninst += self.T.ninst
        self.ntrk += 1
        self.T = Trk(self.nc, self.st, f"k{self.ntrk}", ns=ns)
        for b in self.all:
            b.last_w = None
            b.readers = []
        return self.T

    def _reg(self, b):
        self.all.append(b)
        return b

    def sb(self, shape, dt, name=None):
        self.n += 1
        st = self.sub or self.st
        t = st.enter_context(self.nc.sbuf_tensor(name or f"sb{self.n}", shape, dt))
        return self._reg(Buf(t))

    def ps(self, shape, dt, name=None):
        self.n += 1
        st = self.sub or self.st
        t = st.enter_context(self.nc.psum_tensor(name or f"ps{self.n}", shape, dt))
        return self._reg(Buf(t))

    def ring(self, n, shape, dt, psum=False):
        return [(self.ps if psum else self.sb)(shape, dt) for _ in range(n)]

    def db(self, key):
        b = self.dbufs.get(key)
        if b is None:
            b = self._reg(Buf(None, str(key)))
            self.dbufs[key] = b
        return b


SEQS = (("p", TP), ("s", TS))
OWN = {"p": range(0, 4), "s": range(4, 12)}
NROWS_MIX = 512 * (TP // 128) + 512 * (TS // 128)


def build_fused(small=False):
    nc = bass.Bass("TRN2", target_bir_lowering=False)

    def din(name, shape, dt=F32):
        return nc.dram_tensor(name, list(shape), dt, kind="ExternalInput").ap()

    def dscr(name, shape, dt=F32):
        return nc.dram_tensor(name, list(shape), dt).ap()

    TN = dict(SEQS)
    x_in = {"p": din("xp", [TP, D]), "s": din("xs", [TS, D])}
    xo_in = din("x", [NTOK2, D])
    wdn_in = din("wdn", [4, D, 1024])
    wkv_in = din("wkv", [D, 256])
    wqa_in = din("wqa", [D, 512])
    anorm_in = din("anorm", [128, 8])
    convw_in = din("convw", [4, 128, 15])
    gatep_in = din("gatep", [4, 128, 4])
    dnorm_in = din("dnorm", [128, 1])
    qkg_in = din("qkg", [128, 2])
    ident_in = din("ident", [128, 128])
    ones_in = din("ones", [128, 128])
    onesbd_in = din("onesbd", [128, 128])
    pmask_in = din("pmask", [2, 128, 128])
    strict_in = din("strict", [2, 128, 128])
    permT_in = din("permT", [128, 128])
    cs_in = (din("cosT", [128, TS]), din("sinT", [128, TS]))
    cso_in = (din("cosO", [128, NTOK2]), din("sinO", [128, NTOK2]))
    JOBS = [(sn, h) for sn, _ in SEQS for h in range(4)]
    dn_pre = {jb: dscr(f"dnpre_{jb[0]}{jb[1]}", [8, 128, TN[jb[0]]]) for jb in JOBS}
    att_k = {sn: dscr(f"attk_{sn}", [128, Tn]) for sn, Tn in SEQS}
    att_v = {sn: dscr(f"attv_{sn}", [Tn, 128], BF16) for sn, Tn in SEQS}
    att_kr = {sn: dscr(f"attkr_{sn}", [128, Tn], BF16) for sn, Tn in SEQS}
    att_q = dscr("attq", [512, NTOK2])
    att_qr = dscr("attqr", [512, NTOK2], BF16)
    dnop = {(jb, d): dscr(f"dnop_{jb[0]}{jb[1]}_{d}", [TN[jb[0]] // 128, 5, 128, 128]) for jb in JOBS for d in range(2)}
    dnd = {(jb, d): dscr(f"dnd_{jb[0]}{jb[1]}_{d}", [128, TN[jb[0]] // 128]) for jb in JOBS for d in range(2)}
    dno = {(jb, d): dscr(f"dno_{jb[0]}{jb[1]}_{d}", [128, TN[jb[0]]]) for jb in JOBS for d in range(2)}
    dnmix = dscr("dnmix", [NROWS_MIX, 128], BF16)
    dnmix_v = {"p": dnmix[0:512 * (TP // 128), :].rearrange("(ch b) t -> ch (b t)", b=TP // 128),
               "s": dnmix[512 * (TP // 128):NROWS_MIX, :].rearrange("(ch b) t -> ch (b t)", b=TS // 128)}
    attmix = dscr("attmix", [512, NTOK2], BF16)

    with ExitStack() as st:
        C = Ctx(nc, st)
        TB = [C.T]

        def DMA(out, in_, r, w, eng="sp"):
            return TB[0].dma(eng, "dma_start", dict(out=out, in_=in_), r, w)

        def ACT(out, in_, func, r, w, **kw):
            return TB[0].op("act", "activation", dict(out=out, in_=in_, func=func, **kw), r, w)

        def DVE(name, r, w, **kw):
            return TB[0].op("dve", name, kw, r, w)

        def POOL(name, r, w, **kw):
            return TB[0].op("pool", name, kw, r, w)

        def MM(out, lhsT, rhs, r, w, start=True, stop=True):
            return TB[0].op("pe", "matmul", dict(out=out, lhsT=lhsT, rhs=rhs, start=start, stop=stop), r, w)

        def MMG(lst, r, w):
            return TB[0].group("pe", [("matmul", dict(out=o, lhsT=l, rhs=rh, start=(i == 0), stop=(i == len(lst) - 1))) for i, (o, l, rh) in enumerate(lst)], r, w)

        def TR(out, in_, idn, r, w):
            return TB[0].op("pe", "transpose", dict(out=out, in_=in_, identity=idn), r, w)

        ev = [0]

        def evac(out_ap, in_ap, reads, writes):
            ev[0] += 1
            if ev[0] % 2:
                TB[0].op("act", "copy", dict(out=out_ap, in_=in_ap), reads, writes)
            else:
                TB[0].op("dve", "tensor_copy", dict(out=out_ap, in_=in_ap), reads, writes)

        def rotate(ns=None):
            TB[0] = C.switch_trk(ns)

        C.begin()
        ident = C.sb([128, 128], F32)
        identb = C.sb([128, 128], BF16)
        ones = C.sb([128, 128], F32)
        onesbd = C.sb([128, 128], F32)
        pmask = C.sb([128, 2, 128], F32)
        strict = C.sb([128, 2, 128], F32)
        permT = C.sb([128, 128], F32)
        anorm = C.sb([128, 8], F32)
        convw = C.sb([128, 4, 15], F32)
        gatep = C.sb([128, 4, 4], F32)
        negA = C.sb([128, 4, 2], F32)
        dnorm = C.sb([128, 1], F32)
        qkg = C.sb([128, 2], F32)
        msk = C.sb([128, 512], F32)

        def load_consts():
            DMA(ident[:], ident_in, [], [ident])
            DMA(ones[:], ones_in, [], [ones])
            DMA(onesbd[:], onesbd_in, [], [onesbd])
            DMA(pmask[:], pmask_in.rearrange("d p c -> p d c"), [], [pmask])
            DMA(strict[:], strict_in.rearrange("d p c -> p d c"), [], [strict])
            DMA(permT[:], permT_in, [], [permT])
            DMA(anorm[:], anorm_in, [], [anorm])
            DMA(convw[:], convw_in.rearrange("j p k -> p j k"), [], [convw])
            DMA(gatep[:], gatep_in.rearrange("j p k -> p j k"), [], [gatep])
            DMA(dnorm[:], dnorm_in, [], [dnorm])
            DMA(qkg[:], qkg_in, [], [qkg])
            DVE("tensor_copy", [ident], [identb], out=identb[:], in_=ident[:])
            ACT(negA[:], gatep[:, :, 0:2], AF.Exp, [gatep], [negA])
            DVE("tensor_scalar_mul", [negA], [negA], out=negA[:], in0=negA[:], scalar1=-1.0)
            POOL("memset", [], [msk], ap=msk[:], constant=1.0)
            POOL("memset", [], [msk], ap=msk[:].rearrange("p (n c) -> p n c", c=128)[:, :, 0:1], constant=0.0)
        load_consts()

        C.begin()
        wdn = C.sb([128, 4, 8, 1024], BF16)
        wkv = C.sb([128, 8, 256], BF16)
        wqa = C.sb([128, 8, 512], BF16)
        wst = C.ring(2, [128, 1024], F32)
        k = 0
        for h in range(4):
            for c in range(8):
                w = wst[k % 2]
                k += 1
                DMA(w[:], wdn_in[h, c * 128:(c + 1) * 128, :], [], [w])
                DVE("tensor_scalar_mul", [w, anorm], [wdn], out=wdn[:, h, c, :], in0=w[:], scalar1=anorm[:, c:c + 1])
        for src, dst, ncol in ((wkv_in, wkv, 256), (wqa_in, wqa, 512)):
            for c in range(8):
                w = wst[k % 2]
                k += 1
                DMA(w[:, 0:ncol], src[c * 128:(c + 1) * 128, :], [], [w])
                DVE("tensor_scalar_mul", [w, anorm], [dst], out=dst[:, c, :], in0=w[:, 0:ncol], scalar1=anorm[:, c:c + 1])
        xt = C.ring(2, [128, 4, 1024], F32)
        junk = C.sb([128, 1024], BF16)
        ss = C.ring(2, [128, 4], F32)
        xn = C.ring(2, [128, 4, 1024], BF16)
        xnT = C.ring(2, [128, 8, 512], BF16)
        pT = C.ring(2, [128, 512], BF16, psum=True)
        psA = C.ring(4, [128, 512], F32, psum=True)
        psV = C.ps([128, 4, 128], F32)
        stg = C.ring(4, [128, 512], F32)
        vstg = C.ring(2, [128, 4, 128], BF16)
        work = [("full", sn, i) for sn, Tn in SEQS for i in range(2 if small else Tn // 512)]
        work += [("own", None, ot) for ot in (range(0, 12, 6) if small else range(12))]

        def load_x(wi):
            kind, sn, i = work[wi]
            b = xt[wi % 2]
            src = x_in[sn] if kind == "full" else xo_in
            DMA(b[:], src[i * 512:(i + 1) * 512, :].rearrange("(b p) d -> p b d", p=128), [], [b])

        load_x(0)
        kk = 0
        for wi, (kind, sn, i) in enumerate(work):
            if wi + 1 < len(work):
                load_x(wi + 1)
            x_b, ss_b, xn_b, xnT_b = xt[wi % 2], ss[wi % 2], xn[wi % 2], xnT[wi % 2]
            cols = slice(i * 512, (i + 1) * 512)
            for b in range(4):
                ACT(junk[:], x_b[:, b, :], AF.Square, [x_b], [junk, ss_b], accum_out=ss_b[:, b:b + 1])
            ACT(ss_b[:], ss_b[:], AF.Sqrt, [ss_b], [ss_b], scale=1.0 / D, bias=EPS)
            DVE("reciprocal", [ss_b], [ss_b], out=ss_b[:], in_=ss_b[:])
            for b in range(4):
                DVE("tensor_scalar_mul", [x_b, ss_b], [xn_b], out=xn_b[:, b, :], in0=x_b[:, b, :], scalar1=ss_b[:, b:b + 1])
            for c in range(8):
                p = pT[c % 2]
                TB[0].group("pe", [("transpose", dict(out=p[:, b * 128:(b + 1) * 128], in_=xn_b[:, b, c * 128:(c + 1) * 128], identity=identb[:])) for b in range(4)], [xn_b, identb], [p])
                evac(xnT_b[:, c, :], p[:], [p], [xnT_b])
            if kind == "full":
                for h in range(4):
                    for cc in range(8):
                        p, s_ = psA[kk % 4], stg[kk % 4]
                        kk += 1
                        MMG([(p[:], wdn[:, h, c, cc * 128:(cc + 1) * 128], xnT_b[:, c, :]) for c in range(8)], [wdn, xnT_b], [p])
                        evac(s_[:], p[:], [p], [s_])
                        DMA(dn_pre[(sn, h)][cc, :, cols], s_[:], [s_], [C.db(("dnpre", sn, h, cc, i))])
                p, s_ = psA[kk % 4], stg[kk % 4]
                kk += 1
                MMG([(p[:], wkv[:, c, 0:128], xnT_b[:, c, :]) for c in range(8)], [wkv, xnT_b], [p])
                evac(s_[:], p[:], [p], [s_])
                DMA(att_k[sn][:, cols], s_[:], [s_], [C.db(("attk", sn, i))])
                for b in range(4):
                    MMG([(psV[:, b, :], xnT_b[:, c, b * 128:(b + 1) * 128], wkv[:, c, 128:256]) for c in range(8)], [wkv, xnT_b], [psV])
                vs = vstg[wi % 2]
                evac(vs[:], psV[:], [psV], [vs])
                DMA(att_v[sn][cols, :].rearrange("(b p) d -> p b d", p=128), vs[:], [vs], [C.db(("attv", sn, i))])
            else:
                for qc in range(4):
                    p, s_ = psA[kk % 4], stg[kk % 4]
                    kk += 1
                    MMG([(p[:], wqa[:, c, qc * 128:(qc + 1) * 128], xnT_b[:, c, :]) for c in range(8)], [wqa, xnT_b], [p])
                    evac(s_[:], p[:], [p], [s_])
                    DMA(att_q[qc * 128:(qc + 1) * 128, cols], s_[:], [s_], [C.db(("attq", qc, i))])
        C.release_all()

        def p1b(jobs):
            C.begin()
            pre = C.ring(2, [128, 3, 516], F32)
            gts = C.ring(2, [128, 4, 512], F32)
            cv = C.sb([128, 3, 512], F32)
            sq = C.sb([128, 2, 512], F32)
            rn = C.sb([128, 2, 512], F32)
            qn = C.sb([128, 512], F32)
            kn = C.sb([128, 512], F32)
            sig = C.sb([128, 2, 512], F32)
            gl = C.sb([128, 2, 512], F32)
            gc = C.sb([128, 2, 512], F32)
            tmpb = C.sb([128, 512], F32)
            egc = C.sb([128, 512], F32)
            kb = C.ring(2, [128, 512], F32)
            kbg = C.ring(2, [128, 512], F32)
            qt = C.ring(2, [128, 512], F32)
            vb = C.ring(2, [128, 512], F32)
            ek = C.sb([128, 512], F32)
            kt = C.ring(2, [128, 512], F32)
            dec = C.ring(2, [128, 4], F32)
            psN = C.ring(2, [128, 512], F32, psum=True)
            psX = C.ring(6, [128, 128], F32, psum=True)
            NUQ = 4
            Vb = C.ring(NUQ, [128, 128], F32)
            Kbg = C.ring(NUQ, [128, 128], F32)
            opst = [C.ring(2, [128, 5, 128], F32) for _ in range(NUQ)]
            dif = C.ring(NUQ, [128, 128], F32)
            dcy = C.ring(NUQ, [128, 128], F32)
            dcyS = C.ring(NUQ, [128, 128], F32)
            Pm = [C.ring(2, [128, 128], F32) for _ in range(NUQ)]
            PTm = [C.ring(2, [128, 128], F32) for _ in range(NUQ)]
            TT = [C.ring(2, [128, 128], F32) for _ in range(NUQ)]
            px = [0]

            def nps():
                px[0] += 1
                return psX[px[0] % 6]

            un = 0
            for jb in jobs:
                sn, h = jb
                Tn = TN[sn]
                ntl = 2 if small else Tn // 512
                for i in range(ntl):
                    pr, gt = pre[i % 2], gts[i % 2]
                    lo, hi = i * 512 - 2, i * 512 + 514
                    clo, chi = max(lo, 0), min(hi, Tn)
                    if lo < 0:
                        POOL("memset", [], [pr], ap=pr[:, :, 0:2], constant=0.0)
                    if hi > Tn:
                        POOL("memset", [], [pr], ap=pr[:, :, 514:516], constant=0.0)
                    DMA(pr[:, :, clo - lo:chi - lo], dn_pre[jb][0:3, :, clo:chi].rearrange("m p t -> p m t"),
                        [C.db(("dnpre", sn, h, m_, t_)) for m_ in range(3) for t_ in (i - 1, i, i + 1) if 0 <= t_ < Tn // 512], [pr])
                    DMA(gt[:], dn_pre[jb][4:8, :, i * 512:(i + 1) * 512].rearrange("m p t -> p m t"), [C.db(("dnpre", sn, h, m_, i)) for m_ in range(4, 8)], [gt])
                    for m in range(3):
                        DVE("tensor_scalar_mul", [pr, convw], [cv], out=cv[:, m, :], in0=pr[:, m, 0:512], scalar1=convw[:, h, m * 5:m * 5 + 1])
                        for tap in range(1, 5):
                            DVE("scalar_tensor_tensor", [pr, convw, cv], [cv], out=cv[:, m, :], in0=pr[:, m, tap:tap + 512],
                                scalar=convw[:, h, m * 5 + tap:m * 5 + tap + 1], in1=cv[:, m, :], op0=ALU.mult, op1=ALU.add)
                    ACT(cv[:], cv[:], AF.Silu, [cv], [cv])
                    ACT(sq[:], cv[:, 0:2, :], AF.Square, [cv], [sq])
                    for m in range(2):
                        MM(psN[m][:], ones[:], sq[:, m, :], [ones, sq], [psN[m]])
                        ACT(rn[:, m, :], psN[m][:], AF.Sqrt, [psN[m]], [rn], bias=EPS)
                    DVE("reciprocal", [rn], [rn], out=rn[:], in_=rn[:])
                    DVE("scalar_tensor_tensor", [cv, rn], [qn], out=qn[:], in0=cv[:, 0, :], scalar=float(128 ** -0.5), in1=rn[:, 0, :], op0=ALU.mult, op1=ALU.mult)
                    DVE("tensor_tensor", [cv, rn], [kn], out=kn[:], in0=cv[:, 1, :], in1=rn[:, 1, :], op=ALU.mult)
                    ACT(sig[:], gt[:, 0:2, :], AF.Sigmoid, [gt], [sig])
                    for d in range(2):
                        ACT(gl[:, d, :], gt[:, 2 + d, :], AF.Exp, [gt, gatep], [gl], bias=gatep[:, h, 2 + d:3 + d])
                        ACT(gl[:, d, :], gl[:, d, :], AF.Ln, [gl], [gl], bias=1.0)
                        DVE("tensor_scalar_mul", [gl, negA], [gl], out=gl[:, d, :], in0=gl[:, d, :], scalar1=negA[:, h, d:d + 1])
                    DVE("tensor_tensor_scan", [msk, gl], [gc], out=gc[:, 0, :], data0=msk[:], data1=gl[:, 0, :], initial=0.0, op0=ALU.mult, op1=ALU.add)
                    DVE("tensor_tensor_scan", [msk, gl], [gc], out=gc[:, 1, :], data0=msk[:], data1=gl[:, 1, :], initial=0.0, op0=ALU.mult, op1=ALU.add)
                    DVE("tensor_tensor", [gl, gc], [tmpb], out=tmpb[:], in0=gl[:, 1, :], in1=gc[:, 1, :], op=ALU.subtract)
                    for u in range(4):
                        DVE("tensor_scalar_add", [tmpb, gc], [gc], out=gc[:, 1, u * 128:(u + 1) * 128], in0=tmpb[:, u * 128:(u + 1) * 128], scalar1=gc[:, 1, u * 128 + 127:u * 128 + 128])
                    for d in range(2):
                        qt_b, dec_b = qt[d], dec[d]
                        lastc = 127 if d == 0 else 0
                        ACT(egc[:], gc[:, d, :], AF.Exp, [gc], [egc])
                        POOL("tensor_tensor", [kn, sig], [kb[d]], out=kb[d][:], in0=kn[:], in1=sig[:, d, :], op=ALU.mult)
                        POOL("tensor_tensor", [kb[d], egc], [kbg[d]], out=kbg[d][:], in0=kb[d][:], in1=egc[:], op=ALU.mult)
                        POOL("tensor_tensor", [qn, egc], [qt_b], out=qt_b[:], in0=qn[:], in1=egc[:], op=ALU.mult)
                        POOL("tensor_tensor", [cv, sig], [vb[d]], out=vb[d][:], in0=cv[:, 2, :], in1=sig[:, d, :], op=ALU.mult)
                        for u in range(4):
                            ACT(ek[:, u * 128:(u + 1) * 128], gc[:, d, u * 128:(u + 1) * 128], AF.Exp, [gc], [ek], scale=-1.0, bias=gc[:, d, u * 128 + lastc:u * 128 + lastc + 1])
                        POOL("tensor_tensor", [kn, ek], [kt[d]], out=kt[d][:], in0=kn[:], in1=ek[:], op=ALU.mult)
                        ACT(dec_b[:], gc[:, d, :].rearrange("p (u c) -> p u c", c=128)[:, :, lastc], AF.Exp, [gc], [dec_b])
                        DMA(dnd[(jb, d)][:, i * 4:(i + 1) * 4], dec_b[:], [dec_b], [C.db(("dnd", jb, d))])
                    for u0 in (0, 2):
                        un += 1
                        UN = [(0, u0), (1, u0), (0, u0 + 1), (1, u0 + 1)]
                        SLS = [slice(u * 128, (u + 1) * 128) for (_, u) in UN]
                        osd = [opst[q][un % 2] for q in range(NUQ)]
                        R = [dict() for _ in UN]

                        def run_stage(nalloc, produce, consume):
                            gsz = 2 if nalloc >= 2 else 4
                            for g0 in range(0, len(UN), gsz):
                                qs = range(g0, min(g0 + gsz, len(UN)))
                                for q in qs:
                                    produce(q, UN[q][0], SLS[q])
                                for q in qs:
                                    consume(q, UN[q][0], SLS[q])

                        def tr_p(q, d, sl):
                            R[q]["p1"], R[q]["p2"], R[q]["p3"] = nps(), nps(), nps()
                            TR(R[q]["p1"][:], vb[d][:, sl], ident[:], [vb[d], ident], [R[q]["p1"]])
                            TR(R[q]["p2"][:], kbg[d][:, sl], ident[:], [kbg[d], ident], [R[q]["p2"]])
                            TR(R[q]["p3"][:], kt[d][:, sl], ident[:], [kt[d], ident], [R[q]["p3"]])

                        def tr_c(q, d, sl):
                            evac(Vb[q][:], R[q]["p1"][:], [R[q]["p1"]], [Vb[q]])
                            evac(Kbg[q][:], R[q]["p2"][:], [R[q]["p2"]], [Kbg[q]])
                            evac(osd[q][:, 4, :], R[q]["p3"][:], [R[q]["p3"]], [osd[q]])
                        run_stage(3, tr_p, tr_c)

                        def dec_p(q, d, sl):
                            R[q]["p4"] = nps()
                            MMG([(R[q]["p4"][:], gc[:, d, sl], ident[:]), (R[q]["p4"][:], ident[:], pmask[:, d, :])], [gc, ident, pmask], [R[q]["p4"]])

                        def dec_c(q, d, sl):
                            DVE("tensor_tensor", [gc, R[q]["p4"]], [dif[q]], out=dif[q][:], in0=gc[:, d, sl], in1=R[q]["p4"][:], op=ALU.subtract)
                        run_stage(1, dec_p, dec_c)
                        for q, (d, u) in enumerate(UN):
                            ACT(dcy[q][:], dif[q][:], AF.Exp, [dif[q]], [dcy[q]])
                        for q, (d, u) in enumerate(UN):
                            POOL("tensor_tensor", [dcy[q], strict], [dcyS[q]], out=dcyS[q][:], in0=dcy[q][:], in1=strict[:, d, :], op=ALU.mult)

                        def mm_p(q, d, sl):
                            R[q]["p5"], R[q]["p6"] = nps(), nps()
                            MM(R[q]["p5"][:], kn[:, sl], kb[d][:, sl], [kn, kb[d]], [R[q]["p5"]])
                            MM(R[q]["p6"][:], kn[:, sl], qn[:, sl], [kn, qn], [R[q]["p6"]])

                        def mm_c(q, d, sl):
                            DVE("tensor_tensor", [R[q]["p5"], dcyS[q]], [PTm[q][0]], out=PTm[q][0][:], in0=R[q]["p5"][:], in1=dcyS[q][:], op=ALU.mult)
                            DVE("tensor_tensor", [R[q]["p6"], dcy[q]], [osd[q]], out=osd[q][:, 3, :], in0=R[q]["p6"][:], in1=dcy[q][:], op=ALU.mult)
                        run_stage(2, mm_p, mm_c)

                        def lt_p(q, d, sl):
                            R[q]["p7"] = nps()
                            TR(R[q]["p7"][:], PTm[q][0][:], ident[:], [PTm[q][0], ident], [R[q]["p7"]])

                        def lt_c(q, d, sl):
                            evac(Pm[q][0][:], R[q]["p7"][:], [R[q]["p7"]], [Pm[q][0]])
                            DVE("tensor_tensor", [ident, PTm[q][0]], [TT[q][0]], out=TT[q][0][:], in0=ident[:], in1=PTm[q][0][:], op=ALU.subtract)
                        run_stage(1, lt_p, lt_c)
                        cur = 0
                        for lv in range(1, 7):
                            nx = 1 - cur

                            def sq_p(q, d, sl, cur=cur, nx=nx, lv=lv):
                                R[q]["pa"] = nps()
                                MM(R[q]["pa"][:], PTm[q][cur][:], Pm[q][cur][:], [PTm[q][cur], Pm[q][cur]], [R[q]["pa"]])
                                if lv < 6:
                                    R[q]["pb"] = nps()
                                    MM(R[q]["pb"][:], Pm[q][cur][:], PTm[q][cur][:], [PTm[q][cur], Pm[q][cur]], [R[q]["pb"]])

                            def sq_c(q, d, sl, cur=cur, nx=nx, lv=lv):
                                evac(Pm[q][nx][:], R[q]["pa"][:], [R[q]["pa"]], [Pm[q][nx]])
                                if lv < 6:
                                    evac(PTm[q][nx][:], R[q]["pb"][:], [R[q]["pb"]], [PTm[q][nx]])
                            run_stage(2, sq_p, sq_c)

                            def tt_p(q, d, sl, cur=cur, nx=nx):
                                R[q]["pc"] = nps()
                                MM(R[q]["pc"][:], Pm[q][nx][:], TT[q][cur][:], [Pm[q][nx], TT[q][cur]], [R[q]["pc"]])

                            def tt_c(q, d, sl, cur=cur, nx=nx):
                                DVE("tensor_tensor", [R[q]["pc"], TT[q][cur]], [TT[q][nx]], out=TT[q][nx][:], in0=R[q]["pc"][:], in1=TT[q][cur][:], op=ALU.add)
                            run_stage(1, tt_p, tt_c)
                            cur = nx

                        def uw_p(q, d, sl, cur=cur):
                            R[q]["p8"], R[q]["p9"] = nps(), nps()
                            MM(R[q]["p8"][:], TT[q][cur][:], Vb[q][:], [TT[q][cur], Vb[q]], [R[q]["p8"]])
                            MM(R[q]["p9"][:], Kbg[q][:], TT[q][cur][:], [TT[q][cur], Kbg[q]], [R[q]["p9"]])

                        def uw_c(q, d, sl):
                            evac(osd[q][:, 0, :], R[q]["p8"][:], [R[q]["p8"]], [osd[q]])
                            evac(osd[q][:, 1, :], R[q]["p9"][:], [R[q]["p9"]], [osd[q]])
                            POOL("tensor_copy", [qt[d]], [osd[q]], out=osd[q][:, 2, :], in_=qt[d][:, sl])
                        run_stage(2, uw_p, uw_c)
                        for q, (d, u) in enumerate(UN):
                            n = i * 4 + u
                            DMA(dnop[(jb, d)][n].rearrange("k p c -> p k c"), osd[q][:], [osd[q]], [C.db(("dnop", jb, d, n))])
            C.release_all()

        p1b([("p", h) for h in range(4)] + [("s", 0), ("s", 1)])
        rotate()
        p1b([("s", 2), ("s", 3)])

        C.begin()
        chains = [(jb, d) for jb in ([("s", h) for h in range(4)] + [("p", h) for h in range(4)]) for d in range(2)]
        S = {ch: C.ring(2, [128, 128], F32) for ch in chains}
        decs = {ch: C.sb([128, TN[ch[0][0]] // 128], F32) for ch in chains}
        ops = {ch: C.ring(2, [128, 5, 128], F32) for ch in chains}
        vn = {ch: C.ring(2, [128, 128], F32) for ch in chains}
        ostg = C.ring(4, [128, 128], F32)
        psC = C.ring(8, [128, 128], F32, psum=True)
        pc_ = [0]

        def npc():
            pc_[0] += 1
            return psC[pc_[0] % 8]
        for ch in chains:
            DMA(decs[ch][:], dnd[ch], [C.db(("dnd", ch[0], ch[1]))], [decs[ch]])
            POOL("memset", [], [S[ch][0]], ap=S[ch][0][:], constant=0.0)
        nsteps = {ch: (8 if small else TN[ch[0][0]] // 128) for ch in chains}
        oc = 0
        GS = 4
        for step in range(max(nsteps.values())):
            act_ch = [ch for ch in chains if step < nsteps[ch]]
            for g0 in range(0, len(act_ch), GS):
                grp = act_ch[g0:g0 + GS]
                info = {}
                for ch in grp:
                    jb, d = ch
                    ns = nsteps[ch]
                    nun = TN[jb[0]] // 128
                    n = step if d == 0 else (ns - 1 - step if small else nun - 1 - step)
                    o_ = ops[ch][step % 2]
                    DMA(o_[:], dnop[ch][n].rearrange("k p c -> p k c"), [C.db(("dnop", jb, d, n))], [o_])
                    info[ch] = dict(n=n, o=o_, Sc=S[ch][step % 2], Sn=S[ch][(step + 1) % 2], v=vn[ch][step % 2])
                for ch in grp:
                    I = info[ch]
                    I["pa"] = npc()
                    MM(I["pa"][:], I["o"][:, 1, :], I["Sc"][:], [I["o"], I["Sc"]], [I["pa"]])
                for ch in grp:
                    I = info[ch]
                    DVE("tensor_tensor", [I["o"], I["pa"]], [I["v"]], out=I["v"][:], in0=I["o"][:, 0, :], in1=I["pa"][:], op=ALU.subtract)
                for ch in grp:
                    I = info[ch]
                    I["pb"], I["pc"] = npc(), npc()
                    MMG([(I["pb"][:], I["Sc"][:], I["o"][:, 2, :]), (I["pb"][:], I["v"][:], I["o"][:, 3, :])], [I["Sc"], I["o"], I["v"]], [I["pb"]])
                    MM(I["pc"][:], I["o"][:, 4, :], I["v"][:], [I["o"], I["v"]], [I["pc"]])
                for ch in grp:
                    jb, d = ch
                    I = info[ch]
                    n = I["n"]
                    og = ostg[oc % 4]
                    oc += 1
                    TB[0].op("act", "copy", dict(out=og[:], in_=I["pb"][:]), [I["pb"]], [og])
                    DMA(dno[ch][:, n * 128:(n + 1) * 128], og[:], [og], [C.db(("dno", jb, d, n // 4))])
                    DVE("scalar_tensor_tensor", [I["Sc"], decs[ch], I["pc"]], [I["Sn"]], out=I["Sn"][:], in0=I["Sc"][:], scalar=decs[ch][:, n:n + 1], in1=I["pc"][:], op0=ALU.mult, op1=ALU.add)
        C.release_all()

        C.begin()
        of_ = C.ring(2, [128, 2, 512], F32)
        zt = C.ring(2, [128, 512], F32)
        osum = C.sb([128, 512], F32)
        osq = C.sb([128, 512], F32)
        orn = C.sb([128, 512], F32)
        res = C.ring(2, [128, 512], BF16)
        psD = C.ring(2, [128, 512], F32, psum=True)
        k = 0
        for jb in JOBS:
            sn, h = jb
            Tn = TN[sn]
            for i in range(2 if small else Tn // 512):
                o_b, z_b, r_b, p_ = of_[k % 2], zt[k % 2], res[k % 2], psD[k % 2]
                k += 1
                for d in range(2):
                    DMA(o_b[:, d, :], dno[(jb, d)][:, i * 512:(i + 1) * 512], [C.db(("dno", jb, d, i))], [o_b])
                DMA(z_b[:], dn_pre[jb][3, :, i * 512:(i + 1) * 512], [C.db(("dnpre", sn, h, 3, i))], [z_b])
                DVE("tensor_tensor", [o_b], [osum], out=osum[:], in0=o_b[:, 0, :], in1=o_b[:, 1, :], op=ALU.add)
                ACT(osq[:], osum[:], AF.Square, [osum], [osq])
                MM(p_[:], ones[:], osq[:], [ones, osq], [p_])
                ACT(orn[:], p_[:], AF.Sqrt, [p_], [orn], scale=1.0 / 128, bias=EPS)
                DVE("reciprocal", [orn], [orn], out=orn[:], in_=orn[:])
                DVE("tensor_tensor", [osum, orn], [osum], out=osum[:], in0=osum[:], in1=orn[:], op=ALU.mult)
                ACT(z_b[:], z_b[:], AF.Silu, [z_b], [z_b])
                DVE("scalar_tensor_tensor", [osum, dnorm, z_b], [r_b], out=r_b[:], in0=osum[:], scalar=dnorm[:, 0:1], in1=z_b[:], op0=ALU.mult, op1=ALU.mult)
                DMA(dnmix_v[sn][h * 128:(h + 1) * 128, i * 512:(i + 1) * 512], r_b[:], [r_b], [C.db(("dnmix",))])
        C.release_all()
        rotate()

        C.begin()
        qk = C.ring(2, [128, 512], F32)
        cs = C.ring(2, [128, 2, 512], F32)
        asq = C.sb([128, 512], F32)
        arn = C.sb([128, 512], F32)
        aqn = C.sb([128, 512], F32)
        t1 = C.sb([128, 512], F32)
        t2 = C.sb([128, 512], F32)
        qr = C.ring(2, [128, 512], BF16)
        psE = C.ring(2, [128, 512], F32, psum=True)
        psR = C.ring(2, [128, 512], F32, psum=True)
        k = 0
        ework = [("k", sn, i) for sn, Tn in SEQS for i in range(2 if small else Tn // 512)]
        ework += [("q", None, ot) for ot in (range(0, 12, 6) if small else range(12))]
        for ci, (kind, sn, i) in enumerate(ework):
            c_b = cs[ci % 2]
            cols = slice(i * 512, (i + 1) * 512)
            tabs = cs_in if kind == "k" else cso_in
            DMA(c_b[:, 0, :], tabs[0][:, cols], [], [c_b])
            DMA(c_b[:, 1, :], tabs[1][:, cols], [], [c_b])
            for qc in range(1 if kind == "k" else 4):
                gcol = 1 if kind == "k" else 0
                q_b, o_b, pe_, pr_ = qk[k % 2], qr[k % 2], psE[k % 2], psR[k % 2]
                k += 1
                if kind == "k":
                    DMA(q_b[:], att_k[sn][:, cols], [C.db(("attk", sn, i))], [q_b])
                else:
                    DMA(q_b[:], att_q[qc * 128:(qc + 1) * 128, cols], [C.db(("attq", qc, i))], [q_b])
                ACT(asq[:], q_b[:], AF.Square, [q_b], [asq])
                MM(pe_[:], onesbd[:], asq[:], [onesbd, asq], [pe_])
                ACT(arn[:], pe_[:], AF.Sqrt, [pe_], [arn], scale=1.0 / 64, bias=EPS)
                DVE("reciprocal", [arn], [arn], out=arn[:], in_=arn[:])
                DVE("scalar_tensor_tensor", [q_b, qkg, arn], [aqn], out=aqn[:], in0=q_b[:], scalar=qkg[:, gcol:gcol + 1], in1=arn[:], op0=ALU.mult, op1=ALU.mult)
                MM(pr_[:], permT[:], aqn[:], [permT, aqn], [pr_])
                DVE("tensor_tensor", [aqn, c_b], [t1], out=t1[:], in0=aqn[:], in1=c_b[:, 0, :], op=ALU.mult)
                DVE("tensor_tensor", [pr_, c_b], [t2], out=t2[:], in0=pr_[:], in1=c_b[:, 1, :], op=ALU.mult)
                POOL("tensor_tensor", [t1, t2], [o_b], out=o_b[:], in0=t1[:], in1=t2[:], op=ALU.add)
                if kind == "k":
                    DMA(att_kr[sn][:, cols], o_b[:], [o_b], [C.db(("attkr", sn, i))])
                else:
                    DMA(att_qr[qc * 128:(qc + 1) * 128, cols], o_b[:], [o_b], [C.db(("attqr", i))])
        C.release_all()

        C.begin()
        kr = C.sb([64, TS], BF16)
        vv = C.sb([128, TS // 128, 65], BF16)
        qh = C.ring(2, [64, 512], BF16)
        pT_ = C.ring(4, [128, 2, 512], BF16)
        psS = C.ring(3, [128, 2, 512], F32, psum=True)
        psO = C.ring(1, [128, 512], F32, psum=True)
        psB = C.ps([64, 512], F32)
        rs = C.sb([128, 512], F32)
        bc = C.sb([64, 512], F32)
        ao = C.ring(2, [64, 512], BF16)
        k = 0
        qi = 0
        for sn, Tn in SEQS:
            nkt = 8 if small else Tn // 128
            ots = [o for o in OWN[sn] if (not small or o % 6 == 0)]
            for kv in range(2):
                DMA(kr[:, 0:Tn], att_kr[sn][kv * 64:(kv + 1) * 64, :], [C.db(("attkr", sn, i)) for i in range(Tn // 512)], [kr])
                POOL("memset", [], [vv], ap=vv[:, 0:Tn // 128, 64:65], constant=1.0)
                DMA(vv[:, 0:Tn // 128, 0:64], att_v[sn][:, kv * 64:(kv + 1) * 64].rearrange("(k p) d -> p k d", p=128),
                    [C.db(("attv", sn, i)) for i in range(Tn // 512)], [vv])
                for qh_ in range(4):
                    h = kv * 4 + qh_
                    for ot in ots:
                        q_b, po, a_b = qh[qi % 2], psO[0], ao[qi % 2]
                        qi += 1
                        cols = slice(ot * 512, (ot + 1) * 512)
                        DMA(q_b[:], att_qr[h * 64:(h + 1) * 64, cols], [C.db(("attqr", ot))], [q_b])
                        npair = nkt // 2
                        k0 = k
                        k += npair

                        def emit_s(j):
                            ps_ = psS[(k0 + j) % 3]
                            for e_ in range(2):
                                kt_ = 2 * j + e_
                                MM(ps_[:, e_, :], kr[:, kt_ * 128:(kt_ + 1) * 128], q_b[:], [kr, q_b], [ps_])
                        for j in range(min(3, npair)):
                            emit_s(j)
                        for j in range(npair):
                            ps_, p_b = psS[(k0 + j) % 3], pT_[(k0 + j) % 4]
                            ACT(p_b[:], ps_[:], AF.Exp, [ps_], [p_b], scale=0.125)
                            for e_ in range(2):
                                kt_ = 2 * j + e_
                                MM(po[0:65, :], vv[:, kt_, :], p_b[:, e_, :], [vv, p_b], [po], start=(kt_ == 0), stop=(kt_ == nkt - 1))
                            if j + 3 < npair:
                                emit_s(j + 3)
                        DVE("reciprocal", [po], [rs], out=rs[64:65, :], in_=po[64:65, :])
                        MM(psB[:], ones[64:65, 0:64], rs[64:65, :], [ones, rs], [psB])
                        TB[0].op("act", "copy", dict(out=bc[:], in_=psB[:]), [psB], [bc])
                        DVE("tensor_tensor", [po, bc], [a_b], out=a_b[:], in0=po[0:64, :], in1=bc[:], op=ALU.mult)
                        DMA(attmix[h * 64:(h + 1) * 64, cols], a_b[:], [a_b], [C.db(("attmix", ot))])
        C.release_all()
        C.release_all()
        rotate({"pool": 16})

        build_p2_body(nc, C, TB[0], xo_in, dnmix, attmix, small)
        C.ninst += TB[0].ninst
        print("fused instructions", C.ninst)
    return nc


def build_p2_body(nc, C, T, x_in, dnmix, attmix, small):
    nblk = 2 if small else NTOK2 // 128

    def din(name, shape, dt=F32):
        return nc.dram_tensor(name, list(shape), dt, kind="ExternalInput").ap()

    p_in = din("p", [NTOK2, 256])
    wout_in = din("w_out", [D, D])
    wq_in = din("peer_query", [D, 2048])
    keysT_in = din("keysT", [128, 2, 128])
    pu_in = din("peer_u", [16384, D])
    pv_in = din("peer_v", [16384, D])
    pproj_in = din("ple_proj", [256, D])
    pgate_in = din("ple_gate", [D, D])
    gains_in = din("gains", [3, 128, D])
    ident_in = din("ident2", [128, 128])
    mixidx_in = din("mixidx", [128, (NTOK2 // 128) * 4], I32)
    y_out = nc.dram_tensor("y", [NTOK2, D], F32, kind="ExternalOutput").ap()

    if True:
        def DMA(out, in_, r, w, eng="sp"):
            return T.dma(eng, "dma_start", dict(out=out, in_=in_), r, w)

        def ACT(out, in_, func, r, w, **kw):
            return T.op("act", "activation", dict(out=out, in_=in_, func=func, **kw), r, w)

        def DVE(name, r, w, **kw):
            return T.op("dve", name, kw, r, w)

        def POOL(name, r, w, **kw):
            return T.op("pool", name, kw, r, w)

        def MM(out, lhsT, rhs, r, w, start=True, stop=True):
            return T.op("pe", "matmul", dict(out=out, lhsT=lhsT, rhs=rhs, start=start, stop=stop), r, w)

        def MMG(lst, r, w):
            return T.group("pe", [("matmul", dict(out=o, lhsT=l, rhs=rh, start=(i == 0), stop=(i == len(lst) - 1))) for i, (o, l, rh) in enumerate(lst)], r, w)

        ev = [0]

        def evac(out_ap, in_ap, reads, writes):
            ev[0] += 1
            if ev[0] % 2:
                T.op("act", "copy", dict(out=out_ap, in_=in_ap), reads, writes)
            else:
                T.op("dve", "tensor_copy", dict(out=out_ap, in_=in_ap), reads, writes)

        ident = C.sb([128, 128], F32)
        identb = C.sb([128, 128], BF16)
        gains = C.sb([128, 3, D], F32)
        wout = C.sb([128, 8, D], BF16)
        wq = C.sb([128, 8, 2048], BF16)
        wgate = C.sb([128, 8, D], BF16)
        wproj = C.sb([128, 2, D], BF16)
        keysf = C.sb([128, 2, 128], F32)
        keysT = C.sb([128, 2, 128], BF16)
        C.begin()
        wst = C.ring(2, [128, 2048], F32)
        DMA(ident[:], ident_in, [], [ident])
        DVE("tensor_copy", [ident], [identb], out=identb[:], in_=ident[:])
        DMA(gains[:], gains_in.rearrange("g p d -> p g d"), [], [gains])
        DMA(keysf[:], keysT_in, [], [keysf])
        DVE("tensor_copy", [keysf], [keysT], out=keysT[:], in_=keysf[:])
        k = 0
        for (src, dst, nch, ncol) in ((wout_in, wout, 8, D), (wq_in, wq, 8, 2048), (pgate_in, wgate, 8, D), (pproj_in, wproj, 2, D)):
            for c in range(nch):
                w = wst[k % 2]
                k += 1
                DMA(w[:, 0:ncol], src[c * 128:(c + 1) * 128, :], [], [w])
                evac(dst[:, c, :], w[:, 0:ncol], [w], [dst])
        puv16 = nc.dram_tensor("puv16", [16384, 2, D], BF16).ap()
        s32 = C.ring(2, [128, 4, D], F32)
        s16 = C.ring(2, [128, 4, D], BF16)
        k = 0
        for t_, src in enumerate((pu_in, pv_in)):
            for kk in range(16384 // 512):
                a, b_ = s32[k % 2], s16[k % 2]
                k += 1
                rows = slice(kk * 512, (kk + 1) * 512)
                DMA(a[:], src[rows, :].rearrange("(k p) d -> p k d", p=128), [], [a])
                evac(b_[:, 0:2, :], a[:, 0:2, :], [a], [b_])
                evac(b_[:, 2:4, :], a[:, 2:4, :], [a], [b_])
                DMA(puv16[rows, t_, :].rearrange("(k p) d -> p k d", p=128), b_[:], [b_], [C.db(("tab16", t_, kk))])
        puv_in = puv16.rearrange("e t d -> e (t d)")
        C.release_all()

        mixidx = C.sb([128, (NTOK2 // 128) * 4], I32)
        DMA(mixidx[:], mixidx_in, [], [mixidx])
        xt = C.ring(2, [128, D], F32)
        mxt = C.ring(2, [128, 8, 128], BF16)
        mxb = [[Buf(mxt[r_].t) for _ in range(8)] for r_ in range(2)]
        pt = C.ring(2, [128, 256], F32)
        h1 = C.sb([128, D], F32)
        junk = C.sb([128, D], BF16)
        ss = C.sb([128, 4], F32)
        xnb = C.sb([128, D], BF16)
        xnT = C.sb([128, 8, 128], BF16)
        qT = C.sb([128, 16, 128], BF16)
        sc = C.sb([128, 16, 128], F32)
        wrk = C.sb([128, 128], F32)
        tv = C.sb([128, 16, 16], F32)
        ti = C.sb([128, 16, 16], U32)
        tif = C.sb([128, 16, 16], F32)
        tif128 = C.sb([128, 16, 16], F32)
        cand_s = C.sb([128, 8, 256], F32)
        cand_i = C.sb([128, 8, 256], F32)
        wrk2 = C.sb([128, 256], F32)
        junk2 = C.sb([128, 256], F32)
        ts = C.sb([128, 8, 16], F32)
        negm = C.sb([128, 8], F32)
        gsum = C.sb([128, 8], F32)
        ge = C.sb([128, 8, 16], F32)
        gates = C.sb([128, 128], F32)
        idxf = C.sb([128, 128], F32)
        idx = C.sb([128, 128], I32)
        hraw = C.sb([128, 128], F32)
        hg = C.sb([128, 128], F32)
        wgt = C.sb([128, 128], F32)
        G = 4
        NR = 3
        UVt = C.ring(NR, [128, G, 2 * D], BF16)
        UVb = [[Buf(UVt[r_].t) for _ in range(G)] for r_ in range(NR)]
        dg = C.ring(4, [128, 128], BF16)
        h2 = C.sb([128, D], F32)
        h2b = C.sb([128, D], BF16)
        h2T = C.sb([128, 8, 128], BF16)
        pb = C.sb([128, 256], BF16)
        pTt = C.sb([128, 2, 128], BF16)
        ple = C.sb([128, D], F32)
        sg = h1
        yt = [ple]
        psG = C.ring(4, [128, 512], F32, psum=True)
        psAcc = C.ps([128, 2, 512], F32)
        psT = C.ps([128, 8, 128], BF16)
        gi = [0]

        def npg():
            gi[0] += 1
            return psG[gi[0] % 4]

        def load(b):
            sl = slice(b * 128, (b + 1) * 128)
            DMA(xt[b % 2][:], x_in[sl, :], [], [xt[b % 2]])
            for c in range(4):
                T.dma("pool", "indirect_dma_start", dict(out=mxt[b % 2][:, c, :], out_offset=None, in_=dnmix,
                                                         in_offset=bass.IndirectOffsetOnAxis(ap=mixidx[:, b * 4 + c:b * 4 + c + 1], axis=0)),
                      [mixidx], [mxb[b % 2][c]])
            for c in range(4, 8):
                DMA(mxt[b % 2][:, c, :], attmix[(c - 4) * 128:(c - 3) * 128, sl], [], [mxb[b % 2][c]])
            DMA(pt[b % 2][:], p_in[sl, :], [], [pt[b % 2]])

        def rms(src, gidx, out_ap, out_buf):
            ACT(junk[:], src[:], AF.Square, [src], [junk, ss], accum_out=ss[:, 0:1])
            ACT(ss[:, 0:1], ss[:, 0:1], AF.Sqrt, [ss], [ss], scale=1.0 / D, bias=EPS)
            DVE("reciprocal", [ss], [ss], out=ss[:, 0:1], in_=ss[:, 0:1])
            DVE("scalar_tensor_tensor", [src, ss, gains], [out_buf], out=out_ap, in0=src[:], scalar=ss[:, 0:1], in1=gains[:, gidx, :], op0=ALU.mult, op1=ALU.mult)

        def transpose8(src_b, dstT):
            T.group("pe", [("transpose", dict(out=psT[:, c, :], in_=src_b[:, c * 128:(c + 1) * 128], identity=identb[:])) for c in range(8)], [src_b, identb], [psT])
            evac(dstT[:], psT[:], [psT], [dstT])

        outs = []
        load(0)
        for b in range(nblk):
            if b + 1 < nblk:
                load(b + 1)
            x_b, m_b, p_b = xt[b % 2], mxt[b % 2], pt[b % 2]
            for half in range(2):
                hs = slice(half * 512, (half + 1) * 512)
                pg = npg()
                MMG([(pg[:], m_b[:, c, :], wout[:, c, hs]) for c in range(8)], [wout] + mxb[b % 2], [pg])
                DVE("tensor_tensor", [pg, x_b], [h1], out=h1[:, hs], in0=pg[:], in1=x_b[:, hs], op=ALU.add)
            rms(h1, 0, xnb[:], xnb)
            transpose8(xnb, xnT)
            for g4 in range(4):
                pg = npg()
                for u in range(4):
                    hh = g4 * 4 + u
                    MMG([(pg[:, u * 128:(u + 1) * 128], wq[:, c, hh * 128:(hh + 1) * 128], xnT[:, c, :]) for c in range(8)], [wq, xnT], [pg])
                evac(qT[:, g4 * 4:(g4 + 1) * 4, :], pg[:].rearrange("p (u t) -> p u t", u=4), [pg], [qT])
            for g4 in range(4):
                pg = npg()
                for u in range(4):
                    hh = g4 * 4 + u
                    MM(pg[:, u * 128:(u + 1) * 128], qT[:, hh, :], keysT[:, hh % 2, :], [qT, keysT], [pg])
                evac(sc[:, g4 * 4:(g4 + 1) * 4, :], pg[:].rearrange("p (u t) -> p u t", u=4), [pg], [sc])
            for hh in range(16):
                DVE("max", [sc], [tv], out=tv[:, hh, 0:8], in_=sc[:, hh, :])
                DVE("max_index", [sc, tv], [ti], out=ti[:, hh, 0:8], in_max=tv[:, hh, 0:8], in_values=sc[:, hh, :])
                DVE("match_replace", [sc, tv], [wrk], out=wrk[:], in_to_replace=tv[:, hh, 0:8], in_values=sc[:, hh, :], imm_value=-1e30)
                DVE("max", [wrk], [tv], out=tv[:, hh, 8:16], in_=wrk[:])
                DVE("max_index", [wrk, tv], [ti], out=ti[:, hh, 8:16], in_max=tv[:, hh, 8:16], in_values=wrk[:])
            DVE("tensor_copy", [ti], [tif], out=tif[:], in_=ti[:])
            tv4 = tv[:].rearrange("p (h two) k -> p h two k", two=2)
            tf4 = tif[:].rearrange("p (h two) k -> p h two k", two=2)
            DVE("tensor_scalar_mul", [tif], [tif128], out=tif128[:], in0=tif[:], scalar1=128.0)
            t128 = tif128[:].rearrange("p (h two) k -> p h two k", two=2)
            DVE("tensor_tensor", [tv], [cand_s], out=cand_s[:].rearrange("p h (a b) -> p h a b", a=16),
                in0=tv4[:, :, 0, :].unsqueeze(3).to_broadcast([128, 8, 16, 16]), in1=tv4[:, :, 1, :].unsqueeze(2).to_broadcast([128, 8, 16, 16]), op=ALU.add)
            DVE("tensor_tensor", [tif, tif128], [cand_i], out=cand_i[:].rearrange("p h (a b) -> p h a b", a=16),
                in0=t128[:, :, 0, :].unsqueeze(3).to_broadcast([128, 8, 16, 16]), in1=tf4[:, :, 1, :].unsqueeze(2).to_broadcast([128, 8, 16, 16]), op=ALU.add)
            for h in range(8):
                DVE("max", [cand_s], [ts], out=ts[:, h, 0:8], in_=cand_s[:, h, :])
                DVE("match_replace", [cand_s, ts], [wrk2], out=wrk2[:], in_to_replace=ts[:, h, 0:8], in_values=cand_s[:, h, :], imm_value=-1e30)
                DVE("max", [wrk2], [ts], out=ts[:, h, 8:16], in_=wrk2[:])
                for r_ in range(16):
                    DVE("scalar_tensor_tensor", [cand_s, ts, cand_i], [junk2, idxf], out=junk2[:], in0=cand_s[:, h, :], scalar=ts[:, h, r_:r_ + 1], in1=cand_i[:, h, :],
                        op0=ALU.is_equal, op1=ALU.mult, accum_out=idxf[:, h * 16 + r_:h * 16 + r_ + 1])
            DVE("tensor_scalar_min", [idxf], [idxf], out=idxf[:], in0=idxf[:], scalar1=16383.0)
            DVE("tensor_copy", [idxf], [idx], out=idx[:], in_=idxf[:])
            DVE("tensor_scalar_mul", [ts], [negm], out=negm[:], in0=ts[:, :, 0], scalar1=-1.0)
            for h in range(8):
                ACT(ge[:, h, :], ts[:, h, :], AF.Exp, [ts, negm], [ge, gsum], bias=negm[:, h:h + 1], accum_out=gsum[:, h:h + 1])
            DVE("reciprocal", [gsum], [gsum], out=gsum[:], in_=gsum[:])
            for h in range(8):
                DVE("tensor_scalar_mul", [ge, gsum], [gates], out=gates[:, h * 16:(h + 1) * 16], in0=ge[:, h, :], scalar1=gsum[:, h:h + 1])
            def issue_gathers(grp):
                r_ = grp % NR
                for s in range(G):
                    kk = grp * G + s
                    T.dma("pool", "indirect_dma_start", dict(out=UVt[r_][:, s, :], out_offset=None, in_=puv_in, in_offset=bass.IndirectOffsetOnAxis(ap=idx[:, kk:kk + 1], axis=0)), [idx], [UVb[r_][s]])
            for g_ in range(NR - 1):
                issue_gathers(g_)
            for grp in range(128 // G):
                r_ = grp % NR
                if grp + NR - 1 < 128 // G:
                    issue_gathers(grp + NR - 1)
                for s in range(G):
                    kk = grp * G + s
                    DVE("scalar_tensor_tensor", [UVb[r_][s], xnb], [junk, hraw], out=junk[:], in0=UVt[r_][:, s, 0:D], scalar=1.0, in1=xnb[:], op0=ALU.mult, op1=ALU.mult, accum_out=hraw[:, kk:kk + 1])
                gs = slice(grp * G, (grp + 1) * G)
                ACT(hg[:, gs], hraw[:, gs], AF.Gelu, [hraw], [hg])
                DVE("tensor_tensor", [hg, gates], [wgt], out=wgt[:, gs], in0=hg[:, gs], in1=gates[:, gs], op=ALU.mult)
                for s in range(G):
                    kk = grp * G + s
                    d_ = dg[kk % 4]
                    ACT(d_[:], identb[:], AF.Copy, [identb, wgt], [d_], scale=wgt[:, kk:kk + 1])
                    for half in range(2):
                        MM(psAcc[:, half, :], d_[:], UVt[r_][:, s, D + half * 512:D + (half + 1) * 512], [d_, UVb[r_][s]], [psAcc], start=(kk == 0), stop=(kk == 127))
            for half in range(2):
                hs = slice(half * 512, (half + 1) * 512)
                DVE("tensor_tensor", [psAcc, h1], [h2], out=h2[:, hs], in0=psAcc[:, half, :], in1=h1[:, hs], op=ALU.add)
            T.op("act", "copy", dict(out=pb[:], in_=p_b[:]), [p_b], [pb])
            T.group("pe", [("transpose", dict(out=psT[:, c, :], in_=pb[:, c * 128:(c + 1) * 128], identity=identb[:])) for c in range(2)], [pb, identb], [psT])
            evac(pTt[:], psT[:, 0:2, :], [psT], [pTt])
            pgs = [npg(), npg()]
            for half in range(2):
                hs = slice(half * 512, (half + 1) * 512)
                MMG([(pgs[half][:], pTt[:, c, :], wproj[:, c, hs]) for c in range(2)], [pTt, wproj], [pgs[half]])
                ACT(junk[:, hs], pgs[half][:], AF.Square, [pgs[half]], [junk, ss], accum_out=ss[:, 1 + half:2 + half])
            DVE("tensor_tensor", [ss], [ss], out=ss[:, 3:4], in0=ss[:, 1:2], in1=ss[:, 2:3], op=ALU.add)
            ACT(ss[:, 3:4], ss[:, 3:4], AF.Sqrt, [ss], [ss], scale=1.0 / D, bias=EPS)
            DVE("reciprocal", [ss], [ss], out=ss[:, 3:4], in_=ss[:, 3:4])
            for half in range(2):
                hs = slice(half * 512, (half + 1) * 512)
                DVE("scalar_tensor_tensor", [pgs[half], ss, gains], [ple], out=ple[:, hs], in0=pgs[half][:], scalar=ss[:, 3:4], in1=gains[:, 1, hs], op0=ALU.mult, op1=ALU.mult)
            T.op("act", "copy", dict(out=h2b[:], in_=h2[:]), [h2], [h2b])
            transpose8(h2b, h2T)
            for half in range(2):
                hs = slice(half * 512, (half + 1) * 512)
                pg = npg()
                MMG([(pg[:], h2T[:, c, :], wgate[:, c, hs]) for c in range(8)], [h2T, wgate], [pg])
                ACT(sg[:, hs], pg[:], AF.Sigmoid, [pg], [sg])
            DVE("tensor_tensor", [ple, sg], [ple], out=ple[:], in0=ple[:], in1=sg[:], op=ALU.mult)
            DVE("tensor_tensor", [ple, h2], [h2], out=h2[:], in0=ple[:], in1=h2[:], op=ALU.add)
            y_b = yt[0]
            rms(h2, 2, y_b[:], y_b)
            ob = C.db(("y", b))
            DMA(y_out[b * 128:(b + 1) * 128, :], y_b[:], [y_b], [ob])
        T.final_waits("sp", list(C.dbufs.values()))
        T.emit()


def _consts():
    ident = np.eye(128, dtype=np.float32)
    ones = np.ones((128, 128), np.float32)
    onesbd = np.zeros((128, 128), np.float32)
    onesbd[:64, :64] = 1
    onesbd[64:, 64:] = 1
    s = np.arange(128)[:, None]
    c = np.arange(128)[None, :]
    pmask = np.stack([np.where(c >= s, 0.0, -NEG), np.where(c <= s, 0.0, -NEG)]).astype(np.float32)
    strict = np.stack([(c > s), (c < s)]).astype(np.float32)
    P = np.zeros((64, 64), np.float32)
    for blk in range(2):
        for i in range(32):
            if i < 16:
                P[blk * 32 + i, blk * 32 + i + 16] = -1.0
            else:
                P[blk * 32 + i, blk * 32 + i - 16] = 1.0
    permT = np.zeros((128, 128), np.float32)
    permT[:64, :64] = P.T
    permT[64:, 64:] = P.T
    inv_freq = (np.float32(10000.0) ** (-np.arange(0, 32, 2, dtype=np.float32) / np.float32(32))).astype(np.float32)
    t = np.arange(TS)
    rowpos = (t // 64).astype(np.float32)
    colpos = (t % 64).astype(np.float32)
    ang = np.zeros((64, TS), np.float32)
    for dd in range(64):
        pos = rowpos if dd < 32 else colpos
        ang[dd] = pos * inv_freq[dd % 16]
    cosT = np.tile(np.cos(ang).astype(np.float32), (2, 1))
    sinT = np.tile(np.sin(ang).astype(np.float32), (2, 1))
    return dict(ident=ident, ones=ones, onesbd=onesbd, pmask=pmask, strict=strict, permT=permT, cosT=cosT, sinT=sinT)


def fused_inputs(inp, c, consts):
    w_in = inp["w_in"][0]
    conv_w = inp["conv_w"][0]
    ps_, hp, ss_, r = c // 2, c % 2, c // 4, c % 4
    f = lambda a: np.ascontiguousarray(a, dtype=np.float32)
    wdn = np.empty((4, D, 1024), np.float32)
    convw = np.empty((4, 128, 15), np.float32)
    gatep = np.empty((4, 128, 4), np.float32)
    for hd in range(4):
        cols = [w_in[:, hd * 128:(hd + 1) * 128], w_in[:, 512 + hd * 128:512 + (hd + 1) * 128],
                w_in[:, 1024 + hd * 128:1024 + (hd + 1) * 128], w_in[:, 1536 + hd * 128:1536 + (hd + 1) * 128]]
        for g in range(4):
            cols.append(np.repeat(w_in[:, 2048 + 4 * g + hd:2048 + 4 * g + hd + 1], 128, axis=1))
        wdn[hd] = np.concatenate(cols, axis=1)
        for m in range(3):
            convw[hd][:, m * 5:(m + 1) * 5] = conv_w[:, m * 512 + hd * 128:m * 512 + (hd + 1) * 128].T
        gatep[hd] = np.stack([np.full(128, inp["a_log_fwd"][0][hd]), np.full(128, inp["a_log_bwd"][0][hd]),
                              np.full(128, inp["dt_bias_fwd"][0][hd]), np.full(128, inp["dt_bias_bwd"][0][hd])], axis=1)
    qb = 2064
    x_own = np.concatenate([inp["x_prompt"][ps_, hp * 2048:(hp + 1) * 2048], inp["x_sample"][ss_, r * 4096:(r + 1) * 4096]], axis=0)
    p_own = np.concatenate([inp["p_prompt"][0, ps_, hp * 2048:(hp + 1) * 2048], inp["p_sample"][0, ss_, r * 4096:(r + 1) * 4096]], axis=0)
    cosO = np.concatenate([consts["cosT"][:, hp * 2048:(hp + 1) * 2048], consts["cosT"][:, r * 4096:(r + 1) * 4096]], axis=1)
    sinO = np.concatenate([consts["sinT"][:, hp * 2048:(hp + 1) * 2048], consts["sinT"][:, r * 4096:(r + 1) * 4096]], axis=1)
    gains = np.stack([np.tile(inp["ffn_norm"][0][None], (128, 1)), np.tile(inp["ple_norm"][0][None], (128, 1)), np.tile(inp["final_norm"][None], (128, 1))])
    keysT = np.stack([inp["peer_keys_a"][0].T, inp["peer_keys_b"][0].T], axis=1)
    pp = np.arange(128)[:, None]
    cc = np.arange(4)[None, :]
    mixidx = np.empty((128, 48 * 4), np.int32)
    for b in range(48):
        if b < 16:
            mixidx[:, b * 4:(b + 1) * 4] = (cc * 128 + pp) * (TP // 128) + hp * 16 + b
        else:
            mixidx[:, b * 4:(b + 1) * 4] = 512 * (TP // 128) + (cc * 128 + pp) * (TS // 128) + r * 32 + (b - 16)
    d = dict(xp=f(inp["x_prompt"][ps_]), xs=f(inp["x_sample"][ss_]), x=f(x_own), p=f(p_own), wdn=wdn,
             wkv=f(w_in[:, qb + 512:qb + 768]), wqa=f(w_in[:, qb:qb + 512]),
             anorm=f(inp["attn_norm"][0].reshape(8, 128).T), convw=convw, gatep=f(gatep),
             dnorm=f(inp["dn_out_norm"][0].reshape(128, 1)),
             qkg=f(np.stack([np.tile(inp["q_norm"][0], 2), np.tile(inp["k_norm"][0], 2)], axis=1)),
             ident=consts["ident"], ones=consts["ones"], onesbd=consts["onesbd"], pmask=consts["pmask"], strict=consts["strict"],
             permT=consts["permT"], cosT=consts["cosT"], sinT=consts["sinT"], cosO=f(cosO), sinO=f(sinO),
             w_out=f(inp["w_out"][0]), peer_query=f(inp["peer_query"][0]), keysT=f(keysT), peer_u=f(inp["peer_u"][0]),
             peer_v=f(inp["peer_v"][0]), ple_proj=f(inp["ple_proj"][0]), ple_gate=f(inp["ple_gate"][0]), gains=f(gains),
             ident2=consts["ident"], mixidx=mixidx)
    return d


def kernel(**inputs):
    inp = {k: np.asarray(v) for k, v in inputs.items()}
    consts = _consts()
    cores = list(range(NCORES))
    nc = build_fused()
    in_maps = [fused_inputs(inp, c, consts) for c in cores]
    res = run_bass_kernel_spmd(nc, in_maps, core_ids=cores).results
    y_p = np.empty((4, TP, D), np.float32)
    y_s = np.empty((2, TS, D), np.float32)
    for c in cores:
        ps_, hp, ss_, r = c // 2, c % 2, c // 4, c % 4
        y = res[c]["y"]
        y_p[ps_, hp * 2048:(hp + 1) * 2048] = y[:2048]
        y_s[ss_, r * 4096:(r + 1) * 4096] = y[2048:]
    return (y_p, y_s)
```

```python
import numpy as np
from contextlib import ExitStack
import ml_dtypes
import concourse.bass as bass
import concourse.mybir as mybir
from concourse.bass_utils import run_bass_kernel_spmd

F32 = mybir.dt.float32
BF16 = mybir.dt.bfloat16
U32 = mybir.dt.uint32
I32 = mybir.dt.int32
AF = mybir.ActivationFunctionType
ALU = mybir.AluOpType

D = 1024
TP = 4096
TS = 16384
NCORES = 8
EPS = 1e-6
NEG = -30000.0
NTOK2 = 6144


class Buf:
    __slots__ = ("name", "last_w", "readers", "t")

    def __init__(self, t=None, name=""):
        self.name = name
        self.t = t
        self.last_w = None
        self.readers = []

    def __getitem__(self, k):
        return self.t[k]


class Trk:
    ENGS = ("pe", "act", "dve", "pool", "sp")
    NS = 8

    def __init__(self, nc, stack, tag, ns=None):
        self.nc = nc
        ns = ns or {}
        self.items = {e: [] for e in self.ENGS}
        self.sem = {}
        self.count = {e: 0 for e in self.ENGS}
        self.seen = {e: {} for e in self.ENGS}
        self.dma_n = {e: 0 for e in self.ENGS}
        self.dma_sems = {}
        self.semobj = {}
        self.ninst = 0
        for e in self.ENGS:
            s = stack.enter_context(nc.semaphore(f"{tag}_c_{e}"))
            self.sem[e] = s
            self.semobj[id(s)] = s
        for e in ("sp", "pool"):
            self.dma_sems[e] = []
            for i in range(ns.get(e, self.NS)):
                s = stack.enter_context(nc.semaphore(f"{tag}_d_{e}{i}"))
                self.dma_sems[e].append(s)
                self.semobj[id(s)] = s

    def _deps(self, eng, reads, writes, extra=(), skip_self=False):
        need = {}
        own = id(self.sem[eng])

        def add(tok):
            if tok is None:
                return
            k, v = tok
            if skip_self and k == own:
                return
            if need.get(k, 0) < v:
                need[k] = v
        for r in reads:
            add(r.last_w)
        for w in writes:
            add(w.last_w)
            for t in w.readers:
                add(t)
        for t in extra:
            add(t)
        seen = self.seen[eng]
        waits = []
        for k, v in need.items():
            if seen.get(k, 0) < v:
                seen[k] = v
                waits.append((self.semobj[k], v))
        return waits

    def _finish(self, tok, reads, writes):
        for w in writes:
            w.last_w = tok
            w.readers = []
        for r in reads:
            if r not in writes:
                if len(r.readers) > 64:
                    r.readers = r.readers[-32:]
                r.readers.append(tok)

    def op(self, eng, name, kw, reads=(), writes=(), skip_self=None):
        return self.group(eng, [(name, kw)], reads, writes, skip_self=skip_self)

    def group(self, eng, fns, reads=(), writes=(), skip_self=None):
        waits = self._deps(eng, reads, writes, skip_self=((eng == "pe") if skip_self is None else skip_self))
        self.count[eng] += 1
        tok = (id(self.sem[eng]), self.count[eng])
        self.items[eng].append((waits, list(fns), (self.sem[eng], 1)))
        self._finish(tok, reads, writes)
        self.ninst += len(fns)
        return tok

    def dma(self, eng, name, kw, reads=(), writes=()):
        fn = (name, kw)
        n = self.dma_n[eng]
        self.dma_n[eng] += 1
        nsl = len(self.dma_sems[eng])
        slot = n % nsl
        val = 16 * (n // nsl + 1)
        s = self.dma_sems[eng][slot]
        extra = [(id(s), val - 16)] if val > 16 else []
        waits = self._deps(eng, reads, writes, extra)
        tok = (id(s), val)
        self.items[eng].append((waits, [fn], (s, 16)))
        self._finish(tok, reads, writes)
        self.ninst += 1
        return tok

    def final_waits(self, eng, bufs):
        waits = self._deps(eng, bufs, [])
        self.items[eng].append((waits, [], None))

    def emit(self):
        nc = self.nc
        items = self.items

        def run(e, lst):
            for waits, fns, inc in lst:
                for s, v in waits:
                    e.wait_ge(s, v)
                ins = None
                for name, kw in fns:
                    ins = getattr(e, name)(**kw)
                if inc is not None and ins is not None:
                    ins.then_inc(inc[0], inc[1])

        with nc.Block() as block:
            @block.tensor
            def _(e):
                run(e, items["pe"])

            @block.scalar
            def _(e):
                run(e, items["act"])

            @block.vector
            def _(e):
                run(e, items["dve"])

            @block.gpsimd
            def _(e):
                run(e, items["pool"])

            @block.sync
            def _(e):
                run(e, items["sp"])
        self.items = {e: [] for e in self.ENGS}


class Ctx:
    def __init__(self, nc, st):
        self.nc = nc
        self.st = st
        self.ntrk = 0
        self.T = Trk(nc, st, "k0")
        self.n = 0
        self.dbufs = {}
        self.subs = []
        self.sub = None
        self.all = []
        self.ninst = 0

    def begin(self):
        self.subs.append(ExitStack())
        self.sub = self.subs[-1]

    def release_all(self):
        if any(self.T.items[e] for e in self.T.ENGS):
            self.T.final_waits("sp", list(self.dbufs.values()))
            self.T.final_waits("pool", list(self.dbufs.values()))
            self.T.emit()
        self.subs.pop().close()
        self.sub = self.subs[-1] if self.subs else None

    def switch_trk(self, ns=None):
        assert not any(self.T.items[e] for e in self.T.ENGS)
        self.ninst += self.T.ninst
        self.ntrk += 1
        self.T = Trk(self.nc, self.st, f"k{self.ntrk}", ns=ns)
        for b in self.all:
            b.last_w = None
            b.readers = []
        return self.T

    def _reg(self, b):
        self.all.append(b)
        return b

    def sb(self, shape, dt, name=None):
        self.n += 1
        st = self.sub or self.st
        t = st.enter_context(self.nc.sbuf_tensor(name or f"sb{self.n}", shape, dt))
        return self._reg(Buf(t))

    def ps(self, shape, dt, name=None):
        self.n += 1
        st = self.sub or self.st
        t = st.enter_context(self.nc.psum_tensor(name or f"ps{self.n}", shape, dt))
        return self._reg(Buf(t))

    def ring(self, n, shape, dt, psum=False):
        return [(self.ps if psum else self.sb)(shape, dt) for _ in range(n)]

    def db(self, key):
        b = self.dbufs.get(key)
        if b is None:
            b = self._reg(Buf(None, str(key)))
            self.dbufs[key] = b
        return b


SEQS = (("p", TP), ("s", TS))
OWN = {"p": range(0, 4), "s": range(4, 12)}
NROWS_MIX = 512 * (TP // 128) + 512 * (TS // 128)


def build_fused(small=False):
    nc = bass.Bass("TRN2", target_bir_lowering=False)

    def din(name, shape, dt=F32):
        return nc.dram_tensor(name, list(shape), dt, kind="ExternalInput").ap()

    def dscr(name, shape, dt=F32):
        return nc.dram_tensor(name, list(shape), dt).ap()

    TN = dict(SEQS)
    x_in = {"p": din("xp", [TP, D]), "s": din("xs", [TS, D])}
    xo_in = din("x", [NTOK2, D])
    wdn_in = din("wdn", [4, D, 1024])
    wkv_in = din("wkv", [D, 256])
    wqa_in = din("wqa", [D, 512])
    anorm_in = din("anorm", [128, 8])
    convw_in = din("convw", [4, 128, 15])
    gatep_in = din("gatep", [4, 128, 4])
    dnorm_in = din("dnorm", [128, 1])
    qkg_in = din("qkg", [128, 2])
    ident_in = din("ident", [128, 128])
    ones_in = din("ones", [128, 128])
    onesbd_in = din("onesbd", [128, 128])
    pmask_in = din("pmask", [2, 128, 128])
    strict_in = din("strict", [2, 128, 128])
    permT_in = din("permT", [128, 128])
    cs_in = (din("cosT", [128, TS]), din("sinT", [128, TS]))
    cso_in = (din("cosO", [128, NTOK2]), din("sinO", [128, NTOK2]))
    JOBS = [(sn, h) for sn, _ in SEQS for h in range(4)]
    dn_pre = {jb: dscr(f"dnpre_{jb[0]}{jb[1]}", [8, 128, TN[jb[0]]]) for jb in JOBS}
    att_k = {sn: dscr(f"attk_{sn}", [128, Tn]) for sn, Tn in SEQS}
    att_v = {sn: dscr(f"attv_{sn}", [Tn, 128], BF16) for sn, Tn in SEQS}
    att_kr = {sn: dscr(f"attkr_{sn}", [128, Tn], BF16) for sn, Tn in SEQS}
    att_q = dscr("attq", [512, NTOK2])
    att_qr = dscr("attqr", [512, NTOK2], BF16)
    dnop = {(jb, d): dscr(f"dnop_{jb[0]}{jb[1]}_{d}", [TN[jb[0]] // 128, 5, 128, 128]) for jb in JOBS for d in range(2)}
    dnd = {(jb, d): dscr(f"dnd_{jb[0]}{jb[1]}_{d}", [128, TN[jb[0]] // 128]) for jb in JOBS for d in range(2)}
    dno = {(jb, d): dscr(f"dno_{jb[0]}{jb[1]}_{d}", [128, TN[jb[0]]]) for jb in JOBS for d in range(2)}
    dnmix = dscr("dnmix", [NROWS_MIX, 128], BF16)
    dnmix_v = {"p": dnmix[0:512 * (TP // 128), :].rearrange("(ch b) t -> ch (b t)", b=TP // 128),
               "s": dnmix[512 * (TP // 128):NROWS_MIX, :].rearrange("(ch b) t -> ch (b t)", b=TS // 128)}
    attmix = dscr("attmix", [512, NTOK2], BF16)

    with ExitStack() as st:
        C = Ctx(nc, st)
        TB = [C.T]

        def DMA(out, in_, r, w, eng="sp"):
            return TB[0].dma(eng, "dma_start", dict(out=out, in_=in_), r, w)

        def ACT(out, in_, func, r, w, **kw):
            return TB[0].op("act", "activation", dict(out=out, in_=in_, func=func, **kw), r, w)

        def DVE(name, r, w, **kw):
            return TB[0].op("dve", name, kw, r, w)

        def POOL(name, r, w, **kw):
            return TB[0].op("pool", name, kw, r, w)

        def MM(out, lhsT, rhs, r, w, start=True, stop=True):
            return TB[0].op("pe", "matmul", dict(out=out, lhsT=lhsT, rhs=rhs, start=start, stop=stop), r, w)

        def MMG(lst, r, w):
            return TB[0].group("pe", [("matmul", dict(out=o, lhsT=l, rhs=rh, start=(i == 0), stop=(i == len(lst) - 1))) for i, (o, l, rh) in enumerate(lst)], r, w)

        def TR(out, in_, idn, r, w):
            return TB[0].op("pe", "transpose", dict(out=out, in_=in_, identity=idn), r, w)

        ev = [0]

        def evac(out_ap, in_ap, reads, writes):
            ev[0] += 1
            if ev[0] % 2:
                TB[0].op("act", "copy", dict(out=out_ap, in_=in_ap), reads, writes)
            else:
                TB[0].op("dve", "tensor_copy", dict(out=out_ap, in_=in_ap), reads, writes)

        def rotate(ns=None):
            TB[0] = C.switch_trk(ns)

        C.begin()
        ident = C.sb([128, 128], F32)
        identb = C.sb([128, 128], BF16)
        ones = C.sb([128, 128], F32)
        onesbd = C.sb([128, 128], F32)
        pmask = C.sb([128, 2, 128], F32)
        strict = C.sb([128, 2, 128], F32)
        permT = C.sb([128, 128], F32)
        anorm = C.sb([128, 8], F32)
        convw = C.sb([128, 4, 15], F32)
        gatep = C.sb([128, 4, 4], F32)
        negA = C.sb([128, 4, 2], F32)
        dnorm = C.sb([128, 1], F32)
        qkg = C.sb([128, 2], F32)
        msk = C.sb([128, 512], F32)

        def load_consts():
            DMA(ident[:], ident_in, [], [ident])
            DMA(ones[:], ones_in, [], [ones])
            DMA(onesbd[:], onesbd_in, [], [onesbd])
            DMA(pmask[:], pmask_in.rearrange("d p c -> p d c"), [], [pmask])
            DMA(strict[:], strict_in.rearrange("d p c -> p d c"), [], [strict])
            DMA(permT[:], permT_in, [], [permT])
            DMA(anorm[:], anorm_in, [], [anorm])
            DMA(convw[:], convw_in.rearrange("j p k -> p j k"), [], [convw])
            DMA(gatep[:], gatep_in.rearrange("j p k -> p j k"), [], [gatep])
            DMA(dnorm[:], dnorm_in, [], [dnorm])
            DMA(qkg[:], qkg_in, [], [qkg])
            DVE("tensor_copy", [ident], [identb], out=identb[:], in_=ident[:])
            ACT(negA[:], gatep[:, :, 0:2], AF.Exp, [gatep], [negA])
            DVE("tensor_scalar_mul", [negA], [negA], out=negA[:], in0=negA[:], scalar1=-1.0)
            POOL("memset", [], [msk], ap=msk[:], constant=1.0)
            POOL("memset", [], [msk], ap=msk[:].rearrange("p (n c) -> p n c", c=128)[:, :, 0:1], constant=0.0)
        load_consts()

        C.begin()
        wdn = C.sb([128, 4, 8, 1024], BF16)
        wkv = C.sb([128, 8, 256], BF16)
        wqa = C.sb([128, 8, 512], BF16)
        wst = C.ring(2, [128, 1024], F32)
        k = 0
        for h in range(4):
            for c in range(8):
                w = wst[k % 2]
                k += 1
                DMA(w[:], wdn_in[h, c * 128:(c + 1) * 128, :], [], [w])
                DVE("tensor_scalar_mul", [w, anorm], [wdn], out=wdn[:, h, c, :], in0=w[:], scalar1=anorm[:, c:c + 1])
        for src, dst, ncol in ((wkv_in, wkv, 256), (wqa_in, wqa, 512)):
            for c in range(8):
                w = wst[k % 2]
                k += 1
                DMA(w[:, 0:ncol], src[c * 128:(c + 1) * 128, :], [], [w])
                DVE("tensor_scalar_mul", [w, anorm], [dst], out=dst[:, c, :], in0=w[:, 0:ncol], scalar1=anorm[:, c:c + 1])
        xt = C.ring(2, [128, 4, 1024], F32)
        junk = C.sb([128, 1024], BF16)
        ss = C.ring(2, [128, 4], F32)
        xn = C.ring(2, [128, 4, 1024], BF16)
        xnT = C.ring(2, [128, 8, 512], BF16)
        pT = C.ring(2, [128, 512], BF16, psum=True)
        psA = C.ring(4, [128, 512], F32, psum=True)
        psV = C.ps([128, 4, 128], F32)
        stg = C.ring(4, [128, 512], F32)
        vstg = C.ring(2, [128, 4, 128], BF16)
        work = [("full", sn, i) for sn, Tn in SEQS for i in range(2 if small else Tn // 512)]
        work += [("own", None, ot) for ot in (range(0, 12, 6) if small else range(12))]

        def load_x(wi):
            kind, sn, i = work[wi]
            b = xt[wi % 2]
            src = x_in[sn] if kind == "full" else xo_in
            DMA(b[:], src[i * 512:(i + 1) * 512, :].rearrange("(b p) d -> p b d", p=128), [], [b])

        load_x(0)
        kk = 0
        for wi, (kind, sn, i) in enumerate(work):
            if wi + 1 < len(work):
                load_x(wi + 1)
            x_b, ss_b, xn_b, xnT_b = xt[wi % 2], ss[wi % 2], xn[wi % 2], xnT[wi % 2]
            cols = slice(i * 512, (i + 1) * 512)
            for b in range(4):
                ACT(junk[:], x_b[:, b, :], AF.Square, [x_b], [junk, ss_b], accum_out=ss_b[:, b:b + 1])
            ACT(ss_b[:], ss_b[:], AF.Sqrt, [ss_b], [ss_b], scale=1.0 / D, bias=EPS)
            DVE("reciprocal", [ss_b], [ss_b], out=ss_b[:], in_=ss_b[:])
            for b in range(4):
                DVE("tensor_scalar_mul", [x_b, ss_b], [xn_b], out=xn_b[:, b, :], in0=x_b[:, b, :], scalar1=ss_b[:, b:b + 1])
            for c in range(8):
                p = pT[c % 2]
                TB[0].group("pe", [("transpose", dict(out=p[:, b * 128:(b + 1) * 128], in_=xn_b[:, b, c * 128:(c + 1) * 128], identity=identb[:])) for b in range(4)], [xn_b, identb], [p])
                evac(xnT_b[:, c, :], p[:], [p], [xnT_b])
            if kind == "full":
                for h in range(4):
                    for cc in range(8):
                        p, s_ = psA[kk % 4], stg[kk % 4]
                        kk += 1
                        MMG([(p[:], wdn[:, h, c, cc * 128:(cc + 1) * 128], xnT_b[:, c, :]) for c in range(8)], [wdn, xnT_b], [p])
                        evac(s_[:], p[:], [p], [s_])
                        DMA(dn_pre[(sn, h)][cc, :, cols], s_[:], [s_], [C.db(("dnpre", sn, h, cc, i))])
                p, s_ = psA[kk % 4], stg[kk % 4]
                kk += 1
                MMG([(p[:], wkv[:, c, 0:128], xnT_b[:, c, :]) for c in range(8)], [wkv, xnT_b], [p])
                evac(s_[:], p[:], [p], [s_])
                DMA(att_k[sn][:, cols], s_[:], [s_], [C.db(("attk", sn, i))])
                for b in range(4):
                    MMG([(psV[:, b, :], xnT_b[:, c, b * 128:(b + 1) * 128], wkv[:, c, 128:256]) for c in range(8)], [wkv, xnT_b], [psV])
                vs = vstg[wi % 2]
                evac(vs[:], psV[:], [psV], [vs])
                DMA(att_v[sn][cols, :].rearrange("(b p) d -> p b d", p=128), vs[:], [vs], [C.db(("attv", sn, i))])
            else:
                for qc in range(4):
                    p, s_ = psA[kk % 4], stg[kk % 4]
                    kk += 1
                    MMG([(p[:], wqa[:, c, qc * 128:(qc + 1) * 128], xnT_b[:, c, :]) for c in range(8)], [wqa, xnT_b], [p])
                    evac(s_[:], p[:], [p], [s_])
                    DMA(att_q[qc * 128:(qc + 1) * 128, cols], s_[:], [s_], [C.db(("attq", qc, i))])
        C.release_all()

        def p1b(jobs):
            C.begin()
            pre = C.ring(2, [128, 3, 516], F32)
            gts = C.ring(2, [128, 4, 512], F32)
            cv = C.sb([128, 3, 512], F32)
            sq = C.sb([128, 2, 512], F32)
            rn = C.sb([128, 2, 512], F32)
            qn = C.sb([128, 512], F32)
            kn = C.sb([128, 512], F32)
            sig = C.sb([128, 2, 512], F32)
            gl = C.sb([128, 2, 512], F32)
            gc = C.sb([128, 2, 512], F32)
            tmpb = C.sb([128, 512], F32)
            egc = C.sb([128, 512], F32)
            kb = C.ring(2, [128, 512], F32)
            kbg = C.ring(2, [128, 512], F32)
            qt = C.ring(2, [128, 512], F32)
            vb = C.ring(2, [128, 512], F32)
            ek = C.sb([128, 512], F32)
            kt = C.ring(2, [128, 512], F32)
            dec = C.ring(2, [128, 4], F32)
            psN = C.ring(2, [128, 512], F32, psum=True)
            psX = C.ring(6, [128, 128], F32, psum=True)
            NUQ = 4
            Vb = C.ring(NUQ, [128, 128], F32)
            Kbg = C.ring(NUQ, [128, 128], F32)
            opst = [C.ring(2, [128, 5, 128], F32) for _ in range(NUQ)]
            dif = C.ring(NUQ, [128, 128], F32)
            dcy = C.ring(NUQ, [128, 128], F32)
            dcyS = C.ring(NUQ, [128, 128], F32)
            Pm = [C.ring(2, [128, 128], F32) for _ in range(NUQ)]
            PTm = [C.ring(2, [128, 128], F32) for _ in range(NUQ)]
            TT = [C.ring(2, [128, 128], F32) for _ in range(NUQ)]
            px = [0]

            def nps():
                px[0] += 1
                return psX[px[0] % 6]

            un = 0
            for jb in jobs:
                sn, h = jb
                Tn = TN[sn]
                ntl = 2 if small else Tn // 512
                for i in range(ntl):
                    pr, gt = pre[i % 2], gts[i % 2]
                    lo, hi = i * 512 - 2, i * 512 + 514
                    clo, chi = max(lo, 0), min(hi, Tn)
                    if lo < 0:
                        POOL("memset", [], [pr], ap=pr[:, :, 0:2], constant=0.0)
                    if hi > Tn:
                        POOL("memset", [], [pr], ap=pr[:, :, 514:516], constant=0.0)
                    DMA(pr[:, :, clo - lo:chi - lo], dn_pre[jb][0:3, :, clo:chi].rearrange("m p t -> p m t"),
                        [C.db(("dnpre", sn, h, m_, t_)) for m_ in range(3) for t_ in (i - 1, i, i + 1) if 0 <= t_ < Tn // 512], [pr])
                    DMA(gt[:], dn_pre[jb][4:8, :, i * 512:(i + 1) * 512].rearrange("m p t -> p m t"), [C.db(("dnpre", sn, h, m_, i)) for m_ in range(4, 8)], [gt])
                    for m in range(3):
                        DVE("tensor_scalar_mul", [pr, convw], [cv], out=cv[:, m, :], in0=pr[:, m, 0:512], scalar1=convw[:, h, m * 5:m * 5 + 1])
                        for tap in range(1, 5):
                            DVE("scalar_tensor_tensor", [pr, convw, cv], [cv], out=cv[:, m, :], in0=pr[:, m, tap:tap + 512],
                                scalar=convw[:, h, m * 5 + tap:m * 5 + tap + 1], in1=cv[:, m, :], op0=ALU.mult, op1=ALU.add)
                    ACT(cv[:], cv[:], AF.Silu, [cv], [cv])
                    ACT(sq[:], cv[:, 0:2, :], AF.Square, [cv], [sq])
                    for m in range(2):
                        MM(psN[m][:], ones[:], sq[:, m, :], [ones, sq], [psN[m]])
                        ACT(rn[:, m, :], psN[m][:], AF.Sqrt, [psN[m]], [rn], bias=EPS)
                    DVE("reciprocal", [rn], [rn], out=rn[:], in_=rn[:])
                    DVE("scalar_tensor_tensor", [cv, rn], [qn], out=qn[:], in0=cv[:, 0, :], scalar=float(128 ** -0.5), in1=rn[:, 0, :], op0=ALU.mult, op1=ALU.mult)
                    DVE("tensor_tensor", [cv, rn], [kn], out=kn[:], in0=cv[:, 1, :], in1=rn[:, 1, :], op=ALU.mult)
                    ACT(sig[:], gt[:, 0:2, :], AF.Sigmoid, [gt], [sig])
                    for d in range(2):
                        ACT(gl[:, d, :], gt[:, 2 + d, :], AF.Exp, [gt, gatep], [gl], bias=gatep[:, h, 2 + d:3 + d])
                        ACT(gl[:, d, :], gl[:, d, :], AF.Ln, [gl], [gl], bias=1.0)
                        DVE("tensor_scalar_mul", [gl, negA], [gl], out=gl[:, d, :], in0=gl[:, d, :], scalar1=negA[:, h, d:d + 1])
                    DVE("tensor_tensor_scan", [msk, gl], [gc], out=gc[:, 0, :], data0=msk[:], data1=gl[:, 0, :], initial=0.0, op0=ALU.mult, op1=ALU.add)
                    DVE("tensor_tensor_scan", [msk, gl], [gc], out=gc[:, 1, :], data0=msk[:], data1=gl[:, 1, :], initial=0.0, op0=ALU.mult, op1=ALU.add)
                    DVE("tensor_tensor", [gl, gc], [tmpb], out=tmpb[:], in0=gl[:, 1, :], in1=gc[:, 1, :], op=ALU.subtract)
                    for u in range(4):
                        DVE("tensor_scalar_add", [tmpb, gc], [gc], out=gc[:, 1, u * 128:(u + 1) * 128], in0=tmpb[:, u * 128:(u + 1) * 128], scalar1=gc[:, 1, u * 128 + 127:u * 128 + 128])
                    for d in range(2):
                        qt_b, dec_b = qt[d], dec[d]
                        lastc = 127 if d == 0 else 0
                        ACT(egc[:], gc[:, d, :], AF.Exp, [gc], [egc])
                        POOL("tensor_tensor", [kn, sig], [kb[d]], out=kb[d][:], in0=kn[:], in1=sig[:, d, :], op=ALU.mult)
                        POOL("tensor_tensor", [kb[d], egc], [kbg[d]], out=kbg[d][:], in0=kb[d][:], in1=egc[:], op=ALU.mult)
                        POOL("tensor_tensor", [qn, egc], [qt_b], out=qt_b[:], in0=qn[:], in1=egc[:], op=ALU.mult)
                        POOL("tensor_tensor", [cv, sig], [vb[d]], out=vb[d][:], in0=cv[:, 2, :], in1=sig[:, d, :], op=ALU.mult)
                        for u in range(4):
                            ACT(ek[:, u * 128:(u + 1) * 128], gc[:, d, u * 128:(u + 1) * 128], AF.Exp, [gc], [ek], scale=-1.0, bias=gc[:, d, u * 128 + lastc:u * 128 + lastc + 1])
                        POOL("tensor_tensor", [kn, ek], [kt[d]], out=kt[d][:], in0=kn[:], in1=ek[:], op=ALU.mult)
                        ACT(dec_b[:], gc[:, d, :].rearrange("p (u c) -> p u c", c=128)[:, :, lastc], AF.Exp, [gc], [dec_b])
                        DMA(dnd[(jb, d)][:, i * 4:(i + 1) * 4], dec_b[:], [dec_b], [C.db(("dnd", jb, d))])
                    for u0 in (0, 2):
                        un += 1
                        UN = [(0, u0), (1, u0), (0, u0 + 1), (1, u0 + 1)]
                        SLS = [slice(u * 128, (u + 1) * 128) for (_, u) in UN]
                        osd = [opst[q][un % 2] for q in range(NUQ)]
                        R = [dict() for _ in UN]

                        def run_stage(nalloc, produce, consume):
                            gsz = 2 if nalloc >= 2 else 4
                            for g0 in range(0, len(UN), gsz):
                                qs = range(g0, min(g0 + gsz, len(UN)))
                                for q in qs:
                                    produce(q, UN[q][0], SLS[q])
                                for q in qs:
                                    consume(q, UN[q][0], SLS[q])

                        def tr_p(q, d, sl):
                            R[q]["p1"], R[q]["p2"], R[q]["p3"] = nps(), nps(), nps()
                            TR(R[q]["p1"][:], vb[d][:, sl], ident[:], [vb[d], ident], [R[q]["p1"]])
                            TR(R[q]["p2"][:], kbg[d][:, sl], ident[:], [kbg[d], ident], [R[q]["p2"]])
                            TR(R[q]["p3"][:], kt[d][:, sl], ident[:], [kt[d], ident], [R[q]["p3"]])

                        def tr_c(q, d, sl):
                            evac(Vb[q][:], R[q]["p1"][:], [R[q]["p1"]], [Vb[q]])
                            evac(Kbg[q][:], R[q]["p2"][:], [R[q]["p2"]], [Kbg[q]])
                            evac(osd[q][:, 4, :], R[q]["p3"][:], [R[q]["p3"]], [osd[q]])
                        run_stage(3, tr_p, tr_c)

                        def dec_p(q, d, sl):
                            R[q]["p4"] = nps()
                            MMG([(R[q]["p4"][:], gc[:, d, sl], ident[:]), (R[q]["p4"][:], ident[:], pmask[:, d, :])], [gc, ident, pmask], [R[q]["p4"]])

                        def dec_c(q, d, sl):
                            DVE("tensor_tensor", [gc, R[q]["p4"]], [dif[q]], out=dif[q][:], in0=gc[:, d, sl], in1=R[q]["p4"][:], op=ALU.subtract)
                        run_stage(1, dec_p, dec_c)
                        for q, (d, u) in enumerate(UN):
                            ACT(dcy[q][:], dif[q][:], AF.Exp, [dif[q]], [dcy[q]])
                        for q, (d, u) in enumerate(UN):
                            POOL("tensor_tensor", [dcy[q], strict], [dcyS[q]], out=dcyS[q][:], in0=dcy[q][:], in1=strict[:, d, :], op=ALU.mult)

                        def mm_p(q, d, sl):
                            R[q]["p5"], R[q]["p6"] = nps(), nps()
                            MM(R[q]["p5"][:], kn[:, sl], kb[d][:, sl], [kn, kb[d]], [R[q]["p5"]])
                            MM(R[q]["p6"][:], kn[:, sl], qn[:, sl], [kn, qn], [R[q]["p6"]])

                        def mm_c(q, d, sl):
                            DVE("tensor_tensor", [R[q]["p5"], dcyS[q]], [PTm[q][0]], out=PTm[q][0][:], in0=R[q]["p5"][:], in1=dcyS[q][:], op=ALU.mult)
                            DVE("tensor_tensor", [R[q]["p6"], dcy[q]], [osd[q]], out=osd[q][:, 3, :], in0=R[q]["p6"][:], in1=dcy[q][:], op=ALU.mult)
                        run_stage(2, mm_p, mm_c)

                        def lt_p(q, d, sl):
                            R[q]["p7"] = nps()
                            TR(R[q]["p7"][:], PTm[q][0][:], ident[:], [PTm[q][0], ident], [R[q]["p7"]])

                        def lt_c(q, d, sl):
                            evac(Pm[q][0][:], R[q]["p7"][:], [R[q]["p7"]], [Pm[q][0]])
                            DVE("tensor_tensor", [ident, PTm[q][0]], [TT[q][0]], out=TT[q][0][:], in0=ident[:], in1=PTm[q][0][:], op=ALU.subtract)
                        run_stage(1, lt_p, lt_c)
                        cur = 0
                        for lv in range(1, 7):
                            nx = 1 - cur

                            def sq_p(q, d, sl, cur=cur, nx=nx, lv=lv):
                                R[q]["pa"] = nps()
                                MM(R[q]["pa"][:], PTm[q][cur][:], Pm[q][cur][:], [PTm[q][cur], Pm[q][cur]], [R[q]["pa"]])
                                if lv < 6:
                                    R[q]["pb"] = nps()
                                    MM(R[q]["pb"][:], Pm[q][cur][:], PTm[q][cur][:], [PTm[q][cur], Pm[q][cur]], [R[q]["pb"]])

                            def sq_c(q, d, sl, cur=cur, nx=nx, lv=lv):
                                evac(Pm[q][nx][:], R[q]["pa"][:], [R[q]["pa"]], [Pm[q][nx]])
                                if lv < 6:
                                    evac(PTm[q][nx][:], R[q]["pb"][:], [R[q]["pb"]], [PTm[q][nx]])
                            run_stage(2, sq_p, sq_c)

                            def tt_p(q, d, sl, cur=cur, nx=nx):
                                R[q]["pc"] = nps()
                                MM(R[q]["pc"][:], Pm[q][nx][:], TT[q][cur][:], [Pm[q][nx], TT[q][cur]], [R[q]["pc"]])

                            def tt_c(q, d, sl, cur=cur, nx=nx):
                                DVE("tensor_tensor", [R[q]["pc"], TT[q][cur]], [TT[q][nx]], out=TT[q][nx][:], in0=R[q]["pc"][:], in1=TT[q][cur][:], op=ALU.add)
                            run_stage(1, tt_p, tt_c)
                            cur = nx

                        def uw_p(q, d, sl, cur=cur):
                            R[q]["p8"], R[q]["p9"] = nps(), nps()
                            MM(R[q]["p8"][:], TT[q][cur][:], Vb[q][:], [TT[q][cur], Vb[q]], [R[q]["p8"]])
                            MM(R[q]["p9"][:], Kbg[q][:], TT[q][cur][:], [TT[q][cur], Kbg[q]], [R[q]["p9"]])

                        def uw_c(q, d, sl):
                            evac(osd[q][:, 0, :], R[q]["p8"][:], [R[q]["p8"]], [osd[q]])
                            evac(osd[q][:, 1, :], R[q]["p9"][:], [R[q]["p9"]], [osd[q]])
                            POOL("tensor_copy", [qt[d]], [osd[q]], out=osd[q][:, 2, :], in_=qt[d][:, sl])
                        run_stage(2, uw_p, uw_c)
                        for q, (d, u) in enumerate(UN):
                            n = i * 4 + u
                            DMA(dnop[(jb, d)][n].rearrange("k p c -> p k c"), osd[q][:], [osd[q]], [C.db(("dnop", jb, d, n))])
            C.release_all()

        p1b([("p", h) for h in range(4)] + [("s", 0), ("s", 1)])
        rotate()
        p1b([("s", 2), ("s", 3)])

        C.begin()
        chains = [(jb, d) for jb in ([("s", h) for h in range(4)] + [("p", h) for h in range(4)]) for d in range(2)]
        S = {ch: C.ring(2, [128, 128], F32) for ch in chains}
        decs = {ch: C.sb([128, TN[ch[0][0]] // 128], F32) for ch in chains}
        ops = {ch: C.ring(2, [128, 5, 128], F32) for ch in chains}
        vn = {ch: C.ring(2, [128, 128], F32) for ch in chains}
        ostg = C.ring(4, [128, 128], F32)
        psC = C.ring(8, [128, 128], F32, psum=True)
        pc_ = [0]

        def npc():
            pc_[0] += 1
            return psC[pc_[0] % 8]
        for ch in chains:
            DMA(decs[ch][:], dnd[ch], [C.db(("dnd", ch[0], ch[1]))], [decs[ch]])
            POOL("memset", [], [S[ch][0]], ap=S[ch][0][:], constant=0.0)
        nsteps = {ch: (8 if small else TN[ch[0][0]] // 128) for ch in chains}
        oc = 0
        GS = 4
        for step in range(max(nsteps.values())):
            act_ch = [ch for ch in chains if step < nsteps[ch]]
            for g0 in range(0, len(act_ch), GS):
                grp = act_ch[g0:g0 + GS]
                info = {}
                for ch in grp:
                    jb, d = ch
                    ns = nsteps[ch]
                    nun = TN[jb[0]] // 128
                    n = step if d == 0 else (ns - 1 - step if small else nun - 1 - step)
                    o_ = ops[ch][step % 2]
                    DMA(o_[:], dnop[ch][n].rearrange("k p c -> p k c"), [C.db(("dnop", jb, d, n))], [o_])
                    info[ch] = dict(n=n, o=o_, Sc=S[ch][step % 2], Sn=S[ch][(step + 1) % 2], v=vn[ch][step % 2])
                for ch in grp:
                    I = info[ch]
                    I["pa"] = npc()
                    MM(I["pa"][:], I["o"][:, 1, :], I["Sc"][:], [I["o"], I["Sc"]], [I["pa"]])
                for ch in grp:
                    I = info[ch]
                    DVE("tensor_tensor", [I["o"], I["pa"]], [I["v"]], out=I["v"][:], in0=I["o"][:, 0, :], in1=I["pa"][:], op=ALU.subtract)
                for ch in grp:
                    I = info[ch]
                    I["pb"], I["pc"] = npc(), npc()
                    MMG([(I["pb"][:], I["Sc"][:], I["o"][:, 2, :]), (I["pb"][:], I["v"][:], I["o"][:, 3, :])], [I["Sc"], I["o"], I["v"]], [I["pb"]])
                    MM(I["pc"][:], I["o"][:, 4, :], I["v"][:], [I["o"], I["v"]], [I["pc"]])
                for ch in grp:
                    jb, d = ch
                    I = info[ch]
                    n = I["n"]
                    og = ostg[oc % 4]
                    oc += 1
                    TB[0].op("act", "copy", dict(out=og[:], in_=I["pb"][:]), [I["pb"]], [og])
                    DMA(dno[ch][:, n * 128:(n + 1) * 128], og[:], [og], [C.db(("dno", jb, d, n // 4))])
                    DVE("scalar_tensor_tensor", [I["Sc"], decs[ch], I["pc"]], [I["Sn"]], out=I["Sn"][:], in0=I["Sc"][:], scalar=decs[ch][:, n:n + 1], in1=I["pc"][:], op0=ALU.mult, op1=ALU.add)
        C.release_all()

        C.begin()
        of_ = C.ring(2, [128, 2, 512], F32)
        zt = C.ring(2, [128, 512], F32)
        osum = C.sb([128, 512], F32)
        osq = C.sb([128, 512], F32)
        orn = C.sb([128, 512], F32)
        res = C.ring(2, [128, 512], BF16)
        psD = C.ring(2, [128, 512], F32, psum=True)
        k = 0
        for jb in JOBS:
            sn, h = jb
            Tn = TN[sn]
            for i in range(2 if small else Tn // 512):
                o_b, z_b, r_b, p_ = of_[k % 2], zt[k % 2], res[k % 2], psD[k % 2]
                k += 1
                for d in range(2):
                    DMA(o_b[:, d, :], dno[(jb, d)][:, i * 512:(i + 1) * 512], [C.db(("dno", jb, d, i))], [o_b])
                DMA(z_b[:], dn_pre[jb][3, :, i * 512:(i + 1) * 512], [C.db(("dnpre", sn, h, 3, i))], [z_b])
                DVE("tensor_tensor", [o_b], [osum], out=osum[:], in0=o_b[:, 0, :], in1=o_b[:, 1, :], op=ALU.add)
                ACT(osq[:], osum[:], AF.Square, [osum], [osq])
                MM(p_[:], ones[:], osq[:], [ones, osq], [p_])
                ACT(orn[:], p_[:], AF.Sqrt, [p_], [orn], scale=1.0 / 128, bias=EPS)
                DVE("reciprocal", [orn], [orn], out=orn[:], in_=orn[:])
                DVE("tensor_tensor", [osum, orn], [osum], out=osum[:], in0=osum[:], in1=orn[:], op=ALU.mult)
                ACT(z_b[:], z_b[:], AF.Silu, [z_b], [z_b])
                DVE("scalar_tensor_tensor", [osum, dnorm, z_b], [r_b], out=r_b[:], in0=osum[:], scalar=dnorm[:, 0:1], in1=z_b[:], op0=ALU.mult, op1=ALU.mult)
                DMA(dnmix_v[sn][h * 128:(h + 1) * 128, i * 512:(i + 1) * 512], r_b[:], [r_b], [C.db(("dnmix",))])
        C.release_all()
        rotate()

        C.begin()
        qk = C.ring(2, [128, 512], F32)
        cs = C.ring(2, [128, 2, 512], F32)
        asq = C.sb([128, 512], F32)
        arn = C.sb([128, 512], F32)
        aqn = C.sb([128, 512], F32)
        t1 = C.sb([128, 512], F32)
        t2 = C.sb([128, 512], F32)
        qr = C.ring(2, [128, 512], BF16)
        psE = C.ring(2, [128, 512], F32, psum=True)
        psR = C.ring(2, [128, 512], F32, psum=True)
        k = 0
        ework = [("k", sn, i) for sn, Tn in SEQS for i in range(2 if small else Tn // 512)]
        ework += [("q", None, ot) for ot in (range(0, 12, 6) if small else range(12))]
        for ci, (kind, sn, i) in enumerate(ework):
            c_b = cs[ci % 2]
            cols = slice(i * 512, (i + 1) * 512)
            tabs = cs_in if kind == "k" else cso_in
            DMA(c_b[:, 0, :], tabs[0][:, cols], [], [c_b])
            DMA(c_b[:, 1, :], tabs[1][:, cols], [], [c_b])
            for qc in range(1 if kind == "k" else 4):
                gcol = 1 if kind == "k" else 0
                q_b, o_b, pe_, pr_ = qk[k % 2], qr[k % 2], psE[k % 2], psR[k % 2]
                k += 1
                if kind == "k":
                    DMA(q_b[:], att_k[sn][:, cols], [C.db(("attk", sn, i))], [q_b])
                else:
                    DMA(q_b[:], att_q[qc * 128:(qc + 1) * 128, cols], [C.db(("attq", qc, i))], [q_b])
                ACT(asq[:], q_b[:], AF.Square, [q_b], [asq])
                MM(pe_[:], onesbd[:], asq[:], [onesbd, asq], [pe_])
                ACT(arn[:], pe_[:], AF.Sqrt, [pe_], [arn], scale=1.0 / 64, bias=EPS)
                DVE("reciprocal", [arn], [arn], out=arn[:], in_=arn[:])
                DVE("scalar_tensor_tensor", [q_b, qkg, arn], [aqn], out=aqn[:], in0=q_b[:], scalar=qkg[:, gcol:gcol + 1], in1=arn[:], op0=ALU.mult, op1=ALU.mult)
                MM(pr_[:], permT[:], aqn[:], [permT, aqn], [pr_])
                DVE("tensor_tensor", [aqn, c_b], [t1], out=t1[:], in0=aqn[:], in1=c_b[:, 0, :], op=ALU.mult)
                DVE("tensor_tensor", [pr_, c_b], [t2], out=t2[:], in0=pr_[:], in1=c_b[:, 1, :], op=ALU.mult)
                POOL("tensor_tensor", [t1, t2], [o_b], out=o_b[:], in0=t1[:], in1=t2[:], op=ALU.add)
                if kind == "k":
                    DMA(att_kr[sn][:, cols], o_b[:], [o_b], [C.db(("attkr", sn, i))])
                else:
                    DMA(att_qr[qc * 128:(qc + 1) * 128, cols], o_b[:], [o_b], [C.db(("attqr", i))])
        C.release_all()

        C.begin()
        kr = C.sb([64, TS], BF16)
        vv = C.sb([128, TS // 128, 65], BF16)
        qh = C.ring(2, [64, 512], BF16)
        pT_ = C.ring(4, [128, 2, 512], BF16)
        psS = C.ring(3, [128, 2, 512], F32, psum=True)
        psO = C.ring(1, [128, 512], F32, psum=True)
        psB = C.ps([64, 512], F32)
        rs = C.sb([128, 512], F32)
        bc = C.sb([64, 512], F32)
        ao = C.ring(2, [64, 512], BF16)
        k = 0
        qi = 0
        for sn, Tn in SEQS:
            nkt = 8 if small else Tn // 128
            ots = [o for o in OWN[sn] if (not small or o % 6 == 0)]
            for kv in range(2):
                DMA(kr[:, 0:Tn], att_kr[sn][kv * 64:(kv + 1) * 64, :], [C.db(("attkr", sn, i)) for i in range(Tn // 512)], [kr])
                POOL("memset", [], [vv], ap=vv[:, 0:Tn // 128, 64:65], constant=1.0)
                DMA(vv[:, 0:Tn // 128, 0:64], att_v[sn][:, kv * 64:(kv + 1) * 64].rearrange("(k p) d -> p k d", p=128),
                    [C.db(("attv", sn, i)) for i in range(Tn // 512)], [vv])
                for qh_ in range(4):
                    h = kv * 4 + qh_
                    for ot in ots:
                        q_b, po, a_b = qh[qi % 2], psO[0], ao[qi % 2]
                        qi += 1
                        cols = slice(ot * 512, (ot + 1) * 512)
                        DMA(q_b[:], att_qr[h * 64:(h + 1) * 64, cols], [C.db(("attqr", ot))], [q_b])
                        npair = nkt // 2
                        k0 = k
                        k += npair

                        def emit_s(j):
                            ps_ = psS[(k0 + j) % 3]
                            for e_ in range(2):
                                kt_ = 2 * j + e_
                                MM(ps_[:, e_, :], kr[:, kt_ * 128:(kt_ + 1) * 128], q_b[:], [kr, q_b], [ps_])
                        for j in range(min(3, npair)):
                            emit_s(j)
                        for j in range(npair):
                            ps_, p_b = psS[(k0 + j) % 3], pT_[(k0 + j) % 4]
                            ACT(p_b[:], ps_[:], AF.Exp, [ps_], [p_b], scale=0.125)
                            for e_ in range(2):
                                kt_ = 2 * j + e_
                                MM(po[0:65, :], vv[:, kt_, :], p_b[:, e_, :], [vv, p_b], [po], start=(kt_ == 0), stop=(kt_ == nkt - 1))
                            if j + 3 < npair:
                                emit_s(j + 3)
                        DVE("reciprocal", [po], [rs], out=rs[64:65, :], in_=po[64:65, :])
                        MM(psB[:], ones[64:65, 0:64], rs[64:65, :], [ones, rs], [psB])
                        TB[0].op("act", "copy", dict(out=bc[:], in_=psB[:]), [psB], [bc])
                        DVE("tensor_tensor", [po, bc], [a_b], out=a_b[:], in0=po[0:64, :], in1=bc[:], op=ALU.mult)
                        DMA(attmix[h * 64:(h + 1) * 64, cols], a_b[:], [a_b], [C.db(("attmix", ot))])
        C.release_all()
        C.release_all()
        rotate({"pool": 16})

        build_p2_body(nc, C, TB[0], xo_in, dnmix, attmix, small)
        C.ninst += TB[0].ninst
        print("fused instructions", C.ninst)
    return nc


def build_p2_body(nc, C, T, x_in, dnmix, attmix, small):
    nblk = 2 if small else NTOK2 // 128

    def din(name, shape, dt=F32):
        return nc.dram_tensor(name, list(shape), dt, kind="ExternalInput").ap()

    p_in = din("p", [NTOK2, 256])
    wout_in = din("w_out", [D, D])
    wq_in = din("peer_query", [D, 2048])
    keysT_in = din("keysT", [128, 2, 128])
    pu_in = din("peer_u", [16384, D])
    pv_in = din("peer_v", [16384, D])
    pproj_in = din("ple_proj", [256, D])
    pgate_in = din("ple_gate", [D, D])
    gains_in = din("gains", [3, 128, D])
    ident_in = din("ident2", [128, 128])
    mixidx_in = din("mixidx", [128, (NTOK2 // 128) * 4], I32)
    y_out = nc.dram_tensor("y", [NTOK2, D], F32, kind="ExternalOutput").ap()

    if True:
        def DMA(out, in_, r, w, eng="sp"):
            return T.dma(eng, "dma_start", dict(out=out, in_=in_), r, w)

        def ACT(out, in_, func, r, w, **kw):
            return T.op("act", "activation", dict(out=out, in_=in_, func=func, **kw), r, w)

        def DVE(name, r, w, **kw):
            return T.op("dve", name, kw, r, w)

        def DVEN(name, r, w, **kw):
            return T.op("dve", name, kw, r, w, skip_self=True)

        def POOL(name, r, w, **kw):
            return T.op("pool", name, kw, r, w)

        def MM(out, lhsT, rhs, r, w, start=True, stop=True):
            return T.op("pe", "matmul", dict(out=out, lhsT=lhsT, rhs=rhs, start=start, stop=stop), r, w)

        def MMG(lst, r, w):
            return T.group("pe", [("matmul", dict(out=o, lhsT=l, rhs=rh, start=(i == 0), stop=(i == len(lst) - 1))) for i, (o, l, rh) in enumerate(lst)], r, w)

        ev = [0]

        def evac(out_ap, in_ap, reads, writes):
            ev[0] += 1
            if ev[0] % 2:
                T.op("act", "copy", dict(out=out_ap, in_=in_ap), reads, writes)
            else:
                T.op("dve", "tensor_copy", dict(out=out_ap, in_=in_ap), reads, writes)

        ident = C.sb([128, 128], F32)
        identb = C.sb([128, 128], BF16)
        gains = C.sb([128, 3, D], F32)
        wout = C.sb([128, 8, D], BF16)
        wq = C.sb([128, 8, 2048], BF16)
        wgate = C.sb([128, 8, D], BF16)
        wproj = C.sb([128, 2, D], BF16)
        keysf = C.sb([128, 2, 128], F32)
        keysT = C.sb([128, 2, 128], BF16)
        C.begin()
        wst = C.ring(2, [128, 2048], F32)
        DMA(ident[:], ident_in, [], [ident])
        DVE("tensor_copy", [ident], [identb], out=identb[:], in_=ident[:])
        DMA(gains[:], gains_in.rearrange("g p d -> p g d"), [], [gains])
        DMA(keysf[:], keysT_in, [], [keysf])
        DVE("tensor_copy", [keysf], [keysT], out=keysT[:], in_=keysf[:])
        k = 0
        for (src, dst, nch, ncol) in ((wout_in, wout, 8, D), (wq_in, wq, 8, 2048), (pgate_in, wgate, 8, D), (pproj_in, wproj, 2, D)):
            for c in range(nch):
                w = wst[k % 2]
                k += 1
                DMA(w[:, 0:ncol], src[c * 128:(c + 1) * 128, :], [], [w])
                evac(dst[:, c, :], w[:, 0:ncol], [w], [dst])
        puv16 = nc.dram_tensor("puv16", [16384, 2, D], BF16).ap()
        s32 = C.ring(2, [128, 4, D], F32)
        s16 = C.ring(2, [128, 4, D], BF16)
        k = 0
        for t_, src in enumerate((pu_in, pv_in)):
            for kk in range(16384 // 512):
                a, b_ = s32[k % 2], s16[k % 2]
                k += 1
                rows = slice(kk * 512, (kk + 1) * 512)
                DMA(a[:], src[rows, :].rearrange("(k p) d -> p k d", p=128), [], [a])
                evac(b_[:, 0:2, :], a[:, 0:2, :], [a], [b_])
                evac(b_[:, 2:4, :], a[:, 2:4, :], [a], [b_])
                DMA(puv16[rows, t_, :].rearrange("(k p) d -> p k d", p=128), b_[:], [b_], [C.db(("tab16", t_, kk))])
        puv_in = puv16.rearrange("e t d -> e (t d)")
        C.release_all()

        mixidx = C.sb([128, (NTOK2 // 128) * 4], I32)
        DMA(mixidx[:], mixidx_in, [], [mixidx])
        xt = C.ring(2, [128, D], F32)
        mxt = C.ring(2, [128, 8, 128], BF16)
        mxb = [[Buf(mxt[r_].t) for _ in range(8)] for r_ in range(2)]
        pt = C.ring(2, [128, 256], F32)
        h1 = C.sb([128, D], F32)
        junk = C.sb([128, D], BF16)
        ss = C.sb([128, 4], F32)
        xnb = C.sb([128, D], BF16)
        xnT = C.sb([128, 8, 128], BF16)
        qT = C.sb([128, 16, 128], BF16)
        sc = C.sb([128, 16, 128], F32)
        wrk = C.sb([128, 128], F32)
        tv = C.sb([128, 16, 16], F32)
        ti = C.sb([128, 16, 16], U32)
        tif = C.sb([128, 16, 16], F32)
        tif128 = C.sb([128, 16, 16], F32)
        cand_s = C.sb([128, 8, 256], F32)
        cand_i = C.sb([128, 8, 256], F32)
        wrk2 = C.sb([128, 256], F32)
        junk2 = C.sb([128, 256], F32)
        ts = C.sb([128, 8, 16], F32)
        negm = C.sb([128, 8], F32)
        gsum = C.sb([128, 8], F32)
        ge = C.sb([128, 8, 16], F32)
        gates = C.sb([128, 128], F32)
        idxf = C.sb([128, 128], F32)
        idx = C.sb([128, 128], I32)
        hraw = C.sb([128, 128], F32)
        hg = C.sb([128, 128], F32)
        wgt = C.sb([128, 128], F32)
        G = 4
        NR = 3
        UVt = C.ring(NR, [128, G, 2 * D], BF16)
        UVb = [[Buf(UVt[r_].t) for _ in range(G)] for r_ in range(NR)]
        dg = C.ring(4, [128, 128], BF16)
        h2 = C.sb([128, D], F32)
        h2b = C.sb([128, D], BF16)
        h2T = C.sb([128, 8, 128], BF16)
        pb = C.sb([128, 256], BF16)
        pTt = C.sb([128, 2, 128], BF16)
        ple = C.sb([128, D], F32)
        sg = h1
        yt = [ple]
        psG = C.ring(4, [128, 512], F32, psum=True)
        psAcc = C.ps([128, 2, 512], F32)
        psT = C.ps([128, 8, 128], BF16)
        gi = [0]

        def npg():
            gi[0] += 1
            return psG[gi[0] % 4]

        def load(b):
            sl = slice(b * 128, (b + 1) * 128)
            DMA(xt[b % 2][:], x_in[sl, :], [], [xt[b % 2]])
            for c in range(4):
                T.dma("pool", "indirect_dma_start", dict(out=mxt[b % 2][:, c, :], out_offset=None, in_=dnmix,
                                                         in_offset=bass.IndirectOffsetOnAxis(ap=mixidx[:, b * 4 + c:b * 4 + c + 1], axis=0)),
                      [mixidx], [mxb[b % 2][c]])
            for c in range(4, 8):
                DMA(mxt[b % 2][:, c, :], attmix[(c - 4) * 128:(c - 3) * 128, sl], [], [mxb[b % 2][c]])
            DMA(pt[b % 2][:], p_in[sl, :], [], [pt[b % 2]])

        def rms(src, gidx, out_ap, out_buf):
            ACT(junk[:], src[:], AF.Square, [src], [junk, ss], accum_out=ss[:, 0:1])
            ACT(ss[:, 0:1], ss[:, 0:1], AF.Sqrt, [ss], [ss], scale=1.0 / D, bias=EPS)
            DVE("reciprocal", [ss], [ss], out=ss[:, 0:1], in_=ss[:, 0:1])
            DVE("scalar_tensor_tensor", [src, ss, gains], [out_buf], out=out_ap, in0=src[:], scalar=ss[:, 0:1], in1=gains[:, gidx, :], op0=ALU.mult, op1=ALU.mult)

        def transpose8(src_b, dstT):
            T.group("pe", [("transpose", dict(out=psT[:, c, :], in_=src_b[:, c * 128:(c + 1) * 128], identity=identb[:])) for c in range(8)], [src_b, identb], [psT])
            evac(dstT[:], psT[:], [psT], [dstT])

        outs = []
        load(0)
        for b in range(nblk):
            if b + 1 < nblk:
                load(b + 1)
            x_b, m_b, p_b = xt[b % 2], mxt[b % 2], pt[b % 2]
            for half in range(2):
                hs = slice(half * 512, (half + 1) * 512)
                pg = npg()
                MMG([(pg[:], m_b[:, c, :], wout[:, c, hs]) for c in range(8)], [wout] + mxb[b % 2], [pg])
                DVE("tensor_tensor", [pg, x_b], [h1], out=h1[:, hs], in0=pg[:], in1=x_b[:, hs], op=ALU.add)
            rms(h1, 0, xnb[:], xnb)
            transpose8(xnb, xnT)
            for g4 in range(4):
                pg = npg()
                for u in range(4):
                    hh = g4 * 4 + u
                    MMG([(pg[:, u * 128:(u + 1) * 128], wq[:, c, hh * 128:(hh + 1) * 128], xnT[:, c, :]) for c in range(8)], [wq, xnT], [pg])
                evac(qT[:, g4 * 4:(g4 + 1) * 4, :], pg[:].rearrange("p (u t) -> p u t", u=4), [pg], [qT])
            for g4 in range(4):
                pg = npg()
                for u in range(4):
                    hh = g4 * 4 + u
                    MM(pg[:, u * 128:(u + 1) * 128], qT[:, hh, :], keysT[:, hh % 2, :], [qT, keysT], [pg])
                evac(sc[:, g4 * 4:(g4 + 1) * 4, :], pg[:].rearrange("p (u t) -> p u t", u=4), [pg], [sc])
            for hh in range(16):
                DVE("max", [sc], [tv], out=tv[:, hh, 0:8], in_=sc[:, hh, :])
                DVE("max_index", [sc, tv], [ti], out=ti[:, hh, 0:8], in_max=tv[:, hh, 0:8], in_values=sc[:, hh, :])
                DVE("match_replace", [sc, tv], [wrk], out=wrk[:], in_to_replace=tv[:, hh, 0:8], in_values=sc[:, hh, :], imm_value=-1e30)
                DVE("max", [wrk], [tv], out=tv[:, hh, 8:16], in_=wrk[:])
                DVE("max_index", [wrk, tv], [ti], out=ti[:, hh, 8:16], in_max=tv[:, hh, 8:16], in_values=wrk[:])
            DVE("tensor_copy", [ti], [tif], out=tif[:], in_=ti[:])
            tv4 = tv[:].rearrange("p (h two) k -> p h two k", two=2)
            tf4 = tif[:].rearrange("p (h two) k -> p h two k", two=2)
            DVE("tensor_scalar_mul", [tif], [tif128], out=tif128[:], in0=tif[:], scalar1=128.0)
            t128 = tif128[:].rearrange("p (h two) k -> p h two k", two=2)
            DVE("tensor_tensor", [tv], [cand_s], out=cand_s[:].rearrange("p h (a b) -> p h a b", a=16),
                in0=tv4[:, :, 0, :].unsqueeze(3).to_broadcast([128, 8, 16, 16]), in1=tv4[:, :, 1, :].unsqueeze(2).to_broadcast([128, 8, 16, 16]), op=ALU.add)
            DVE("tensor_tensor", [tif, tif128], [cand_i], out=cand_i[:].rearrange("p h (a b) -> p h a b", a=16),
                in0=t128[:, :, 0, :].unsqueeze(3).to_broadcast([128, 8, 16, 16]), in1=tf4[:, :, 1, :].unsqueeze(2).to_broadcast([128, 8, 16, 16]), op=ALU.add)
            for h in range(8):
                DVE("max", [cand_s], [ts], out=ts[:, h, 0:8], in_=cand_s[:, h, :])
                DVE("match_replace", [cand_s, ts], [wrk2], out=wrk2[:], in_to_replace=ts[:, h, 0:8], in_values=cand_s[:, h, :], imm_value=-1e30)
                DVE("max", [wrk2], [ts], out=ts[:, h, 8:16], in_=wrk2[:])
                for r_ in range(16):
                    (DVE if r_ == 0 else DVEN)("scalar_tensor_tensor", [cand_s, ts, cand_i], [junk2, idxf], out=junk2[:], in0=cand_s[:, h, :], scalar=ts[:, h, r_:r_ + 1], in1=cand_i[:, h, :],
                        op0=ALU.is_equal, op1=ALU.mult, accum_out=idxf[:, h * 16 + r_:h * 16 + r_ + 1])
            DVE("tensor_scalar_min", [idxf], [idxf], out=idxf[:], in0=idxf[:], scalar1=16383.0)
            DVE("tensor_copy", [idxf], [idx], out=idx[:], in_=idxf[:])
            DVE("tensor_scalar_mul", [ts], [negm], out=negm[:], in0=ts[:, :, 0], scalar1=-1.0)
            for h in range(8):
                ACT(ge[:, h, :], ts[:, h, :], AF.Exp, [ts, negm], [ge, gsum], bias=negm[:, h:h + 1], accum_out=gsum[:, h:h + 1])
            DVE("reciprocal", [gsum], [gsum], out=gsum[:], in_=gsum[:])
            for h in range(8):
                DVE("tensor_scalar_mul", [ge, gsum], [gates], out=gates[:, h * 16:(h + 1) * 16], in0=ge[:, h, :], scalar1=gsum[:, h:h + 1])
            def issue_gathers(grp):
                r_ = grp % NR
                for s in range(G):
                    kk = grp * G + s
                    T.dma("pool", "indirect_dma_start", dict(out=UVt[r_][:, s, :], out_offset=None, in_=puv_in, in_offset=bass.IndirectOffsetOnAxis(ap=idx[:, kk:kk + 1], axis=0)), [idx], [UVb[r_][s]])
            for g_ in range(NR - 1):
                issue_gathers(g_)
            for grp in range(128 // G):
                r_ = grp % NR
                if grp + NR - 1 < 128 // G:
                    issue_gathers(grp + NR - 1)
                for s in range(G):
                    kk = grp * G + s
                    (DVE if s == 0 else DVEN)("scalar_tensor_tensor", [UVb[r_][s], xnb], [junk, hraw], out=junk[:], in0=UVt[r_][:, s, 0:D], scalar=1.0, in1=xnb[:], op0=ALU.mult, op1=ALU.mult, accum_out=hraw[:, kk:kk + 1])
                gs = slice(grp * G, (grp + 1) * G)
                ACT(hg[:, gs], hraw[:, gs], AF.Gelu, [hraw], [hg])
                DVE("tensor_tensor", [hg, gates], [wgt], out=wgt[:, gs], in0=hg[:, gs], in1=gates[:, gs], op=ALU.mult)
                for s in range(G):
                    kk = grp * G + s
                    d_ = dg[kk % 4]
                    ACT(d_[:], identb[:], AF.Copy, [identb, wgt], [d_], scale=wgt[:, kk:kk + 1])
                    for half in range(2):
                        MM(psAcc[:, half, :], d_[:], UVt[r_][:, s, D + half * 512:D + (half + 1) * 512], [d_, UVb[r_][s]], [psAcc], start=(kk == 0), stop=(kk == 127))
            for half in range(2):
                hs = slice(half * 512, (half + 1) * 512)
                DVE("tensor_tensor", [psAcc, h1], [h2], out=h2[:, hs], in0=psAcc[:, half, :], in1=h1[:, hs], op=ALU.add)
            T.op("act", "copy", dict(out=pb[:], in_=p_b[:]), [p_b], [pb])
            T.group("pe", [("transpose", dict(out=psT[:, c, :], in_=pb[:, c * 128:(c + 1) * 128], identity=identb[:])) for c in range(2)], [pb, identb], [psT])
            evac(pTt[:], psT[:, 0:2, :], [psT], [pTt])
            pgs = [npg(), npg()]
            for half in range(2):
                hs = slice(half * 512, (half + 1) * 512)
                MMG([(pgs[half][:], pTt[:, c, :], wproj[:, c, hs]) for c in range(2)], [pTt, wproj], [pgs[half]])
                ACT(junk[:, hs], pgs[half][:], AF.Square, [pgs[half]], [junk, ss], accum_out=ss[:, 1 + half:2 + half])
            DVE("tensor_tensor", [ss], [ss], out=ss[:, 3:4], in0=ss[:, 1:2], in1=ss[:, 2:3], op=ALU.add)
            ACT(ss[:, 3:4], ss[:, 3:4], AF.Sqrt, [ss], [ss], scale=1.0 / D, bias=EPS)
            DVE("reciprocal", [ss], [ss], out=ss[:, 3:4], in_=ss[:, 3:4])
            for half in range(2):
                hs = slice(half * 512, (half + 1) * 512)
                DVE("scalar_tensor_tensor", [pgs[half], ss, gains], [ple], out=ple[:, hs], in0=pgs[half][:], scalar=ss[:, 3:4], in1=gains[:, 1, hs], op0=ALU.mult, op1=ALU.mult)
            T.op("act", "copy", dict(out=h2b[:], in_=h2[:]), [h2], [h2b])
            transpose8(h2b, h2T)
            for half in range(2):
                hs = slice(half * 512, (half + 1) * 512)
                pg = npg()
                MMG([(pg[:], h2T[:, c, :], wgate[:, c, hs]) for c in range(8)], [h2T, wgate], [pg])
                ACT(sg[:, hs], pg[:], AF.Sigmoid, [pg], [sg])
            DVE("tensor_tensor", [ple, sg], [ple], out=ple[:], in0=ple[:], in1=sg[:], op=ALU.mult)
            DVE("tensor_tensor", [ple, h2], [h2], out=h2[:], in0=ple[:], in1=h2[:], op=ALU.add)
            y_b = yt[0]
            rms(h2, 2, y_b[:], y_b)
            ob = C.db(("y", b))
            DMA(y_out[b * 128:(b + 1) * 128, :], y_b[:], [y_b], [ob])
        T.final_waits("sp", list(C.dbufs.values()))
        T.emit()


def _consts():
    ident = np.eye(128, dtype=np.float32)
    ones = np.ones((128, 128), np.float32)
    onesbd = np.zeros((128, 128), np.float32)
    onesbd[:64, :64] = 1
    onesbd[64:, 64:] = 1
    s = np.arange(128)[:, None]
    c = np.arange(128)[None, :]
    pmask = np.stack([np.where(c >= s, 0.0, -NEG), np.where(c <= s, 0.0, -NEG)]).astype(np.float32)
    strict = np.stack([(c > s), (c < s)]).astype(np.float32)
    P = np.zeros((64, 64), np.float32)
    for blk in range(2):
        for i in range(32):
            if i < 16:
                P[blk * 32 + i, blk * 32 + i + 16] = -1.0
            else:
                P[blk * 32 + i, blk * 32 + i - 16] = 1.0
    permT = np.zeros((128, 128), np.float32)
    permT[:64, :64] = P.T
    permT[64:, 64:] = P.T
    inv_freq = (np.float32(10000.0) ** (-np.arange(0, 32, 2, dtype=np.float32) / np.float32(32))).astype(np.float32)
    t = np.arange(TS)
    rowpos = (t // 64).astype(np.float32)
    colpos = (t % 64).astype(np.float32)
    ang = np.zeros((64, TS), np.float32)
    for dd in range(64):
        pos = rowpos if dd < 32 else colpos
        ang[dd] = pos * inv_freq[dd % 16]
    cosT = np.tile(np.cos(ang).astype(np.float32), (2, 1))
    sinT = np.tile(np.sin(ang).astype(np.float32), (2, 1))
    return dict(ident=ident, ones=ones, onesbd=onesbd, pmask=pmask, strict=strict, permT=permT, cosT=cosT, sinT=sinT)


def fused_inputs(inp, c, consts):
    w_in = inp["w_in"][0]
    conv_w = inp["conv_w"][0]
    ps_, hp, ss_, r = c // 2, c % 2, c // 4, c % 4
    f = lambda a: np.ascontiguousarray(a, dtype=np.float32)
    wdn = np.empty((4, D, 1024), np.float32)
    convw = np.empty((4, 128, 15), np.float32)
    gatep = np.empty((4, 128, 4), np.float32)
    for hd in range(4):
        cols = [w_in[:, hd * 128:(hd + 1) * 128], w_in[:, 512 + hd * 128:512 + (hd + 1) * 128],
                w_in[:, 1024 + hd * 128:1024 + (hd + 1) * 128], w_in[:, 1536 + hd * 128:1536 + (hd + 1) * 128]]
        for g in range(4):
            cols.append(np.repeat(w_in[:, 2048 + 4 * g + hd:2048 + 4 * g + hd + 1], 128, axis=1))
        wdn[hd] = np.concatenate(cols, axis=1)
        for m in range(3):
            convw[hd][:, m * 5:(m + 1) * 5] = conv_w[:, m * 512 + hd * 128:m * 512 + (hd + 1) * 128].T
        gatep[hd] = np.stack([np.full(128, inp["a_log_fwd"][0][hd]), np.full(128, inp["a_log_bwd"][0][hd]),
                              np.full(128, inp["dt_bias_fwd"][0][hd]), np.full(128, inp["dt_bias_bwd"][0][hd])], axis=1)
    qb = 2064
    x_own = np.concatenate([inp["x_prompt"][ps_, hp * 2048:(hp + 1) * 2048], inp["x_sample"][ss_, r * 4096:(r + 1) * 4096]], axis=0)
    p_own = np.concatenate([inp["p_prompt"][0, ps_, hp * 2048:(hp + 1) * 2048], inp["p_sample"][0, ss_, r * 4096:(r + 1) * 4096]], axis=0)
    cosO = np.concatenate([consts["cosT"][:, hp * 2048:(hp + 1) * 2048], consts["cosT"][:, r * 4096:(r + 1) * 4096]], axis=1)
    sinO = np.concatenate([consts["sinT"][:, hp * 2048:(hp + 1) * 2048], consts["sinT"][:, r * 4096:(r + 1) * 4096]], axis=1)
    gains = np.stack([np.tile(inp["ffn_norm"][0][None], (128, 1)), np.tile(inp["ple_norm"][0][None], (128, 1)), np.tile(inp["final_norm"][None], (128, 1))])
    keysT = np.stack([inp["peer_keys_a"][0].T, inp["peer_keys_b"][0].T], axis=1)
    pp = np.arange(128)[:, None]
    cc = np.arange(4)[None, :]
    mixidx = np.empty((128, 48 * 4), np.int32)
    for b in range(48):
        if b < 16:
            mixidx[:, b * 4:(b + 1) * 4] = (cc * 128 + pp) * (TP // 128) + hp * 16 + b
        else:
            mixidx[:, b * 4:(b + 1) * 4] = 512 * (TP // 128) + (cc * 128 + pp) * (TS // 128) + r * 32 + (b - 16)
    d = dict(xp=f(inp["x_prompt"][ps_]), xs=f(inp["x_sample"][ss_]), x=f(x_own), p=f(p_own), wdn=wdn,
             wkv=f(w_in[:, qb + 512:qb + 768]), wqa=f(w_in[:, qb:qb + 512]),
             anorm=f(inp["attn_norm"][0].reshape(8, 128).T), convw=convw, gatep=f(gatep),
             dnorm=f(inp["dn_out_norm"][0].reshape(128, 1)),
             qkg=f(np.stack([np.tile(inp["q_norm"][0], 2), np.tile(inp["k_norm"][0], 2)], axis=1)),
             ident=consts["ident"], ones=consts["ones"], onesbd=consts["onesbd"], pmask=consts["pmask"], strict=consts["strict"],
             permT=consts["permT"], cosT=consts["cosT"], sinT=consts["sinT"], cosO=f(cosO), sinO=f(sinO),
             w_out=f(inp["w_out"][0]), peer_query=f(inp["peer_query"][0]), keysT=f(keysT), peer_u=f(inp["peer_u"][0]),
             peer_v=f(inp["peer_v"][0]), ple_proj=f(inp["ple_proj"][0]), ple_gate=f(inp["ple_gate"][0]), gains=f(gains),
             ident2=consts["ident"], mixidx=mixidx)
    return d


def kernel(**inputs):
    inp = {k: np.asarray(v) for k, v in inputs.items()}
    consts = _consts()
    cores = list(range(NCORES))
    nc = build_fused()
    in_maps = [fused_inputs(inp, c, consts) for c in cores]
    res = run_bass_kernel_spmd(nc, in_maps, core_ids=cores).results
    y_p = np.empty((4, TP, D), np.float32)
    y_s = np.empty((2, TS, D), np.float32)
    for c in cores:
        ps_, hp, ss_, r = c // 2, c % 2, c // 4, c % 4
        y = res[c]["y"]
        y_p[ps_, hp * 2048:(hp + 1) * 2048] = y[:2048]
        y_s[ss_, r * 4096:(r + 1) * 4096] = y[2048:]
    return (y_p, y_s)
```
